# Optimizing a Trainium2 kernel written in Bass

```python
import jax, jax.numpy as jnp
from jax import lax
import numpy as np

D_MODEL = 1024
BATCH = 4
SEQ = 8192
DEPTH = 1

CHUNK = 64
Q_BLOCK = 128
DIFF_HEADS = 4
DIFF_HEAD_DIM = 64
DIFF_V_DIM = 2 * DIFF_HEAD_DIM
DIFF_WIDTH = DIFF_HEADS * DIFF_V_DIM
RET_HEADS = 4
RET_KEY_DIM = 64
RET_V_DIM = 128
RET_WIDTH = RET_HEADS * RET_V_DIM
MIX_WIDTH = DIFF_WIDTH + RET_WIDTH
D_FF = 2816
CONV_WIDTH = 3
EPS = 1e-6
IN_SPLIT = (DIFF_HEADS * 2 * DIFF_HEAD_DIM, DIFF_HEADS * 2 * DIFF_HEAD_DIM, DIFF_WIDTH,
            RET_HEADS * RET_KEY_DIM, RET_HEADS * RET_KEY_DIM, RET_WIDTH, RET_WIDTH)
IN_WIDTH = sum(IN_SPLIT)

kernel_name = "hymba_style_diffattn_retention_convffn"


def rmsnorm(x, g):
    xf = x.astype(jnp.float32)
    y = xf * lax.rsqrt(jnp.mean(xf * xf, axis=-1, keepdims=True) + EPS)
    return (y * g.astype(jnp.float32)).astype(x.dtype)


def alibi_slopes(n_heads):
    return jnp.exp2(-8.0 * jnp.arange(1, n_heads + 1, dtype=jnp.float32) / n_heads)


def retention_log_gamma(n_heads):
    return jnp.log1p(-jnp.exp2(-5.0 - jnp.arange(n_heads, dtype=jnp.float32)))


def diff_attention(q, k, v, lam, lam_init, subln_g):
    B, S = q.shape[0], q.shape[1]
    nb = S // Q_BLOCK
    q = q * (DIFF_HEAD_DIM ** -0.5)
    qb = q.reshape(B, nb, Q_BLOCK, DIFF_HEADS, 2, DIFF_HEAD_DIM).transpose(1, 0, 2, 3, 4, 5)
    slopes = alibi_slopes(DIFF_HEADS)
    kpos = jnp.arange(S)

    def block(args):
        q_blk, b = args
        qpos = b * Q_BLOCK + jnp.arange(Q_BLOCK)
        s = jnp.einsum('bqhid,bkhid->bihqk', q_blk, k).astype(jnp.float32)
        dist = jnp.abs(qpos[:, None] - kpos[None, :]).astype(jnp.float32)
        allowed = (kpos[None, :] // CHUNK) <= (qpos[:, None] // CHUNK)
        bias = jnp.where(allowed[None], -slopes[:, None, None] * dist[None], -jnp.inf)
        p = jax.nn.softmax(s + bias, axis=-1)
        a = p[:, 0] - lam * p[:, 1]
        return jnp.einsum('bhqk,bkhe->bqhe', a.astype(v.dtype), v)

    o = lax.map(block, (qb, jnp.arange(nb)))
    o = o.transpose(1, 0, 2, 3, 4).reshape(B, S, DIFF_HEADS, DIFF_V_DIM)
    o = rmsnorm(o, subln_g) * (1.0 - lam_init)
    return o.reshape(B, S, DIFF_WIDTH)


def retention(q, k, v):
    B, S = q.shape[0], q.shape[1]
    nc = S // CHUNK
    log_g = retention_log_gamma(RET_HEADS)
    q = q.reshape(B, nc, CHUNK, RET_HEADS, RET_KEY_DIM)
    k = k.reshape(B, nc, CHUNK, RET_HEADS, RET_KEY_DIM) * (RET_KEY_DIM ** -0.5)
    v = v.reshape(B, nc, CHUNK, RET_HEADS, RET_V_DIM)
    n = jnp.arange(CHUNK, dtype=jnp.float32)
    intra_decay = jnp.exp(log_g[:, None, None] * jnp.abs(n[:, None] - n[None, :]))
    s = jnp.einsum('bcnhd,bcmhd->bchnm', q, k) * intra_decay
    o_intra = jnp.einsum('bchnm,bcmhe->bcnhe', s, v)
    k_decay = jnp.exp(log_g[:, None] * (CHUNK - 1.0 - n))
    kv = jnp.einsum('bcmhd,bcmhe,hm->cbhde', k, v, k_decay)
    chunk_decay = jnp.exp(log_g * CHUNK)[None, :, None, None]

    def step(state, kv_c):
        return chunk_decay * state + kv_c, state

    _, s_prev = lax.scan(step, jnp.zeros_like(kv[0]), kv)
    q_decay = jnp.exp(log_g[:, None] * (n + 1.0))
    o_cross = jnp.einsum('bcnhd,cbhde,hn->bcnhe', q, s_prev, q_decay)
    return (o_intra + o_cross).reshape(B, S, RET_HEADS, RET_V_DIM)


def head_groupnorm(o, g):
    of = o.astype(jnp.float32)
    mu = jnp.mean(of, axis=-1, keepdims=True)
    var = jnp.mean(jnp.square(of - mu), axis=-1, keepdims=True)
    y = (of - mu) * lax.rsqrt(var + EPS)
    return y.reshape(o.shape[0], o.shape[1], -1) * g.astype(jnp.float32)


def conv_ffn(x, w_up, conv_w, conv_b, w_down):
    h = x @ w_up
    a, b = jnp.split(h, 2, axis=-1)
    a = lax.conv_general_dilated(a, conv_w[:, None, :].astype(a.dtype), window_strides=(1,),
                                 padding=[(CONV_WIDTH - 1, 0)],
                                 dimension_numbers=('NWC', 'WIO', 'NWC'),
                                 feature_group_count=D_FF) + conv_b
    return (jax.nn.gelu(a) * b) @ w_down


def setup_inputs(seed: int = 0) -> dict:
    key = jax.random.key(seed)
    ks = jax.random.split(key, 20)
    nrm = lambda k, shape, s: jax.random.normal(k, shape, jnp.float32) * s
    L = DEPTH
    return {
        "x": nrm(ks[0], (BATCH, SEQ, D_MODEL), 1.0),
        "norm_mix_g": 1.0 + nrm(ks[1], (L, D_MODEL), 0.02),
        "w_in": nrm(ks[2], (L, D_MODEL, IN_WIDTH), D_MODEL ** -0.5),
        "lambda_q1": nrm(ks[3], (L, DIFF_HEAD_DIM), 0.1),
        "lambda_k1": nrm(ks[4], (L, DIFF_HEAD_DIM), 0.1),
        "lambda_q2": nrm(ks[5], (L, DIFF_HEAD_DIM), 0.1),
        "lambda_k2": nrm(ks[6], (L, DIFF_HEAD_DIM), 0.1),
        "diff_subln_g": 1.0 + nrm(ks[7], (L, DIFF_V_DIM), 0.02),
        "ret_gn_g": 1.0 + nrm(ks[8], (L, RET_WIDTH), 0.02),
        "w_out": nrm(ks[9], (L, MIX_WIDTH, D_MODEL), MIX_WIDTH ** -0.5),
        "norm_ffn_g": 1.0 + nrm(ks[10], (L, D_MODEL), 0.02),
        "w_up": nrm(ks[11], (L, D_MODEL, 2 * D_FF), D_MODEL ** -0.5),
        "conv_w": nrm(ks[12], (L, CONV_WIDTH, D_FF), CONV_WIDTH ** -0.5),
        "conv_b": nrm(ks[13], (L, D_FF), 0.02),
        "w_down": nrm(ks[14], (L, D_FF, D_MODEL), D_FF ** -0.5),
        "final_norm_g": 1.0 + nrm(ks[15], (D_MODEL,), 0.02),
    }


def reference(x, norm_mix_g, w_in, lambda_q1, lambda_k1, lambda_q2, lambda_k2, diff_subln_g,
              ret_gn_g, w_out, norm_ffn_g, w_up, conv_w, conv_b, w_down, final_norm_g):
    B, S, _ = x.shape
    split_idx = [int(i) for i in np.cumsum(IN_SPLIT)[:-1]]
    for l in range(DEPTH):
        lam_init = 0.8 - 0.6 * float(np.exp(-0.3 * l))
        h = rmsnorm(x, norm_mix_g[l])
        proj = h @ w_in[l]
        dq, dk, dv, rq, rk, rv, rg = jnp.split(proj, split_idx, axis=-1)
        lam = (jnp.exp(jnp.sum(lambda_q1[l].astype(jnp.float32) * lambda_k1[l].astype(jnp.float32)))
               - jnp.exp(jnp.sum(lambda_q2[l].astype(jnp.float32) * lambda_k2[l].astype(jnp.float32)))
               + lam_init)
        o_diff = diff_attention(dq.reshape(B, S, DIFF_HEADS, 2, DIFF_HEAD_DIM),
                                dk.reshape(B, S, DIFF_HEADS, 2, DIFF_HEAD_DIM),
                                dv.reshape(B, S, DIFF_HEADS, DIFF_V_DIM),
                                lam, lam_init, diff_subln_g[l])
        o_ret = retention(rq.reshape(B, S, RET_HEADS, RET_KEY_DIM).astype(jnp.float32),
                          rk.reshape(B, S, RET_HEADS, RET_KEY_DIM).astype(jnp.float32),
                          rv.reshape(B, S, RET_HEADS, RET_V_DIM).astype(jnp.float32))
        o_ret = (head_groupnorm(o_ret, ret_gn_g[l]) * jax.nn.silu(rg.astype(jnp.float32))).astype(x.dtype)
        x = x + jnp.concatenate([o_diff.astype(x.dtype), o_ret], axis=-1) @ w_out[l]
        x = x + conv_ffn(rmsnorm(x, norm_ffn_g[l]), w_up[l], conv_w[l], conv_b[l], w_down[l])
    return rmsnorm(x, final_norm_g)
```

```python
import os
import numpy as np
from contextlib import ExitStack
import concourse.bass as bass
import concourse.mybir as mybir
from concourse.bass_utils import run_bass_kernel_spmd

F32 = mybir.dt.float32
BF16 = mybir.dt.bfloat16
AF = mybir.ActivationFunctionType
ALU = mybir.AluOpType

D = 1024
NH = 4
FF = 2816
NFC = 22
SLOPES = [2.0 ** -2, 2.0 ** -4, 2.0 ** -6, 2.0 ** -8]
GAM = [1.0 - 2.0 ** (-5 - h) for h in range(4)]
THR = 120.0
NEG = -30000.0
EPS = 1e-6
LAM_INIT = 0.2
RING = [8, 18, 64, 64]
NW = 8
FIRST_T = 15
LAST_T = 32

_off = {}
_cur = 0
for _n, _w in (("ident", 128), ("ones", 128), ("avg", 128), ("sel0", 128), ("sel1", 128), ("B", 512),
               ("Dt", 896), ("DM", 512), ("Ddec", 256), ("Dq", 256), ("g128", 2), ("e0", 128), ("e32", 128)):
    _off[_n] = _cur
    _cur += _w
NCT = _cur
_poff = {}
_cur = 0
for _n, _w in (("g1", 8), ("g2", 8), ("gf", 1024), ("subg", 1), ("gng", 4), ("cw", 66), ("cb", 22), ("lamv", 256)):
    _poff[_n] = _cur
    _cur += _w
NPT = _cur
DIDX = {(0, 0): 0, (1, 0): 1, (1, 1): 2, (2, 0): 3, (2, 1): 4, (3, 0): 5, (3, 1): 6}


def _ctab(hf):
    p = np.arange(128, dtype=np.float64)
    t = np.zeros((128, NCT), np.float64)
    t[:, _off["ident"]:_off["ident"] + 128] = np.eye(128)
    t[:, _off["ones"]:_off["ones"] + 128] = 1.0
    t[:, _off["avg"]:_off["avg"] + 128] = 1.0 / 128
    t[0, _off["e0"]:_off["e0"] + 128] = 1.0
    t[32, _off["e32"]:_off["e32"] + 128] = 1.0
    t[:, _off["sel0"] + 0] = 1.0
    t[:, _off["sel1"] + 32] = 1.0
    for h in range(4):
        m = SLOPES[h]
        for st in range(2):
            for dj in range(64):
                v = m * (p - 128.0 * dj)
                if st == 1 and hf == 0:
                    v = np.full(128, NEG)
                t[:, _off["B"] + (h * 2 + st) * 64 + dj] = v
    k = p[:, None]
    q = p[None, :]
    allowed = (k // 64) <= (q // 64)
    for (h, j), di in DIDX.items():
        m = SLOPES[h]
        v = np.where(allowed, -m * np.abs(q - k) + m * (128.0 * j + q), NEG)
        t[:, _off["Dt"] + di * 128:_off["Dt"] + di * 128 + 128] = v
    for h in range(4):
        v = np.where(allowed, GAM[h] ** np.abs(q - k) / 8.0, 0.0)
        t[:, _off["DM"] + h * 128:_off["DM"] + h * 128 + 128] = v
        t[:, _off["Ddec"] + h * 64:_off["Ddec"] + h * 64 + 64] = (GAM[h] ** (127.0 - p))[:, None]
    for pair in range(2):
        for half in range(2):
            h = 2 * pair + half
            rows = slice(64 * half, 64 * half + 64)
            t[rows, _off["Dq"] + pair * 128:_off["Dq"] + pair * 128 + 128] = (GAM[h] ** (p + 1.0) / 8.0)[None, :]
            t[rows, _off["g128"] + pair] = GAM[h] ** 128.0
    return t.astype(np.float32)


def _ptab(inp):
    t = np.zeros((128, NPT), np.float32)
    t[:, _poff["g1"]:_poff["g1"] + 8] = inp["norm_mix_g"][0].reshape(8, 128).T
    t[:, _poff["g2"]:_poff["g2"] + 8] = inp["norm_ffn_g"][0].reshape(8, 128).T
    t[:, _poff["gf"]:_poff["gf"] + 1024] = np.broadcast_to(inp["final_norm_g"][None, :], (128, 1024))
    t[:, _poff["subg"]] = inp["diff_subln_g"][0]
    t[:, _poff["gng"]:_poff["gng"] + 4] = inp["ret_gn_g"][0].reshape(4, 128).T
    t[:, _poff["cw"]:_poff["cw"] + 66] = inp["conv_w"][0].reshape(3, 22, 128).transpose(2, 0, 1).reshape(128, 66)
    t[:, _poff["cb"]:_poff["cb"] + 22] = inp["conv_b"][0].reshape(22, 128).T
    lv = np.concatenate([inp["lambda_q1"][0], inp["lambda_k1"][0], inp["lambda_q2"][0], inp["lambda_k2"][0]])
    t[:, _poff["lamv"]:_poff["lamv"] + 256] = np.broadcast_to(lv[None, :], (128, 256))
    return t


class Tok:
    __slots__ = ("sem", "val", "eng")

    def __init__(self, sem, val, eng):
        self.sem, self.val, self.eng = sem, val, eng


class Buf:
    def __init__(self, name=""):
        self.name = name
        self.w = None
        self.rs = []
        self.excl = name in ("S0", "S1", "O0", "O1", "Zb", "TR", "G0", "G1", "SA", "SB")
        self.group = []
        self.alias = []


class Sched:
    ENGS = ("pe", "act", "dve", "pool", "sp")

    def __init__(self, nc, es):
        self.nc = nc
        self.es = es
        self.prog = {e: [] for e in self.ENGS}
        self.sem = {e: es.enter_context(nc.semaphore("ps_" + e)) for e in self.ENGS}
        self.count = {e: 0 for e in self.ENGS}
        self.waited = {e: {} for e in self.ENGS}
        self.dsems = {}
        self.dcount = {}

    def _waits(self, eng, reads, writes):
        need = []
        for b in reads:
            if b.w is not None:
                need.append((b.w, True))
            for ob in b.alias:
                if ob.w is not None:
                    need.append((ob.w, True))
            if b.excl:
                for t in b.rs:
                    need.append((t, False))
                for ob in b.group + b.alias:
                    for t in ob.rs:
                        need.append((t, False))
        for b in writes:
            if b.w is not None:
                need.append((b.w, False))
            for t in b.rs:
                need.append((t, False))
            for ob in b.alias:
                if ob.w is not None:
                    need.append((ob.w, False))
                for t in ob.rs:
                    need.append((t, False))
        best = {}
        for t, raw in need:
            if t.eng == eng and (eng == "pe" or not raw):
                continue
            k = id(t.sem)
            if self.waited[eng].get(k, 0) >= t.val:
                continue
            if k not in best or best[k].val < t.val:
                best[k] = t
        out = []
        for k, t in best.items():
            self.waited[eng][k] = t.val
            out.append(t)
            if os.environ.get("KLOG"):
                print("LOG", eng, "wait", t.eng, t.val)
        return out

    def _emit_waits(self, eng, toks):
        for t in toks:
            self.prog[eng].append(lambda e, sem=t.sem, val=t.val: e.wait_ge(sem, val))

    def _commit(self, tok, reads, writes):
        for b in reads:
            b.rs = [t for t in b.rs if t.sem is not tok.sem]
            b.rs.append(tok)
        for b in writes:
            b.w = tok
            b.rs = []

    def op(self, eng, fn, reads=(), writes=(), inc=True):
        toks = self._waits(eng, reads, writes)
        emb = None
        if toks and not os.environ.get("KNOEMB"):
            emb = toks[-1]
            toks = toks[:-1]
        self._emit_waits(eng, toks)
        if emb is not None:
            fn0, esem, eval_ = fn, emb.sem, emb.val
            fn = lambda e, fn0=fn0, esem=esem, eval_=eval_: fn0(e)._wait_ge(esem, eval_)
        sem = self.sem[eng]
        tok = Tok(sem, self.count[eng] + 1, eng)
        if os.environ.get("KLOG"):
            print("LOG", eng, "op", inc, self.count[eng] + 1, [b.name for b in reads], [b.name for b in writes])
        if inc:
            self.count[eng] += 1
            self.prog[eng].append(lambda e, fn=fn, sem=sem: fn(e).then_inc(sem, 1))
        else:
            self.prog[eng].append(lambda e, fn=fn: fn(e))
        self._commit(tok, reads, writes)
        return tok

    def dma(self, eng, dname, out, in_, reads=(), writes=()):
        self._emit_waits(eng, self._waits(eng, reads, writes))
        if dname not in self.dsems:
            self.dsems[dname] = self.es.enter_context(self.nc.semaphore("d_" + dname))
            self.dcount[dname] = 0
        sem = self.dsems[dname]
        self.dcount[dname] += 16
        tok = Tok(sem, self.dcount[dname], "dma")
        self.prog[eng].append(lambda e, out=out, in_=in_, sem=sem: e.dma_start(out=out, in_=in_).then_inc(sem, 16))
        self._commit(tok, reads, writes)
        return tok

    def wait_tok(self, eng, tok):
        k = id(tok.sem)
        if self.waited[eng].get(k, 0) >= tok.val:
            return
        self.waited[eng][k] = tok.val
        self.prog[eng].append(lambda e, sem=tok.sem, val=tok.val: e.wait_ge(sem, val))

    def emit(self):
        with self.nc.Block() as block:
            @block.sync
            def _(e):
                for f in self.prog["sp"]:
                    f(e)

            @block.tensor
            def _(e):
                for f in self.prog["pe"]:
                    f(e)

            @block.scalar
            def _(e):
                for f in self.prog["act"]:
                    f(e)

            @block.vector
            def _(e):
                for f in self.prog["dve"]:
                    f(e)

            @block.gpsimd
            def _(e):
                for f in self.prog["pool"]:
                    f(e)


import os


class _Stop(Exception):
    pass


_CUR = [0]


def ck(name):
    ks = os.environ.get("KSTOP")
    if ks == name or ks == f"{name}@{_CUR[0]}":
        raise _Stop()


def build_nc(h0=16, nown=16):
    first_t, last_t = h0 - 1, h0 + nown
    nc = bass.Bass("TRN2", target_bir_lowering=False)
    xin = nc.dram_tensor("x", [256 * (h0 + nown), D], F32, kind="ExternalInput").ap()
    ctab_d = nc.dram_tensor("ctab", [128, NCT], F32, kind="ExternalInput").ap()
    ptab_d = nc.dram_tensor("ptab", [128, NPT], F32, kind="ExternalInput").ap()
    w_in_d = nc.dram_tensor("w_in", [D, 3072], F32, kind="ExternalInput").ap()
    w_out_d = nc.dram_tensor("w_out", [D, D], F32, kind="ExternalInput").ap()
    w_up_d = nc.dram_tensor("w_up", [D, 2 * FF], F32, kind="ExternalInput").ap()
    w_dn_d = nc.dram_tensor("w_down", [FF, D], F32, kind="ExternalInput").ap()
    yout = nc.dram_tensor("y", [256 * nown, D], F32, kind="ExternalOutput").ap()
    win_s = nc.dram_tensor("win_s", [12, 128, 8, 256], BF16, kind="Internal").ap()
    wout_s = nc.dram_tensor("wout_s", [4, 128, 8, 256], BF16, kind="Internal").ap()
    wup_s = nc.dram_tensor("wup_s", [22, 128, 8, 256], BF16, kind="Internal").ap()
    wdn_s = nc.dram_tensor("wdn_s", [4, 3, 128, 8, 256], BF16, kind="Internal").ap()

    es = ExitStack()
    with es:
        S = Sched(nc, es)
        A = nc.alloc_sbuf_tensor

        def PS(name, shape, dt=F32):
            return es.enter_context(nc.psum_tensor(name, shape, dt))

        ctab = A("ctab_t", [128, NCT], F32)
        ptab = A("ptab_t", [128, NPT], F32)
        idb = A("idb", [128, 128], BF16)
        selb = A("selb", [128, 256], BF16)
        epsc = A("epsc", [128, 1], F32)
        lamc = A("lamc", [128, 4], F32)
        lamt = A("lamt", [128, 128], F32)
        gcol = A("gcol", [128, 1], F32)
        KT = [A(f"KT{h}", [128, RING[h] * 128], BF16) for h in range(4)]
        VV = [A(f"VV{h}", [128, RING[h], 128], BF16) for h in range(4)]
        wring = [A(f"wr{i}", [128, 8, 256], BF16) for i in range(NW)]
        xb = [A(f"xb{i}", [128, D], F32) for i in range(4)]
        xn = [A(f"xn{i}", [128, D], BF16) for i in range(2)]
        ssq = A("ssq", [128, 8], F32)
        hT = A("hT", [128, 8, 256], BF16)
        QTc = A("QTc", [128, 4, 2, 256], BF16)
        rqT = A("rqT", [128, 2, 256], BF16)
        rqdT = A("rqdT", [128, 2, 256], BF16)
        rkT = A("rkT", [128, 2, 256], BF16)
        kd = [A(f"kd{i}", [128, 256], BF16) for i in range(2)]
        rvt = [A(f"rvt{i}", [128, 512], BF16) for i in range(2)]
        St = A("St", [128, 2, 128], F32)
        Stb = A("Stb", [128, 2, 128], BF16)
        sg = A("sg", [128, 4, 256], F32)
        oT = A("oT", [128, 8, 256], BF16)
        PT = [A(f"PT{i}", [128, 2, 256], BF16) for i in range(3)]
        dtmp = [A("dtmp0", [128, 2, 128], F32)] * 2
        zaccs = [A("zacc0", [128, 512], F32), A("zacc1", [128, 512], F32)]
        epsz = A("epsz", [128, 1], F32)
        t1 = A("t1", [128, 512], F32)
        t1s = [t1, A("t1b", [128, 512], F32)]
        t2 = A("t2", [128, 512], F32)
        t3 = A("t3", [128, 512], F32)
        t4 = A("t4", [128, 512], F32)
        AT = [A(f"AT{i}", [128, 128], BF16) for i in range(4)]
        gT = A("gT", [128, NFC, 256], BF16)
        aext = [A(f"aext{i}", [128, 258], F32) for i in range(2)]
        cc = [A(f"cc{i}", [128, 256], F32) for i in range(3)]
        carry = A("carry", [128, NFC, 2], F32)

        S2 = PS("S2", [128, 2, 512])
        O2 = PS("O2", [128, 2, 512])
        Zb = PS("Zb", [128, 512])
        TRf = PS("TR", [128, 512], F32)
        TR = TRf[:].bitcast(BF16)
        G = [PS("G0", [128, 512]), PS("G1", [128, 512])]

        B = {}

        def bf(name):
            if name not in B:
                B[name] = Buf(name)
            return B[name]

        in_att = [False]
        gctr = [0]

        def galloc():
            if in_att[0]:
                i = gctr[0] % 2
                gctr[0] += 1
                return G[i], bf(f"G{i}")
            i = gctr[0] % 4
            gctr[0] += 1
            if i < 2:
                return G[i], bf(f"G{i}")
            return S2[:, i - 2, :], bf(f"S{i - 2}")

        def ct(name, lo=0, n=None):
            o = _off[name] + lo
            return ctab[:, o:o + (n if n is not None else 1)]

        def pt(name, lo=0, n=None):
            o = _poff[name] + lo
            return ptab[:, o:o + (n if n is not None else 1)]

        tc = S.dma("sp", "c0", ctab[:], ctab_d, writes=[bf("ctab")])
        tp = S.dma("sp", "c1", ptab[:], ptab_d, writes=[bf("ptab")])
        for wsrc, wdst, nm in ((w_in_d, win_s, "win"), (w_out_d, wout_s, "wout"), (w_up_d, wup_s, "wup")):
            tok = None
            for kc in range(8):
                tok = S.dma("pool", "cast_" + nm, wdst[:, :, kc, :], wsrc[128 * kc:128 * kc + 128, :].rearrange("p (u n) -> u p n", n=256))
            bf(nm).w = tok
        tok = None
        for rb in range(NFC):
            tok = S.dma("pool", "cast_wdn", wdn_s[:, rb // 8, :, rb % 8, :], w_dn_d[128 * rb:128 * rb + 128, :].rearrange("p (u n) -> u p n", n=256))
        bf("wdn").w = tok
        S.op("dve", lambda e: e.tensor_copy(out=idb[:], in_=ct("ident", 0, 128)), reads=[bf("ctab")], writes=[bf("idb")])
        S.op("dve", lambda e: e.tensor_copy(out=selb[:], in_=ct("sel0", 0, 256)), reads=[bf("ctab")], writes=[bf("selb")])
        S.op("dve", lambda e: e.memset(epsc[:], EPS), writes=[bf("epsc")])
        S.op("dve", lambda e: e.memset(St[:], 0.0), writes=[bf("St")])
        S.op("dve", lambda e: e.memset(Stb[:], 0.0), writes=[bf("Stb")])
        S.op("dve", lambda e: e.memset(carry[:], 0.0), writes=[bf("carry")])
        S.op("dve", lambda e: e.memset(oT[:], 0.0), writes=[bf(f"oT{k_}") for k_ in range(8)])
        S.op("dve", lambda e: e.memset(QTc[:], 0.0), writes=[bf("QT")])
        S.op("dve", lambda e: e.memset(epsz[:], 1e-18), writes=[bf("epsz")])
        S.op("dve", lambda e: e.tensor_tensor(out=lamt[:, 0:64], in0=pt("lamv", 0, 64), in1=pt("lamv", 64, 64), op=ALU.mult),
             reads=[bf("ptab")], writes=[bf("lamt")])
        S.op("dve", lambda e: e.tensor_tensor(out=lamt[:, 64:128], in0=pt("lamv", 128, 64), in1=pt("lamv", 192, 64), op=ALU.mult),
             reads=[bf("ptab")], writes=[bf("lamt")])
        S.op("dve", lambda e: e.reduce_sum(out=lamc[:, 2:3], in_=lamt[:, 0:64], axis=mybir.AxisListType.X), reads=[bf("lamt")], writes=[bf("lamc")])
        S.op("dve", lambda e: e.reduce_sum(out=lamc[:, 3:4], in_=lamt[:, 64:128], axis=mybir.AxisListType.X), reads=[bf("lamt")], writes=[bf("lamc")])
        S.op("act", lambda e: e.activation(out=lamc[:, 2:4], in_=lamc[:, 2:4], func=AF.Exp), reads=[bf("lamc")], writes=[bf("lamc")])
        S.op("dve", lambda e: e.tensor_tensor(out=lamc[:, 0:1], in0=lamc[:, 2:3], in1=lamc[:, 3:4], op=ALU.subtract), reads=[bf("lamc")], writes=[bf("lamc")])
        S.op("dve", lambda e: e.tensor_scalar(out=lamc[:, 1:2], in0=lamc[:, 0:1], scalar1=-1.0, scalar2=-LAM_INIT, op0=ALU.mult, op1=ALU.add),
             reads=[bf("lamc")], writes=[bf("lamc2")])
        S.op("dve", lambda e: e.tensor_scalar(out=gcol[:], in0=pt("subg"), scalar1=1.0 - LAM_INIT, scalar2=None, op0=ALU.mult),
             reads=[bf("ptab")], writes=[bf("gcol")])

        wctr = [0]

        def load_unit(src, nk, srcbuf):
            i = wctr[0] % NW
            wctr[0] += 1
            S.dma("sp", f"w{i}", wring[i][:, 0:nk, :], src, reads=[bf(srcbuf)], writes=[bf(f"wr{i}")])
            return wring[i], bf(f"wr{i}")

        def unit_in(c0):
            return load_unit(win_s[c0 // 256], 8, "win")

        def load_x(i):
            S.dma("sp", f"x{i % 4}", xb[i % 4][:], xin[i * 128:(i + 1) * 128, :], writes=[bf(f"xb{i % 4}")])

        def mm(out, lhsT, rhs, start, stop, reads, writes, inc, **kw):
            S.op("pe", lambda e: e.matmul(out=out, lhsT=lhsT, rhs=rhs, start=start, stop=stop, **kw), reads=reads, writes=writes, inc=inc)


        def ACT(out, in_, func, reads, writes, **kw):
            return S.op("act", lambda e: e.activation(out=out, in_=in_, func=func, **kw), reads=reads, writes=writes)

        def TT(out, in0, in1, op, reads, writes, eng="dve"):
            S.op(eng, lambda e: e.tensor_tensor(out=out, in0=in0, in1=in1, op=op), reads=reads, writes=writes)

        def TS(out, in0, s1, s2, op0, op1, reads, writes):
            if op1 is None:
                S.op("dve", lambda e: e.tensor_scalar(out=out, in0=in0, scalar1=s1, scalar2=None, op0=op0), reads=reads, writes=writes)
            else:
                S.op("dve", lambda e: e.tensor_scalar(out=out, in0=in0, scalar1=s1, scalar2=s2, op0=op0, op1=op1), reads=reads, writes=writes)

        def STT(out, in0, scalar, in1, op0, op1, reads, writes):
            S.op("dve", lambda e: e.scalar_tensor_tensor(out=out, in0=in0, scalar=scalar, in1=in1, op0=op0, op1=op1), reads=reads, writes=writes)

        def CP(out, in_, reads, writes, eng="dve"):
            if os.environ.get("KCP") == "copy":
                S.op(eng, lambda e: e.tensor_copy(out=out, in_=in_), reads=reads, writes=writes)
            else:
                S.op(eng, lambda e: e.tensor_scalar(out=out, in0=in_, scalar1=1.0, scalar2=None, op0=ALU.mult), reads=reads, writes=writes)

        def TRN(out, in_, reads, writes, inc):
            S.op("pe", lambda e: e.transpose(out=out, in_=in_, identity=idb[:]), reads=reads, writes=writes, inc=inc)

        def norm_T(i, tt, gname):
            xt, bx = xb[i % 4], bf(f"xb{i % 4}")
            xnn, bxn = xn[i % 2], bf(f"xn{i % 2}")
            sc = ssq[:, (i % 2) * 4:(i % 2) * 4 + 1]
            rs = ssq[:, (i % 2) * 4 + 1:(i % 2) * 4 + 2]
            bs = bf(f"ssq{i % 2}")
            ACT(xnn[:], xt[:], AF.Square, [bx], [bxn, bs], accum_out=sc)
            ACT(rs, sc, AF.Ln, [bs, bf("epsc")], [bs], scale=1.0 / D, bias=epsc[:])
            ACT(rs, rs, AF.Exp, [bs], [bs], scale=-0.5)
            ACT(xnn[:], xt[:], AF.Copy, [bx, bs], [bxn], scale=rs)
            for c in range(8):
                TRN(TR[:, c * 128:(c + 1) * 128], xnn[:, c * 128:(c + 1) * 128], [bxn, bf("idb")], [bf("TR")], inc=(c == 7))
            for c in range(8):
                TS(hT[:, c, tt * 128:(tt + 1) * 128], TR[:, c * 128:(c + 1) * 128], pt(gname, c, 1), None, ALU.mult, None,
                   [bf("TR"), bf("ptab")], [bf("hT")])

        def v2(ap):
            return ap.rearrange("p (a b) -> p a b", a=2)

        def proj_fm(wt, bw, cols, lo=0):
            bank, bb = galloc()
            n = len(cols)
            for ci, c0 in enumerate(cols):
                for kc in range(8):
                    mm(bank[:, ci * 256 + lo:ci * 256 + 256], wt[:, kc, c0:c0 + 128], hT[:, kc, lo:256], kc == 0, kc == 7,
                       [bf("hT"), bw], [bb], inc=(kc == 7 and ci == n - 1))
            return bank, bb

        def proj_tm(wt, bw, tt):
            bank, bb = galloc()
            for kc in range(8):
                mm(bank[:, 0:256], hT[:, kc, tt * 128:(tt + 1) * 128], wt[:, kc, :], kc == 0, kc == 7,
                   [bf("hT"), bw], [bb], inc=(kc == 7))
            return bank, bb

        def kt_lo(h, T):
            lo = 0
            Q0 = 256 * T
            for kt in range(2 * T + 2):
                if (Q0 - 128 * kt - 127) * SLOPES[h] >= THR:
                    lo = kt + 1
            return lo

        def need_kv(h, T):
            for kt in (2 * T, 2 * T + 1):
                for Tq in range(max(T, first_t), last_t):
                    if kt >= kt_lo(h, Tq):
                        return True
            return False

        def segs(h, T, kt, qlo):
            out = []
            if h == 0:
                for qb in (0, 1):
                    lo, hi = max(qlo, 128 * qb), 128 * qb + 128
                    if lo >= hi:
                        continue
                    ktd = 2 * T + qb
                    if kt < ktd:
                        out.append((lo, hi, "past", ktd - kt, 0))
                    elif kt == ktd:
                        out.append((lo, hi, "diag", 0, 128 * qb))
            else:
                if kt < 2 * T:
                    out.append((qlo, 256, "past", 2 * T - kt, 0))
                elif kt == 2 * T:
                    if qlo < 128:
                        out.append((qlo, 128, "diag", 0, 0))
                    out.append((max(qlo, 128), 256, "past", 0, 0))
                else:
                    out.append((max(qlo, 128), 256, "diag", 1, 128))
            return out

        actr = [0]
        att_stages = []
        ones_c = _off["ones"]
        avg_ap = ctab[:, _off["avg"]:_off["avg"] + 128]

        def sbank(pb):
            return S2[:, pb, :] if pb < 2 else TRf[:, :]

        def sbuf_(pb):
            return bf(f"S{pb}") if pb < 2 else bf("TR")

        def attention(T, h, qlo):
            pbO = h % 2
            bO = bf(f"O{pbO}")
            bZ = bf("Zb")
            zoff = 256 * (h % 2)
            kts = [kt for kt in range(kt_lo(h, T), 2 * T + 2) if segs(h, T, kt, qlo)]
            info = []
            for kt in kts:
                sg_ = segs(h, T, kt, qlo)
                clo = min(s[0] for s in sg_)
                pb = actr[0] % 3
                pi = actr[0] % 3
                actr[0] += 1
                info.append((kt, sg_, clo, pb, pi))

            def qk(ki):
                kt, sg_, clo, pb, pi = info[ki]
                slot = kt % RING[h]
                if clo == 0:
                    mm(sbank(pb), KT[h][:, slot * 128:(slot + 1) * 128], QTc[:, h, :, :].rearrange("p a b -> p (a b)"),
                       True, True, [bf(f"KT{h}_{slot}"), bf("QT")], [sbuf_(pb)], inc=True)
                else:
                    for half in range(2):
                        mm(sbank(pb)[:, 256 * half + clo:256 * half + 256], KT[h][:, slot * 128:(slot + 1) * 128], QTc[:, h, half, clo:256],
                           True, True, [bf(f"KT{h}_{slot}"), bf("QT")], [sbuf_(pb)], inc=(half == 1))

            exp_toks = []
            add_toks = []
            pool_toks = []
            qk(0)
            if len(info) > 1:
                qk(1)
            for ki, (kt, sg_, clo, pb, pi) in enumerate(info):
                bS = sbuf_(pb)
                slot = kt % RING[h]
                bV = bf(f"VV{h}_{slot}")
                bP = bf(f"PT{pi}")
                sset = 1 if kt < 2 * h0 else 0
                if ki >= 2:
                    S.wait_tok("act", add_toks[ki - 2])
                for (lo, hi, typ, prm, bs0) in sg_:
                    if typ == "past":
                        col = _off["B"] + (h * 2 + sset) * 64 + prm
                        tk = ACT(PT[pi][:, :, lo:hi], v2(sbank(pb))[:, :, lo:hi], AF.Exp, [bS, bf("ctab")], [bP], bias=ctab[:, col:col + 1])
                    else:
                        di = DIDX[(h, prm)]
                        dsel = (ki + h) % 2
                        dt_, bd = dtmp[dsel], bf("dtmp0")
                        dcol = _off["Dt"] + di * 128
                        for half in range(2):
                            TT(dt_[:, half, lo - bs0:hi - bs0], sbank(pb)[:, 256 * half + lo:256 * half + hi],
                               ctab[:, dcol + lo - bs0:dcol + hi - bs0], ALU.add, [bS, bf("ctab")], [bd])
                        tk = ACT(PT[pi][:, :, lo:hi], dt_[:, :, lo - bs0:hi - bs0], AF.Exp, [bd], [bP])
                exp_toks.append(tk)
                if ki + 2 < len(info):
                    qk(ki + 2)
                if ki % 2 == 1 and att_stages:
                    att_stages.pop(0)()
                first, last = (ki == 0), (ki == len(info) - 1)
                if clo == 0:
                    mm(O2[:, pbO, :], VV[h][:, slot, :], PT[pi][:].rearrange("p a b -> p (a b)"), first, last,
                       [bV, bP], [bO], inc=last, skip_group_check=True)
                else:
                    for half in range(2):
                        mm(O2[:, pbO, 256 * half + clo:256 * half + 256], VV[h][:, slot, :], PT[pi][:, half, clo:256], first and half == 0, last,
                           [bV, bP], [bO], inc=(last and half == 1), skip_group_check=True)
                zv = v2(Zb[:])
                zaccv = v2(zaccs[h % 2][:])
                bzB = bf(f"zaccB{h % 2}")
                S.wait_tok("pool", exp_toks[ki])
                if first:
                    tokp = S.op("pool", lambda e, zaccv=zaccv, pi=pi, clo=clo: e.tensor_copy(out=zaccv[:, 1, clo:256], in_=PT[pi][:, 1, clo:256]),
                                reads=[], writes=[bzB])
                else:
                    tokp = S.op("pool", lambda e, zaccv=zaccv, pi=pi, clo=clo: e.tensor_tensor(out=zaccv[:, 1, clo:256], in0=zaccv[:, 1, clo:256],
                                                                                             in1=PT[pi][:, 1, clo:256], op=ALU.add), reads=[bzB], writes=[bzB])
                if pool_toks:
                    S.wait_tok("dve", pool_toks[-1])
                pool_toks.append(tokp)
                if first:
                    tokd = S.op("dve", lambda e, zv=zv, pi=pi, clo=clo: e.tensor_copy(out=zv[:, 0, clo:256], in_=PT[pi][:, 0, clo:256]), reads=[bP], writes=[bZ])
                else:
                    tokd = S.op("dve", lambda e, zv=zv, pi=pi, clo=clo: e.tensor_tensor(out=zv[:, 0, clo:256], in0=zv[:, 0, clo:256],
                                                                                       in1=PT[pi][:, 0, clo:256], op=ALU.add), reads=[bZ, bP], writes=[bZ])
                add_toks.append(tokd)
            S.wait_tok("act", pool_toks[-1])
            zacc, bza = zaccs[h % 2], bf(f"zacc{h % 2}")
            ACT(zacc[:, qlo:256], Zb[:, qlo:256], AF.Copy, [bZ], [bza])
            bzB_ = bf(f"zaccB{h % 2}")
            t1h, bt1 = t1s[h % 2], bf(f"t1_{h % 2}")
            t1v = v2(t1h[:])
            ACT(t1v[:, :, qlo:256], v2(O2[:, pbO, :])[:, :, qlo:256], AF.Copy, [bO], [bt1])
            hold = {}
            rinv = v2(t4[:])
            t2v = v2(t2[:])

            def s1():
                bank, bb = galloc()
                if qlo == 0:
                    mm(bank[:, :], ctab[:, ones_c:ones_c + 128], zacc[:, :], True, True, [bza, bzB_, bf("ctab")], [bb], inc=True)
                else:
                    for half in range(2):
                        mm(bank[:, 256 * half + qlo:256 * half + 256], ctab[:, ones_c:ones_c + 128], zacc[:, 256 * half + qlo:256 * half + 256],
                           True, True, [bza, bzB_, bf("ctab")], [bb], inc=(half == 1))
                ACT(rinv[:, :, qlo:256], v2(bank[:])[:, :, qlo:256], AF.Ln, [bb, bf("epsz")], [bf("t4"), bf("t4b")], bias=epsz[:])

            def s2():
                ACT(rinv[:, :, qlo:256], rinv[:, :, qlo:256], AF.Exp, [bf("t4")], [bf("t4"), bf("t4b")], scale=-1.0)
                TS(rinv[:, 1, qlo:256], rinv[:, 1, qlo:256], lamc[:, 1:2], None, ALU.mult, None, [bf("t4"), bf("lamc2")], [bf("t4"), bf("t4b")])

            def s3():
                TT(t2v[:, :, qlo:256], t1v[:, :, qlo:256], rinv[:, :, qlo:256], ALU.mult, [bt1, bf("t4")], [bf("t2")])
                TT(t3[:, qlo:256], t2v[:, 0, qlo:256], t2v[:, 1, qlo:256], ALU.add, [bf("t2")], [bf("t3")])
                ACT(t2[:, qlo:256], t3[:, qlo:256], AF.Square, [bf("t3")], [bf("t2")])

            def s4():
                bank2, bb2 = galloc()
                mm(bank2[:, qlo:256], avg_ap, t2[:, qlo:256], True, True, [bf("t2"), bf("ctab")], [bb2], inc=True)
                ACT(t2[:, 256 + qlo:512], bank2[:, qlo:256], AF.Ln, [bb2, bf("epsc")], [bf("t2")], bias=epsc[:])

            def s5():
                ACT(t2[:, 256 + qlo:512], t2[:, 256 + qlo:512], AF.Exp, [bf("t2")], [bf("t2")], scale=-0.5)
                STT(oT[:, h, qlo:256], t3[:, qlo:256], gcol[:], t2[:, 256 + qlo:512], ALU.mult, ALU.mult,
                    [bf("t3"), bf("t2"), bf("gcol")], [bf(f"oT{h}")])
            return [s1, s2, s3, s4, s5]

        prenorm = set()

        def tile(T, mode):
            own = mode != "hist"
            qlo = 254 if mode == "halo" else 0
            full = mode == "own"
            if T not in prenorm:
                for tt in range(2):
                    norm_T(2 * T + tt, tt, "g1")
            if not own:
                ensure_loaded(min(2 * T + 4, 2 * last_t))
            ck("norm")
            hs_kv = [h for h in range(4) if need_kv(h, T)]
            if own:
                wt, bw = unit_in(1536)
                bank, bb = proj_fm(wt, bw, [0, 128], lo=qlo)
                bv = v2(bank[:])
                ACT(rqT[:, :, qlo:256], bv[:, :, qlo:256], AF.Copy, [bb], [bf("rqT")])
                dqv = v2(ctab[:, _off["Dq"]:_off["Dq"] + 256])
                for tt in range(2):
                    lo = max(qlo, 128 * tt)
                    if lo >= 128 * tt + 128:
                        continue
                    TT(rqdT[:, :, lo:128 * tt + 128], bv[:, :, lo:128 * tt + 128], dqv[:, :, lo - 128 * tt:128], ALU.mult,
                       [bb, bf("ctab")], [bf("rqdT")])
            wt, bw = unit_in(1792)
            for tt in range(2):
                bank, bb = proj_tm(wt, bw, tt)
                TT(kd[tt][:], bank[:, 0:256], ctab[:, _off["Ddec"]:_off["Ddec"] + 256], ALU.mult, [bb, bf("ctab")], [bf(f"kd{tt}")])
            if own:
                bank, bb = proj_fm(wt, bw, [0, 128], lo=0)
                ACT(rkT[:], v2(bank[:]), AF.Copy, [bb], [bf("rkT")])
            ck("rk")
            for u in range(2):
                wt, bw = unit_in(2048 + 256 * u)
                for tt in range(2):
                    bank, bb = proj_tm(wt, bw, tt)
                    ACT(rvt[tt][:, 256 * u:256 * u + 256], bank[:, 0:256], AF.Copy, [bb], [bf(f"rvt{tt}_{u}")])
            if own:
                for u in range(2):
                    wt, bw = unit_in(2560 + 256 * u)
                    bank, bb = proj_fm(wt, bw, [0, 128], lo=qlo)
                    ACT(sg[:, 2 * u:2 * u + 2, qlo:256], v2(bank[:])[:, :, qlo:256], AF.Silu, [bb], [bf(f"sg{u}")])
            stages = []
            rbanks = [(O2[:, 0, :], bf("O0")), (O2[:, 1, :], bf("O1"))] if own else None
            for tt in range(2):
                if own and not (mode == "halo" and tt == 0):
                    lo = max(qlo, 128 * tt) - 128 * tt

                    def st_a(tt=tt, lo=lo):
                        for h in range(4):
                            rows = slice(64 * (h % 2), 64 * (h % 2) + 64)
                            pr = h // 2
                            so = 128 * pr
                            mm(S2[:, h % 2, so + lo:so + 128], rkT[rows, pr, 128 * tt:128 * tt + 128], rqT[rows, pr, 128 * tt + lo:128 * tt + 128],
                               True, True, [bf("rkT"), bf("rqT")], [bf(f"S{h % 2}")], inc=True)

                        st_b(tt, lo)

                    def st_b(tt, lo):
                        for h in range(4):
                            so = 128 * (h // 2)
                            dmc = _off["DM"] + 128 * h
                            TT(AT[h][:, lo:128], S2[:, h % 2, so + lo:so + 128], ctab[:, dmc + lo:dmc + 128], ALU.mult,
                               [bf(f"S{h % 2}"), bf("ctab")], [bf(f"AT{h}")])

                    def st_c(tt=tt, lo=lo):
                        for h in range(4):
                            rows = slice(64 * (h % 2), 64 * (h % 2) + 64)
                            pr = h // 2
                            rb = rbanks[pr][0]
                            rbb = [bf(f"O{pr}")]
                            oo = 256 * (h % 2) + 128 * tt
                            mm(rb[:, oo + lo:oo + 128], rvt[tt][:, 128 * h:128 * h + 128], AT[h][:, lo:128], True, False,
                               [bf(f"AT{h}"), bf(f"rvt{tt}_{h // 2}")], rbb, inc=False)
                            mm(rb[:, oo + lo:oo + 128], Stb[rows, pr, :], rqdT[rows, pr, 128 * tt + lo:128 * tt + 128], False, True,
                               [bf("Stb"), bf("rqdT")], rbb, inc=True)
                    stages += [st_a, st_c]

                def st_d(tt=tt):
                    bank, bb = galloc()
                    kvv = v2(bank[:, 0:256])
                    for h in range(4):
                        mm(kvv[64 * (h % 2):64 * (h % 2) + 64, h // 2, :], kd[tt][:, 64 * h:64 * h + 64], rvt[tt][:, 128 * h:128 * h + 128],
                           True, True, [bf(f"kd{tt}"), bf(f"rvt{tt}_{h // 2}")], [bb], inc=(h == 3))

                    def st_e():
                        for pr in range(2):
                            STT(St[:, pr, :], St[:, pr, :], ctab[:, _off["g128"] + pr:_off["g128"] + pr + 1], kvv[:, pr, :], ALU.mult, ALU.add,
                                [bb, bf("St"), bf("ctab")], [bf("St")])
                        ACT(Stb[:], St[:], AF.Copy, [bf("St")], [bf("Stb")])
                    st_e()
                stages.append(st_d)
            if own:
                v1_, v2_, v3_, v4_ = v2(t1[:]), v2(t2[:]), v2(t3[:]), v2(t4[:])
                for pr in range(2):
                    hold = {}

                    def pa(pr=pr, hold=hold):
                        rv_ = v2(rbanks[pr][0])
                        ACT(v1_[:, :, qlo:256], rv_[:, :, qlo:256], AF.Copy, [bf(f"O{pr}")], [bf("t1_0")])
                        bk, bkb = Zb, bf("Zb")
                        bkv = v2(bk[:])
                        for a_ in range(2):
                            mm(bkv[:, a_, qlo:256], avg_ap, v1_[:, a_, qlo:256], True, True, [bf("t1_0"), bf("ctab")], [bkb], inc=(a_ == 1))
                        hold["bkv"], hold["bkb"] = bkv, bkb

                    def pb_(pr=pr, hold=hold):
                        TT(v2_[:, :, qlo:256], v1_[:, :, qlo:256], hold["bkv"][:, :, qlo:256], ALU.subtract, [bf("t1_0"), hold["bkb"]], [bf("t2")])
                        ACT(v3_[:, :, qlo:256], v2_[:, :, qlo:256], AF.Square, [bf("t2")], [bf("t3")])
                        bk2, bkb2 = rbanks[pr][0], bf(f"O{pr}")
                        bkv2 = v2(bk2)
                        for a_ in range(2):
                            mm(bkv2[:, a_, qlo:256], avg_ap, v3_[:, a_, qlo:256], True, True, [bf("t3"), bf("ctab")], [bkb2], inc=(a_ == 1))
                        hold["bkv2"], hold["bkb2"] = bkv2, bkb2

                    def pc(pr=pr, hold=hold):
                        ACT(v4_[:, :, qlo:256], hold["bkv2"][:, :, qlo:256], AF.Ln, [hold["bkb2"], bf("epsc")], [bf("t4"), bf("t4b")], bias=epsc[:])
                        ACT(v4_[:, :, qlo:256], v4_[:, :, qlo:256], AF.Exp, [bf("t4")], [bf("t4"), bf("t4b")], scale=-0.5)
                        for a_ in range(2):
                            h = 2 * pr + a_
                            STT(v1_[:, a_, qlo:256], v2_[:, a_, qlo:256], ptab[:, _poff["gng"] + h:_poff["gng"] + h + 1], v4_[:, a_, qlo:256],
                                ALU.mult, ALU.mult, [bf("t2"), bf("t4"), bf("ptab")], [bf("t1_0")])
                        TT(oT[:, 4 + 2 * pr:6 + 2 * pr, qlo:256], v1_[:, :, qlo:256], sg[:, 2 * pr:2 * pr + 2, qlo:256], ALU.mult,
                           [bf("t1_0"), bf(f"sg{pr}")], [bf(f"oT{4 + 2 * pr}"), bf(f"oT{5 + 2 * pr}")])
                    stages += [pa, pb_, pc]

            def filler(n=1):
                for _ in range(n):
                    if stages:
                        stages.pop(0)()

            if own:
                for u in range(2):
                    wt, bw = unit_in(256 * u)
                    filler(2)
                    bank, bb = proj_fm(wt, bw, [0, 128], lo=qlo)
                    for half in range(2):
                        rws = slice(64 * half, 64 * half + 64)
                        ACT(QTc[rws, 2 * u:2 * u + 2, half, qlo:256], v2(bank[:])[rws, :, qlo:256], AF.Copy, [bb], [bf("QT")], scale=0.125)
            for u in range(2):
                hh = [h for h in (2 * u, 2 * u + 1) if h in hs_kv]
                filler(2)
                if not hh:
                    continue
                wt, bw = unit_in(512 + 256 * u)
                bank, bb = proj_fm(wt, bw, [128 * (h - 2 * u) for h in hh])
                for ci, h in enumerate(hh):
                    s0 = (2 * T) % RING[h]
                    wr = [bf(f"KT{h}_{s0}"), bf(f"KT{h}_{s0 + 1}")]
                    if ci == 0:
                        ACT(KT[h][:, s0 * 128:s0 * 128 + 256], bank[:, 0:256], AF.Copy, [bb], wr)
                    else:
                        CP(KT[h][:, s0 * 128:s0 * 128 + 256], bank[:, 256:512], [bb], wr)
            ck("dk")
            for u in range(2):
                hh = [h for h in (2 * u, 2 * u + 1) if h in hs_kv]
                if not hh:
                    filler(2)
                    continue
                wt, bw = unit_in(1024 + 256 * u)
                for tt in range(2):
                    filler(2)
                    bank, bb = proj_tm(wt, bw, tt)
                    for h in hh:
                        slot = (2 * T + tt) % RING[h]
                        c = 128 * (h - 2 * u)
                        CP(VV[h][:, slot, :], bank[:, c:c + 128], [bb], [bf(f"VV{h}_{slot}")])
            filler(100)
            ck("ret")
            if not own:
                return
            ck("retpost")
            in_att[0] = True
            for h in range(4):
                new_st = attention(T, h, qlo)
                while att_stages:
                    att_stages.pop(0)()
                att_stages.extend(new_st)
            ck("attn")
            ensure_loaded(min(2 * T + 4, 2 * last_t))
            tts = [1] if mode == "halo" else [0, 1]
            ounits = [load_unit(wout_s[cb], 8, "wout") for cb in range(4)]
            obanks = [(O2[:, 0, :], bf("O0")), (O2[:, 1, :], bf("O1")), (S2[:, 0, :], bf("S0")), (S2[:, 1, :], bf("S1"))]
            korder = [4, 5, 6, 7, 0, 1, 2]
            oldo = bool(os.environ.get("KOLDO"))
            if oldo:
                korder = [4, 5, 6, 7, 0, 1, 2, 3]
            for cb in range(4):
                wt, bw = ounits[cb]
                bank, bb = obanks[cb]
                firstmm = True
                for _ in range(2):
                    if att_stages:
                        att_stages.pop(0)()
                for tt in tts:
                    for kc in korder:
                        mm(bank[:, 256 * tt:256 * tt + 256], oT[:, kc, 128 * tt:128 * tt + 128], wt[:, kc, :], firstmm, oldo and kc == 3,
                           [bf(f"oT{kc}"), bw], [bb], inc=(oldo and kc == 3), skip_group_check=True)
                        firstmm = False
            while att_stages:
                att_stages.pop(0)()
            in_att[0] = False
            for cb in range(4):
                wt, bw = ounits[cb]
                bank, bb = obanks[cb]
                for tt in tts:
                    if oldo:
                        continue
                    mm(bank[:, 256 * tt:256 * tt + 256], oT[:, 3, 128 * tt:128 * tt + 128], wt[:, 3, :], False, True,
                       [bf("oT3"), bw], [bb], inc=(tt == tts[-1]), skip_group_check=True)
                for tt in tts:
                    i = 2 * T + tt
                    xs = xb[i % 4][:, 256 * cb:256 * cb + 256]
                    TT(xs, xs, bank[:, 256 * tt:256 * tt + 256], ALU.add, [bb], [bf(f"xb{i % 4}")])
            ck("oproj")
            for tt in tts:
                norm_T(2 * T + tt, tt, "g2")
            chain = []
            for j in range(11):
                wa, bwa = load_unit(wup_s[j], 8, "wup")
                if full:
                    wb, bwb = load_unit(wup_s[11 + j], 8, "wup")
                for sub in range(2):
                    fc = 2 * j + sub
                    bank, bb = galloc()
                    for kc in range(8):
                        mm(bank[:, qlo:256], wa[:, kc, 128 * sub:128 * sub + 128], hT[:, kc, qlo:256], kc == 0, kc == 7, [bf("hT"), bwa], [bb],
                           inc=(kc == 7 and not full))
                    if not full:
                        ACT(carry[:, fc, :], bank[:, 254:256], AF.Copy, [bb], [bf(f"carry{fc}")])
                        continue
                    for kc in range(8):
                        mm(bank[:, 256:512], wb[:, kc, 128 * sub:128 * sub + 128], hT[:, kc, :], kc == 0, kc == 7, [bf("hT"), bwb], [bb], inc=(kc == 7))
                    ae, bae = aext[fc % 2], bf(f"aext{fc % 2}")
                    c_, bc_ = cc[fc % 3], bf(f"cc{fc % 3}")
                    bcar = bf(f"carry{fc}")
                    cwc = [ptab[:, _poff["cw"] + k * 22 + fc:_poff["cw"] + k * 22 + fc + 1] for k in range(3)]
                    ACT(ae[:, 0:2], carry[:, fc, :], AF.Copy, [bcar], [bae])
                    ACT(ae[:, 2:258], bank[:, 0:256], AF.Copy, [bb], [bae])
                    ACT(c_[:], ae[:, 0:256], AF.Identity, [bae, bf("ptab")], [bc_], scale=cwc[0], bias=ptab[:, _poff["cb"] + fc:_poff["cb"] + fc + 1])
                    ACT(carry[:, fc, :], ae[:, 256:258], AF.Copy, [bae], [bcar])
                    if chain:
                        chain[-1][0]()
                    STT(c_[:], ae[:, 1:257], cwc[1], c_[:], ALU.mult, ALU.add, [bae, bc_], [bc_])
                    STT(c_[:], ae[:, 2:258], cwc[2], c_[:], ALU.mult, ALU.add, [bae, bc_], [bc_])
                    if chain:
                        chain[-1][1]()
                        chain.pop()

                    def _gelu(c_=c_, bc_=bc_):
                        ACT(c_[:], c_[:], AF.Gelu_apprx_tanh, [bc_], [bc_])

                    def _mult(c_=c_, bc_=bc_, bank=bank, bb=bb, fc=fc):
                        TT(gT[:, fc, :], c_[:], bank[:, 256:512], ALU.mult, [bc_, bb], [bf("gT")])
                    chain.append((_gelu, _mult))
            while chain:
                chain[-1][0]()
                chain[-1][1]()
                chain.pop()
            ck("ffn_up")
            if not full:
                return
            if T + 1 < last_t and not os.environ.get("KNOPRE"):
                for tt in range(2):
                    norm_T(2 * (T + 1) + tt, tt, "g1")
                prenorm.add(T + 1)
            for cb in range(4):
                units = []
                for kg in range(3):
                    nk = 8 if kg < 2 else 6
                    units.append(load_unit(wdn_s[cb, kg, :, 0:nk, :], nk, "wdn"))
                bank, bb = galloc()
                for tt in range(2):
                    for fc in range(NFC):
                        wt, bw = units[fc // 8]
                        mm(bank[:, 256 * tt:256 * tt + 256], gT[:, fc, 128 * tt:128 * tt + 128], wt[:, fc % 8, :], fc == 0, fc == NFC - 1,
                           [bf("gT"), bw], [bb], inc=(fc == NFC - 1))
                for tt in range(2):
                    i = 2 * T + tt
                    xs = xb[i % 4][:, 256 * cb:256 * cb + 256]
                    TT(xs, xs, bank[:, 256 * tt:256 * tt + 256], ALU.add, [bb], [bf(f"xb{i % 4}")])
            ck("ffn_dn")
            for tt in range(2):
                i = 2 * T + tt
                xt, bx = xb[i % 4], bf(f"xb{i % 4}")
                sc = ssq[:, (i % 2) * 4 + 2:(i % 2) * 4 + 3]
                rs = ssq[:, (i % 2) * 4 + 3:(i % 2) * 4 + 4]
                bs = bf(f"ssqf{i % 2}")
                ACT(xn[i % 2][:], xt[:], AF.Square, [bx], [bf(f"xn{i % 2}"), bs], accum_out=sc)
                ACT(rs, sc, AF.Ln, [bs, bf("epsc")], [bs], scale=1.0 / D, bias=epsc[:])
                ACT(rs, rs, AF.Exp, [bs], [bs], scale=-0.5)
                STT(xt[:], xt[:], rs, ptab[:, _poff["gf"]:_poff["gf"] + D], ALU.mult, ALU.mult, [bx, bs, bf("ptab")], [bx])
                r0 = (T - h0) * 256 + tt * 128
                out_toks.append(S.dma("pool", f"out{i % 4}", yout[r0:r0 + 128, :], xt[:], reads=[bx]))

        out_toks = []
        nload = [0]

        def ensure_loaded(upto):
            while nload[0] < upto:
                load_x(nload[0])
                nload[0] += 1

        try:
            ck("init")
            for T in range(0, last_t):
                _CUR[0] = T
                ensure_loaded(2 * T + 2)
                if T < first_t:
                    tile(T, "hist")
                    ck("hist")
                elif T < h0:
                    tile(T, "halo")
                    ck("halo")
                else:
                    tile(T, "own")
                    ck("own")
        except _Stop:
            pass
        for e_ in ("pe", "act", "dve"):
            pass
        for tok in out_toks[-4:]:
            S.wait_tok("pool", tok)
        S.emit()
    return nc


_NC_CACHE = {}


def kernel(**inp):
    inp = {k: np.asarray(v) for k, v in inp.items()}
    x = inp["x"].astype(np.float32)
    if "nc" not in _NC_CACHE:
        _NC_CACHE["nc"] = build_nc()
    nc = _NC_CACHE["nc"]
    ptab = _ptab(inp)
    ctabs = [_ctab(0), _ctab(1)]
    in_maps = []
    for c in range(8):
        b, hf = c // 2, c % 2
        if hf == 1:
            xl = np.ascontiguousarray(x[b])
        else:
            xl = np.concatenate([np.zeros((4096, D), np.float32), x[b, :4096]], axis=0)
        in_maps.append({"x": xl, "ctab": ctabs[hf], "ptab": ptab,
                        "w_in": np.ascontiguousarray(inp["w_in"][0]), "w_out": np.ascontiguousarray(inp["w_out"][0]),
                        "w_up": np.ascontiguousarray(inp["w_up"][0]), "w_down": np.ascontiguousarray(inp["w_down"][0])})
    res = run_bass_kernel_spmd(nc, in_maps, core_ids=list(range(8)))
    out = np.zeros((4, 8192, D), np.float32)
    for c in range(8):
        b, hf = c // 2, c % 2
        out[b, hf * 4096:(hf + 1) * 4096] = res.results[c]["y"]
    return out
```

```python
import os
import numpy as np
from contextlib import ExitStack
import concourse.bass as bass
import concourse.mybir as mybir
from concourse.bass_utils import run_bass_kernel_spmd

F32 = mybir.dt.float32
BF16 = mybir.dt.bfloat16
AF = mybir.ActivationFunctionType
ALU = mybir.AluOpType

D = 1024
NH = 4
FF = 2816
NFC = 22
SLOPES = [2.0 ** -2, 2.0 ** -4, 2.0 ** -6, 2.0 ** -8]
GAM = [1.0 - 2.0 ** (-5 - h) for h in range(4)]
THR = 120.0
NEG = -30000.0
EPS = 1e-6
LAM_INIT = 0.2
RING = [8, 18, 64, 64]
NW = 8
FIRST_T = 15
LAST_T = 32

_off = {}
_cur = 0
for _n, _w in (("ident", 128), ("ones", 128), ("avg", 128), ("sel0", 128), ("sel1", 128), ("B", 512),
               ("Dt", 896), ("DM", 512), ("Ddec", 256), ("Dq", 256), ("g128", 2), ("e0", 128), ("e32", 128)):
    _off[_n] = _cur
    _cur += _w
NCT = _cur
_poff = {}
_cur = 0
for _n, _w in (("g1", 8), ("g2", 8), ("gf", 1024), ("subg", 1), ("gng", 4), ("cw", 66), ("cb", 22), ("lamv", 256)):
    _poff[_n] = _cur
    _cur += _w
NPT = _cur
DIDX = {(0, 0): 0, (1, 0): 1, (1, 1): 2, (2, 0): 3, (2, 1): 4, (3, 0): 5, (3, 1): 6}


def _ctab(hf):
    p = np.arange(128, dtype=np.float64)
    t = np.zeros((128, NCT), np.float64)
    t[:, _off["ident"]:_off["ident"] + 128] = np.eye(128)
    t[:, _off["ones"]:_off["ones"] + 128] = 1.0
    t[:, _off["avg"]:_off["avg"] + 128] = 1.0 / 128
    t[0, _off["e0"]:_off["e0"] + 128] = 1.0
    t[32, _off["e32"]:_off["e32"] + 128] = 1.0
    t[:, _off["sel0"] + 0] = 1.0
    t[:, _off["sel1"] + 32] = 1.0
    for h in range(4):
        m = SLOPES[h]
        for st in range(2):
            for dj in range(64):
                v = m * (p - 128.0 * dj)
                if st == 1 and hf == 0:
                    v = np.full(128, NEG)
                t[:, _off["B"] + (h * 2 + st) * 64 + dj] = v
    k = p[:, None]
    q = p[None, :]
    allowed = (k // 64) <= (q // 64)
    for (h, j), di in DIDX.items():
        m = SLOPES[h]
        v = np.where(allowed, -m * np.abs(q - k) + m * (128.0 * j + q), NEG)
        t[:, _off["Dt"] + di * 128:_off["Dt"] + di * 128 + 128] = v
    for h in range(4):
        v = np.where(allowed, GAM[h] ** np.abs(q - k) / 8.0, 0.0)
        t[:, _off["DM"] + h * 128:_off["DM"] + h * 128 + 128] = v
        t[:, _off["Ddec"] + h * 64:_off["Ddec"] + h * 64 + 64] = (GAM[h] ** (127.0 - p))[:, None]
    for pair in range(2):
        for half in range(2):
            h = 2 * pair + half
            rows = slice(64 * half, 64 * half + 64)
            t[rows, _off["Dq"] + pair * 128:_off["Dq"] + pair * 128 + 128] = (GAM[h] ** (p + 1.0) / 8.0)[None, :]
            t[rows, _off["g128"] + pair] = GAM[h] ** 128.0
    return t.astype(np.float32)


def _ptab(inp):
    t = np.zeros((128, NPT), np.float32)
    t[:, _poff["g1"]:_poff["g1"] + 8] = inp["norm_mix_g"][0].reshape(8, 128).T
    t[:, _poff["g2"]:_poff["g2"] + 8] = inp["norm_ffn_g"][0].reshape(8, 128).T
    t[:, _poff["gf"]:_poff["gf"] + 1024] = np.broadcast_to(inp["final_norm_g"][None, :], (128, 1024))
    t[:, _poff["subg"]] = inp["diff_subln_g"][0]
    t[:, _poff["gng"]:_poff["gng"] + 4] = inp["ret_gn_g"][0].reshape(4, 128).T
    t[:, _poff["cw"]:_poff["cw"] + 66] = inp["conv_w"][0].reshape(3, 22, 128).transpose(2, 0, 1).reshape(128, 66)
    t[:, _poff["cb"]:_poff["cb"] + 22] = inp["conv_b"][0].reshape(22, 128).T
    lv = np.concatenate([inp["lambda_q1"][0], inp["lambda_k1"][0], inp["lambda_q2"][0], inp["lambda_k2"][0]])
    t[:, _poff["lamv"]:_poff["lamv"] + 256] = np.broadcast_to(lv[None, :], (128, 256))
    return t


class Tok:
    __slots__ = ("sem", "val", "eng")

    def __init__(self, sem, val, eng):
        self.sem, self.val, self.eng = sem, val, eng


class Buf:
    def __init__(self, name=""):
        self.name = name
        self.w = None
        self.rs = []
        self.excl = name in ("S0", "S1", "O0", "O1", "Zb", "TR", "G0", "G1", "SA", "SB")
        self.group = []
        self.alias = []


class Sched:
    ENGS = ("pe", "act", "dve", "pool", "sp")

    def __init__(self, nc, es):
        self.nc = nc
        self.es = es
        self.prog = {e: [] for e in self.ENGS}
        self.sem = {e: es.enter_context(nc.semaphore("ps_" + e)) for e in self.ENGS}
        self.count = {e: 0 for e in self.ENGS}
        self.waited = {e: {} for e in self.ENGS}
        self.dsems = {}
        self.dcount = {}

    def _waits(self, eng, reads, writes):
        need = []
        for b in reads:
            if b.w is not None:
                need.append((b.w, True))
            for ob in b.alias:
                if ob.w is not None:
                    need.append((ob.w, True))
            if b.excl:
                for t in b.rs:
                    need.append((t, False))
                for ob in b.group + b.alias:
                    for t in ob.rs:
                        need.append((t, False))
        for b in writes:
            if b.w is not None:
                need.append((b.w, False))
            for t in b.rs:
                need.append((t, False))
            for ob in b.alias:
                if ob.w is not None:
                    need.append((ob.w, False))
                for t in ob.rs:
                    need.append((t, False))
        best = {}
        for t, raw in need:
            if t.eng == eng and (eng == "pe" or not raw):
                continue
            k = id(t.sem)
            if self.waited[eng].get(k, 0) >= t.val:
                continue
            if k not in best or best[k].val < t.val:
                best[k] = t
        out = []
        for k, t in best.items():
            self.waited[eng][k] = t.val
            out.append(t)
            if os.environ.get("KLOG"):
                print("LOG", eng, "wait", t.eng, t.val)
        return out

    def _emit_waits(self, eng, toks):
        for t in toks:
            self.prog[eng].append(lambda e, sem=t.sem, val=t.val: e.wait_ge(sem, val))

    def _commit(self, tok, reads, writes):
        for b in reads:
            b.rs = [t for t in b.rs if t.sem is not tok.sem]
            b.rs.append(tok)
        for b in writes:
            b.w = tok
            b.rs = []

    def op(self, eng, fn, reads=(), writes=(), inc=True):
        toks = self._waits(eng, reads, writes)
        emb = None
        if toks and not os.environ.get("KNOEMB"):
            emb = toks[-1]
            toks = toks[:-1]
        self._emit_waits(eng, toks)
        if emb is not None:
            fn0, esem, eval_ = fn, emb.sem, emb.val
            fn = lambda e, fn0=fn0, esem=esem, eval_=eval_: fn0(e)._wait_ge(esem, eval_)
        sem = self.sem[eng]
        tok = Tok(sem, self.count[eng] + 1, eng)
        if os.environ.get("KLOG"):
            print("LOG", eng, "op", inc, self.count[eng] + 1, [b.name for b in reads], [b.name for b in writes])
        if inc:
            self.count[eng] += 1
            self.prog[eng].append(lambda e, fn=fn, sem=sem: fn(e).then_inc(sem, 1))
        else:
            self.prog[eng].append(lambda e, fn=fn: fn(e))
        self._commit(tok, reads, writes)
        return tok

    def dma(self, eng, dname, out, in_, reads=(), writes=()):
        self._emit_waits(eng, self._waits(eng, reads, writes))
        if dname not in self.dsems:
            self.dsems[dname] = self.es.enter_context(self.nc.semaphore("d_" + dname))
            self.dcount[dname] = 0
        sem = self.dsems[dname]
        self.dcount[dname] += 16
        tok = Tok(sem, self.dcount[dname], "dma")
        self.prog[eng].append(lambda e, out=out, in_=in_, sem=sem: e.dma_start(out=out, in_=in_).then_inc(sem, 16))
        self._commit(tok, reads, writes)
        return tok

    def wait_tok(self, eng, tok):
        k = id(tok.sem)
        if self.waited[eng].get(k, 0) >= tok.val:
            return
        self.waited[eng][k] = tok.val
        self.prog[eng].append(lambda e, sem=tok.sem, val=tok.val: e.wait_ge(sem, val))

    def emit(self):
        with self.nc.Block() as block:
            @block.sync
            def _(e):
                for f in self.prog["sp"]:
                    f(e)

            @block.tensor
            def _(e):
                for f in self.prog["pe"]:
                    f(e)

            @block.scalar
            def _(e):
                for f in self.prog["act"]:
                    f(e)

            @block.vector
            def _(e):
                for f in self.prog["dve"]:
                    f(e)

            @block.gpsimd
            def _(e):
                for f in self.prog["pool"]:
                    f(e)


import os


class _Stop(Exception):
    pass


_CUR = [0]


def ck(name):
    ks = os.environ.get("KSTOP")
    if ks == name or ks == f"{name}@{_CUR[0]}":
        raise _Stop()


def build_nc(h0=16, nown=16):
    first_t, last_t = h0 - 1, h0 + nown
    nc = bass.Bass("TRN2", target_bir_lowering=False)
    xin = nc.dram_tensor("x", [256 * (h0 + nown), D], F32, kind="ExternalInput").ap()
    ctab_d = nc.dram_tensor("ctab", [128, NCT], F32, kind="ExternalInput").ap()
    ptab_d = nc.dram_tensor("ptab", [128, NPT], F32, kind="ExternalInput").ap()
    w_in_d = nc.dram_tensor("w_in", [D, 3072], F32, kind="ExternalInput").ap()
    w_out_d = nc.dram_tensor("w_out", [D, D], F32, kind="ExternalInput").ap()
    w_up_d = nc.dram_tensor("w_up", [D, 2 * FF], F32, kind="ExternalInput").ap()
    w_dn_d = nc.dram_tensor("w_down", [FF, D], F32, kind="ExternalInput").ap()
    yout = nc.dram_tensor("y", [256 * nown, D], F32, kind="ExternalOutput").ap()
    win_s = nc.dram_tensor("win_s", [12, 128, 8, 256], BF16, kind="Internal").ap()
    wout_s = nc.dram_tensor("wout_s", [4, 128, 8, 256], BF16, kind="Internal").ap()
    wup_s = nc.dram_tensor("wup_s", [22, 128, 8, 256], BF16, kind="Internal").ap()
    wdn_s = nc.dram_tensor("wdn_s", [4, 3, 128, 8, 256], BF16, kind="Internal").ap()

    es = ExitStack()
    with es:
        S = Sched(nc, es)
        A = nc.alloc_sbuf_tensor

        def PS(name, shape, dt=F32):
            return es.enter_context(nc.psum_tensor(name, shape, dt))

        ctab = A("ctab_t", [128, NCT], F32)
        ptab = A("ptab_t", [128, NPT], F32)
        idb = A("idb", [128, 128], BF16)
        selb = A("selb", [128, 256], BF16)
        epsc = A("epsc", [128, 1], F32)
        lamc = A("lamc", [128, 4], F32)
        lamt = A("lamt", [128, 128], F32)
        gcol = A("gcol", [128, 1], F32)
        KT = [A(f"KT{h}", [128, RING[h] * 128], BF16) for h in range(4)]
        VV = [A(f"VV{h}", [128, RING[h], 128], BF16) for h in range(4)]
        wring = [A(f"wr{i}", [128, 8, 256], BF16) for i in range(NW)]
        xb = [A(f"xb{i}", [128, D], F32) for i in range(4)]
        xn = [A(f"xn{i}", [128, D], BF16) for i in range(2)]
        ssq = A("ssq", [128, 8], F32)
        hT = A("hT", [128, 8, 256], BF16)
        QTc = A("QTc", [128, 4, 2, 256], BF16)
        rqT = A("rqT", [128, 2, 256], BF16)
        rqdT = A("rqdT", [128, 2, 256], BF16)
        rkT = A("rkT", [128, 2, 256], BF16)
        kd = [A(f"kd{i}", [128, 256], BF16) for i in range(2)]
        rvt = [A(f"rvt{i}", [128, 512], BF16) for i in range(2)]
        St = A("St", [128, 2, 128], F32)
        Stb = A("Stb", [128, 2, 128], BF16)
        sg = A("sg", [128, 4, 256], F32)
        oT = A("oT", [128, 8, 256], BF16)
        PT = [A(f"PT{i}", [128, 2, 256], BF16) for i in range(3)]
        dtmp = [A("dtmp0", [128, 2, 128], F32)] * 2
        zaccs = [A("zacc0", [128, 512], F32), A("zacc1", [128, 512], F32)]
        epsz = A("epsz", [128, 1], F32)
        t1 = A("t1", [128, 512], F32)
        t1s = [t1, A("t1b", [128, 512], F32)]
        t2 = A("t2", [128, 512], F32)
        t3 = A("t3", [128, 512], F32)
        t4 = A("t4", [128, 512], F32)
        AT = [A(f"AT{i}", [128, 128], BF16) for i in range(4)]
        gT = A("gT", [128, NFC, 256], BF16)
        aext = [A(f"aext{i}", [128, 258], F32) for i in range(2)]
        cc = [A(f"cc{i}", [128, 256], F32) for i in range(3)]
        carry = A("carry", [128, NFC, 2], F32)

        S2 = PS("S2", [128, 2, 512])
        O2 = PS("O2", [128, 2, 512])
        Zb = PS("Zb", [128, 512])
        TRf = PS("TR", [128, 512], F32)
        TR = TRf[:].bitcast(BF16)
        G = [PS("G0", [128, 512]), PS("G1", [128, 512])]

        B = {}

        def bf(name):
            if name not in B:
                B[name] = Buf(name)
            return B[name]

        in_att = [False]
        gctr = [0]

        def galloc():
            if in_att[0]:
                i = gctr[0] % 2
                gctr[0] += 1
                return G[i], bf(f"G{i}")
            i = gctr[0] % 4
            gctr[0] += 1
            if i < 2:
                return G[i], bf(f"G{i}")
            return S2[:, i - 2, :], bf(f"S{i - 2}")

        def ct(name, lo=0, n=None):
            o = _off[name] + lo
            return ctab[:, o:o + (n if n is not None else 1)]

        def pt(name, lo=0, n=None):
            o = _poff[name] + lo
            return ptab[:, o:o + (n if n is not None else 1)]

        tc = S.dma("sp", "c0", ctab[:], ctab_d, writes=[bf("ctab")])
        tp = S.dma("sp", "c1", ptab[:], ptab_d, writes=[bf("ptab")])
        for wsrc, wdst, nm in ((w_in_d, win_s, "win"), (w_out_d, wout_s, "wout"), (w_up_d, wup_s, "wup")):
            tok = None
            for kc in range(8):
                tok = S.dma("pool", "cast_" + nm, wdst[:, :, kc, :], wsrc[128 * kc:128 * kc + 128, :].rearrange("p (u n) -> u p n", n=256))
            bf(nm).w = tok
        tok = None
        for rb in range(NFC):
            tok = S.dma("pool", "cast_wdn", wdn_s[:, rb // 8, :, rb % 8, :], w_dn_d[128 * rb:128 * rb + 128, :].rearrange("p (u n) -> u p n", n=256))
        bf("wdn").w = tok
        S.op("dve", lambda e: e.tensor_copy(out=idb[:], in_=ct("ident", 0, 128)), reads=[bf("ctab")], writes=[bf("idb")])
        S.op("dve", lambda e: e.tensor_copy(out=selb[:], in_=ct("sel0", 0, 256)), reads=[bf("ctab")], writes=[bf("selb")])
        S.op("dve", lambda e: e.memset(epsc[:], EPS), writes=[bf("epsc")])
        S.op("dve", lambda e: e.memset(St[:], 0.0), writes=[bf("St")])
        S.op("dve", lambda e: e.memset(Stb[:], 0.0), writes=[bf("Stb")])
        S.op("dve", lambda e: e.memset(carry[:], 0.0), writes=[bf("carry")])
        S.op("dve", lambda e: e.memset(oT[:], 0.0), writes=[bf(f"oT{k_}") for k_ in range(8)])
        S.op("dve", lambda e: e.memset(QTc[:], 0.0), writes=[bf("QT")])
        S.op("dve", lambda e: e.memset(epsz[:], 1e-18), writes=[bf("epsz")])
        S.op("dve", lambda e: e.tensor_tensor(out=lamt[:, 0:64], in0=pt("lamv", 0, 64), in1=pt("lamv", 64, 64), op=ALU.mult),
             reads=[bf("ptab")], writes=[bf("lamt")])
        S.op("dve", lambda e: e.tensor_tensor(out=lamt[:, 64:128], in0=pt("lamv", 128, 64), in1=pt("lamv", 192, 64), op=ALU.mult),
             reads=[bf("ptab")], writes=[bf("lamt")])
        S.op("dve", lambda e: e.reduce_sum(out=lamc[:, 2:3], in_=lamt[:, 0:64], axis=mybir.AxisListType.X), reads=[bf("lamt")], writes=[bf("lamc")])
        S.op("dve", lambda e: e.reduce_sum(out=lamc[:, 3:4], in_=lamt[:, 64:128], axis=mybir.AxisListType.X), reads=[bf("lamt")], writes=[bf("lamc")])
        S.op("act", lambda e: e.activation(out=lamc[:, 2:4], in_=lamc[:, 2:4], func=AF.Exp), reads=[bf("lamc")], writes=[bf("lamc")])
        S.op("dve", lambda e: e.tensor_tensor(out=lamc[:, 0:1], in0=lamc[:, 2:3], in1=lamc[:, 3:4], op=ALU.subtract), reads=[bf("lamc")], writes=[bf("lamc")])
        S.op("dve", lambda e: e.tensor_scalar(out=lamc[:, 1:2], in0=lamc[:, 0:1], scalar1=-1.0, scalar2=-LAM_INIT, op0=ALU.mult, op1=ALU.add),
             reads=[bf("lamc")], writes=[bf("lamc2")])
        S.op("dve", lambda e: e.tensor_scalar(out=gcol[:], in0=pt("subg"), scalar1=1.0 - LAM_INIT, scalar2=None, op0=ALU.mult),
             reads=[bf("ptab")], writes=[bf("gcol")])

        wctr = [0]

        def load_unit(src, nk, srcbuf):
            i = wctr[0] % NW
            wctr[0] += 1
            S.dma("sp", f"w{i}", wring[i][:, 0:nk, :], src, reads=[bf(srcbuf)], writes=[bf(f"wr{i}")])
            return wring[i], bf(f"wr{i}")

        ucache = {}
        hist_mode = [False]

        def unit_in(c0):
            if hist_mode[0] and c0 in ucache:
                return ucache[c0]
            r = load_unit(win_s[c0 // 256], 8, "win")
            if hist_mode[0]:
                ucache[c0] = r
            return r

        def load_x(i):
            S.dma("sp", f"x{i % 4}", xb[i % 4][:], xin[i * 128:(i + 1) * 128, :], writes=[bf(f"xb{i % 4}")])

        def mm(out, lhsT, rhs, start, stop, reads, writes, inc, **kw):
            S.op("pe", lambda e: e.matmul(out=out, lhsT=lhsT, rhs=rhs, start=start, stop=stop, **kw), reads=reads, writes=writes, inc=inc)


        def ACT(out, in_, func, reads, writes, **kw):
            return S.op("act", lambda e: e.activation(out=out, in_=in_, func=func, **kw), reads=reads, writes=writes)

        def TT(out, in0, in1, op, reads, writes, eng="dve"):
            S.op(eng, lambda e: e.tensor_tensor(out=out, in0=in0, in1=in1, op=op), reads=reads, writes=writes)

        def TS(out, in0, s1, s2, op0, op1, reads, writes):
            if op1 is None:
                S.op("dve", lambda e: e.tensor_scalar(out=out, in0=in0, scalar1=s1, scalar2=None, op0=op0), reads=reads, writes=writes)
            else:
                S.op("dve", lambda e: e.tensor_scalar(out=out, in0=in0, scalar1=s1, scalar2=s2, op0=op0, op1=op1), reads=reads, writes=writes)

        def STT(out, in0, scalar, in1, op0, op1, reads, writes):
            S.op("dve", lambda e: e.scalar_tensor_tensor(out=out, in0=in0, scalar=scalar, in1=in1, op0=op0, op1=op1), reads=reads, writes=writes)

        def CP(out, in_, reads, writes, eng="dve"):
            if os.environ.get("KCP") == "copy":
                S.op(eng, lambda e: e.tensor_copy(out=out, in_=in_), reads=reads, writes=writes)
            else:
                S.op(eng, lambda e: e.tensor_scalar(out=out, in0=in_, scalar1=1.0, scalar2=None, op0=ALU.mult), reads=reads, writes=writes)

        def TRN(out, in_, reads, writes, inc):
            S.op("pe", lambda e: e.transpose(out=out, in_=in_, identity=idb[:]), reads=reads, writes=writes, inc=inc)

        def norm_T(i, tt, gname):
            xt, bx = xb[i % 4], bf(f"xb{i % 4}")
            xnn, bxn = xn[i % 2], bf(f"xn{i % 2}")
            sc = ssq[:, (i % 2) * 4:(i % 2) * 4 + 1]
            rs = ssq[:, (i % 2) * 4 + 1:(i % 2) * 4 + 2]
            bs = bf(f"ssq{i % 2}")
            ACT(xnn[:], xt[:], AF.Square, [bx], [bxn, bs], accum_out=sc)
            ACT(rs, sc, AF.Ln, [bs, bf("epsc")], [bs], scale=1.0 / D, bias=epsc[:])
            ACT(rs, rs, AF.Exp, [bs], [bs], scale=-0.5)
            ACT(xnn[:], xt[:], AF.Copy, [bx, bs], [bxn], scale=rs)
            for c in range(8):
                TRN(TR[:, c * 128:(c + 1) * 128], xnn[:, c * 128:(c + 1) * 128], [bxn, bf("idb")], [bf("TR")], inc=(c == 7))
            for c in range(8):
                TS(hT[:, c, tt * 128:(tt + 1) * 128], TR[:, c * 128:(c + 1) * 128], pt(gname, c, 1), None, ALU.mult, None,
                   [bf("TR"), bf("ptab")], [bf("hT")])

        def v2(ap):
            return ap.rearrange("p (a b) -> p a b", a=2)

        def proj_fm(wt, bw, cols, lo=0):
            bank, bb = galloc()
            n = len(cols)
            for ci, c0 in enumerate(cols):
                for kc in range(8):
                    mm(bank[:, ci * 256 + lo:ci * 256 + 256], wt[:, kc, c0:c0 + 128], hT[:, kc, lo:256], kc == 0, kc == 7,
                       [bf("hT"), bw], [bb], inc=(kc == 7 and ci == n - 1))
            return bank, bb

        def proj_tm(wt, bw, tt):
            bank, bb = galloc()
            for kc in range(8):
                mm(bank[:, 0:256], hT[:, kc, tt * 128:(tt + 1) * 128], wt[:, kc, :], kc == 0, kc == 7,
                   [bf("hT"), bw], [bb], inc=(kc == 7))
            return bank, bb

        def kt_lo(h, T):
            lo = 0
            Q0 = 256 * T
            for kt in range(2 * T + 2):
                if (Q0 - 128 * kt - 127) * SLOPES[h] >= THR:
                    lo = kt + 1
            return lo

        def need_kv(h, T):
            for kt in (2 * T, 2 * T + 1):
                for Tq in range(max(T, first_t), last_t):
                    if kt >= kt_lo(h, Tq):
                        return True
            return False

        def segs(h, T, kt, qlo):
            out = []
            if h == 0:
                for qb in (0, 1):
                    lo, hi = max(qlo, 128 * qb), 128 * qb + 128
                    if lo >= hi:
                        continue
                    ktd = 2 * T + qb
                    if kt < ktd:
                        out.append((lo, hi, "past", ktd - kt, 0))
                    elif kt == ktd:
                        out.append((lo, hi, "diag", 0, 128 * qb))
            else:
                if kt < 2 * T:
                    out.append((qlo, 256, "past", 2 * T - kt, 0))
                elif kt == 2 * T:
                    if qlo < 128:
                        out.append((qlo, 128, "diag", 0, 0))
                    out.append((max(qlo, 128), 256, "past", 0, 0))
                else:
                    out.append((max(qlo, 128), 256, "diag", 1, 128))
            return out

        actr = [0]
        att_stages = []
        ones_c = _off["ones"]
        avg_ap = ctab[:, _off["avg"]:_off["avg"] + 128]

        def sbank(pb):
            return S2[:, pb, :] if pb < 2 else TRf[:, :]

        def sbuf_(pb):
            return bf(f"S{pb}") if pb < 2 else bf("TR")

        def attention(T, h, qlo):
            pbO = h % 2
            bO = bf(f"O{pbO}")
            bZ = bf("Zb")
            zoff = 256 * (h % 2)
            kts = [kt for kt in range(kt_lo(h, T), 2 * T + 2) if segs(h, T, kt, qlo)]
            info = []
            for kt in kts:
                sg_ = segs(h, T, kt, qlo)
                clo = min(s[0] for s in sg_)
                pb = actr[0] % 3
                pi = actr[0] % 3
                actr[0] += 1
                info.append((kt, sg_, clo, pb, pi))

            def qk(ki):
                kt, sg_, clo, pb, pi = info[ki]
                slot = kt % RING[h]
                if clo == 0:
                    mm(sbank(pb), KT[h][:, slot * 128:(slot + 1) * 128], QTc[:, h, :, :].rearrange("p a b -> p (a b)"),
                       True, True, [bf(f"KT{h}_{slot}"), bf("QT")], [sbuf_(pb)], inc=True)
                else:
                    for half in range(2):
                        mm(sbank(pb)[:, 256 * half + clo:256 * half + 256], KT[h][:, slot * 128:(slot + 1) * 128], QTc[:, h, half, clo:256],
                           True, True, [bf(f"KT{h}_{slot}"), bf("QT")], [sbuf_(pb)], inc=(half == 1))

            exp_toks = []
            add_toks = []
            pool_toks = []
            qk(0)
            if len(info) > 1:
                qk(1)
            for ki, (kt, sg_, clo, pb, pi) in enumerate(info):
                bS = sbuf_(pb)
                slot = kt % RING[h]
                bV = bf(f"VV{h}_{slot}")
                bP = bf(f"PT{pi}")
                sset = 1 if kt < 2 * h0 else 0
                if ki >= 2:
                    S.wait_tok("act", add_toks[ki - 2])
                for (lo, hi, typ, prm, bs0) in sg_:
                    if typ == "past":
                        col = _off["B"] + (h * 2 + sset) * 64 + prm
                        tk = ACT(PT[pi][:, :, lo:hi], v2(sbank(pb))[:, :, lo:hi], AF.Exp, [bS, bf("ctab")], [bP], bias=ctab[:, col:col + 1])
                    else:
                        di = DIDX[(h, prm)]
                        dsel = (ki + h) % 2
                        dt_, bd = dtmp[dsel], bf("dtmp0")
                        dcol = _off["Dt"] + di * 128
                        for half in range(2):
                            TT(dt_[:, half, lo - bs0:hi - bs0], sbank(pb)[:, 256 * half + lo:256 * half + hi],
                               ctab[:, dcol + lo - bs0:dcol + hi - bs0], ALU.add, [bS, bf("ctab")], [bd])
                        tk = ACT(PT[pi][:, :, lo:hi], dt_[:, :, lo - bs0:hi - bs0], AF.Exp, [bd], [bP])
                exp_toks.append(tk)
                if ki + 2 < len(info):
                    qk(ki + 2)
                if ki % 2 == 1 and att_stages:
                    att_stages.pop(0)()
                first, last = (ki == 0), (ki == len(info) - 1)
                if clo == 0:
                    mm(O2[:, pbO, :], VV[h][:, slot, :], PT[pi][:].rearrange("p a b -> p (a b)"), first, last,
                       [bV, bP], [bO], inc=last, skip_group_check=True)
                else:
                    for half in range(2):
                        mm(O2[:, pbO, 256 * half + clo:256 * half + 256], VV[h][:, slot, :], PT[pi][:, half, clo:256], first and half == 0, last,
                           [bV, bP], [bO], inc=(last and half == 1), skip_group_check=True)
                zv = v2(Zb[:])
                zaccv = v2(zaccs[h % 2][:])
                bzB = bf(f"zaccB{h % 2}")
                S.wait_tok("pool", exp_toks[ki])
                if first:
                    tokp = S.op("pool", lambda e, zaccv=zaccv, pi=pi, clo=clo: e.tensor_copy(out=zaccv[:, 1, clo:256], in_=PT[pi][:, 1, clo:256]),
                                reads=[], writes=[bzB])
                else:
                    tokp = S.op("pool", lambda e, zaccv=zaccv, pi=pi, clo=clo: e.tensor_tensor(out=zaccv[:, 1, clo:256], in0=zaccv[:, 1, clo:256],
                                                                                             in1=PT[pi][:, 1, clo:256], op=ALU.add), reads=[bzB], writes=[bzB])
                if pool_toks:
                    S.wait_tok("dve", pool_toks[-1])
                pool_toks.append(tokp)
                if first:
                    tokd = S.op("dve", lambda e, zv=zv, pi=pi, clo=clo: e.tensor_copy(out=zv[:, 0, clo:256], in_=PT[pi][:, 0, clo:256]), reads=[bP], writes=[bZ])
                else:
                    tokd = S.op("dve", lambda e, zv=zv, pi=pi, clo=clo: e.tensor_tensor(out=zv[:, 0, clo:256], in0=zv[:, 0, clo:256],
                                                                                       in1=PT[pi][:, 0, clo:256], op=ALU.add), reads=[bZ, bP], writes=[bZ])
                add_toks.append(tokd)
            S.wait_tok("act", pool_toks[-1])
            zacc, bza = zaccs[h % 2], bf(f"zacc{h % 2}")
            ACT(zacc[:, qlo:256], Zb[:, qlo:256], AF.Copy, [bZ], [bza])
            bzB_ = bf(f"zaccB{h % 2}")
            t1h, bt1 = t1s[h % 2], bf(f"t1_{h % 2}")
            t1v = v2(t1h[:])
            ACT(t1v[:, :, qlo:256], v2(O2[:, pbO, :])[:, :, qlo:256], AF.Copy, [bO], [bt1])
            hold = {}
            rinv = v2(t4[:])
            t2v = v2(t2[:])

            def s1():
                bank, bb = galloc()
                if qlo == 0:
                    mm(bank[:, :], ctab[:, ones_c:ones_c + 128], zacc[:, :], True, True, [bza, bzB_, bf("ctab")], [bb], inc=True)
                else:
                    for half in range(2):
                        mm(bank[:, 256 * half + qlo:256 * half + 256], ctab[:, ones_c:ones_c + 128], zacc[:, 256 * half + qlo:256 * half + 256],
                           True, True, [bza, bzB_, bf("ctab")], [bb], inc=(half == 1))
                ACT(rinv[:, :, qlo:256], v2(bank[:])[:, :, qlo:256], AF.Ln, [bb, bf("epsz")], [bf("t4"), bf("t4b")], bias=epsz[:])

            def s2():
                ACT(rinv[:, :, qlo:256], rinv[:, :, qlo:256], AF.Exp, [bf("t4")], [bf("t4"), bf("t4b")], scale=-1.0)
                TS(rinv[:, 1, qlo:256], rinv[:, 1, qlo:256], lamc[:, 1:2], None, ALU.mult, None, [bf("t4"), bf("lamc2")], [bf("t4"), bf("t4b")])

            def s3():
                TT(t2v[:, :, qlo:256], t1v[:, :, qlo:256], rinv[:, :, qlo:256], ALU.mult, [bt1, bf("t4")], [bf("t2")])
                TT(t3[:, qlo:256], t2v[:, 0, qlo:256], t2v[:, 1, qlo:256], ALU.add, [bf("t2")], [bf("t3")])
                ACT(t2[:, qlo:256], t3[:, qlo:256], AF.Square, [bf("t3")], [bf("t2")])

            def s4():
                bank2, bb2 = galloc()
                mm(bank2[:, qlo:256], avg_ap, t2[:, qlo:256], True, True, [bf("t2"), bf("ctab")], [bb2], inc=True)
                ACT(t2[:, 256 + qlo:512], bank2[:, qlo:256], AF.Ln, [bb2, bf("epsc")], [bf("t2")], bias=epsc[:])

            def s5():
                ACT(t2[:, 256 + qlo:512], t2[:, 256 + qlo:512], AF.Exp, [bf("t2")], [bf("t2")], scale=-0.5)
                STT(oT[:, h, qlo:256], t3[:, qlo:256], gcol[:], t2[:, 256 + qlo:512], ALU.mult, ALU.mult,
                    [bf("t3"), bf("t2"), bf("gcol")], [bf(f"oT{h}")])
            return [s1, s2, s3, s4, s5]

        prenorm = set()

        def tile(T, mode):
            own = mode != "hist"
            qlo = 254 if mode == "halo" else 0
            full = mode == "own"
            if T not in prenorm:
                for tt in range(2):
                    norm_T(2 * T + tt, tt, "g1")
            if not own:
                ensure_loaded(min(2 * T + 4, 2 * last_t))
            ck("norm")
            hs_kv = [h for h in range(4) if need_kv(h, T)]
            if own:
                wt, bw = unit_in(1536)
                bank, bb = proj_fm(wt, bw, [0, 128], lo=qlo)
                bv = v2(bank[:])
                ACT(rqT[:, :, qlo:256], bv[:, :, qlo:256], AF.Copy, [bb], [bf("rqT")])
                dqv = v2(ctab[:, _off["Dq"]:_off["Dq"] + 256])
                for tt in range(2):
                    lo = max(qlo, 128 * tt)
                    if lo >= 128 * tt + 128:
                        continue
                    TT(rqdT[:, :, lo:128 * tt + 128], bv[:, :, lo:128 * tt + 128], dqv[:, :, lo - 128 * tt:128], ALU.mult,
                       [bb, bf("ctab")], [bf("rqdT")])
            wt, bw = unit_in(1792)
            for tt in range(2):
                bank, bb = proj_tm(wt, bw, tt)
                TT(kd[tt][:], bank[:, 0:256], ctab[:, _off["Ddec"]:_off["Ddec"] + 256], ALU.mult, [bb, bf("ctab")], [bf(f"kd{tt}")])
            if own:
                bank, bb = proj_fm(wt, bw, [0, 128], lo=0)
                ACT(rkT[:], v2(bank[:]), AF.Copy, [bb], [bf("rkT")])
            ck("rk")
            for u in range(2):
                wt, bw = unit_in(2048 + 256 * u)
                for tt in range(2):
                    bank, bb = proj_tm(wt, bw, tt)
                    ACT(rvt[tt][:, 256 * u:256 * u + 256], bank[:, 0:256], AF.Copy, [bb], [bf(f"rvt{tt}_{u}")])
            if own:
                for u in range(2):
                    wt, bw = unit_in(2560 + 256 * u)
                    bank, bb = proj_fm(wt, bw, [0, 128], lo=qlo)
                    ACT(sg[:, 2 * u:2 * u + 2, qlo:256], v2(bank[:])[:, :, qlo:256], AF.Silu, [bb], [bf(f"sg{u}")])
            stages = []
            rbanks = [(O2[:, 0, :], bf("O0")), (O2[:, 1, :], bf("O1"))] if own else None
            for tt in range(2):
                if own and not (mode == "halo" and tt == 0):
                    lo = max(qlo, 128 * tt) - 128 * tt

                    def st_a(tt=tt, lo=lo):
                        for h in range(4):
                            rows = slice(64 * (h % 2), 64 * (h % 2) + 64)
                            pr = h // 2
                            so = 128 * pr
                            mm(S2[:, h % 2, so + lo:so + 128], rkT[rows, pr, 128 * tt:128 * tt + 128], rqT[rows, pr, 128 * tt + lo:128 * tt + 128],
                               True, True, [bf("rkT"), bf("rqT")], [bf(f"S{h % 2}")], inc=True)

                        st_b(tt, lo)

                    def st_b(tt, lo):
                        for h in range(4):
                            so = 128 * (h // 2)
                            dmc = _off["DM"] + 128 * h
                            TT(AT[h][:, lo:128], S2[:, h % 2, so + lo:so + 128], ctab[:, dmc + lo:dmc + 128], ALU.mult,
                               [bf(f"S{h % 2}"), bf("ctab")], [bf(f"AT{h}")])

                    def st_c(tt=tt, lo=lo):
                        for h in range(4):
                            rows = slice(64 * (h % 2), 64 * (h % 2) + 64)
                            pr = h // 2
                            rb = rbanks[pr][0]
                            rbb = [bf(f"O{pr}")]
                            oo = 256 * (h % 2) + 128 * tt
                            mm(rb[:, oo + lo:oo + 128], rvt[tt][:, 128 * h:128 * h + 128], AT[h][:, lo:128], True, False,
                               [bf(f"AT{h}"), bf(f"rvt{tt}_{h // 2}")], rbb, inc=False)
                            mm(rb[:, oo + lo:oo + 128], Stb[rows, pr, :], rqdT[rows, pr, 128 * tt + lo:128 * tt + 128], False, True,
                               [bf("Stb"), bf("rqdT")], rbb, inc=True)
                    stages += [st_a, st_c]

                def st_d(tt=tt):
                    bank, bb = galloc()
                    kvv = v2(bank[:, 0:256])
                    for h in range(4):
                        mm(kvv[64 * (h % 2):64 * (h % 2) + 64, h // 2, :], kd[tt][:, 64 * h:64 * h + 64], rvt[tt][:, 128 * h:128 * h + 128],
                           True, True, [bf(f"kd{tt}"), bf(f"rvt{tt}_{h // 2}")], [bb], inc=(h == 3))

                    def st_e():
                        for pr in range(2):
                            STT(St[:, pr, :], St[:, pr, :], ctab[:, _off["g128"] + pr:_off["g128"] + pr + 1], kvv[:, pr, :], ALU.mult, ALU.add,
                                [bb, bf("St"), bf("ctab")], [bf("St")])
                        ACT(Stb[:], St[:], AF.Copy, [bf("St")], [bf("Stb")])
                    st_e()
                stages.append(st_d)
            if own:
                v1_, v2_, v3_, v4_ = v2(t1[:]), v2(t2[:]), v2(t3[:]), v2(t4[:])
                for pr in range(2):
                    hold = {}

                    def pa(pr=pr, hold=hold):
                        rv_ = v2(rbanks[pr][0])
                        ACT(v1_[:, :, qlo:256], rv_[:, :, qlo:256], AF.Copy, [bf(f"O{pr}")], [bf("t1_0")])
                        bk, bkb = Zb, bf("Zb")
                        bkv = v2(bk[:])
                        for a_ in range(2):
                            mm(bkv[:, a_, qlo:256], avg_ap, v1_[:, a_, qlo:256], True, True, [bf("t1_0"), bf("ctab")], [bkb], inc=(a_ == 1))
                        hold["bkv"], hold["bkb"] = bkv, bkb

                    def pb_(pr=pr, hold=hold):
                        TT(v2_[:, :, qlo:256], v1_[:, :, qlo:256], hold["bkv"][:, :, qlo:256], ALU.subtract, [bf("t1_0"), hold["bkb"]], [bf("t2")])
                        ACT(v3_[:, :, qlo:256], v2_[:, :, qlo:256], AF.Square, [bf("t2")], [bf("t3")])
                        bk2, bkb2 = rbanks[pr][0], bf(f"O{pr}")
                        bkv2 = v2(bk2)
                        for a_ in range(2):
                            mm(bkv2[:, a_, qlo:256], avg_ap, v3_[:, a_, qlo:256], True, True, [bf("t3"), bf("ctab")], [bkb2], inc=(a_ == 1))
                        hold["bkv2"], hold["bkb2"] = bkv2, bkb2

                    def pc(pr=pr, hold=hold):
                        ACT(v4_[:, :, qlo:256], hold["bkv2"][:, :, qlo:256], AF.Ln, [hold["bkb2"], bf("epsc")], [bf("t4"), bf("t4b")], bias=epsc[:])
                        ACT(v4_[:, :, qlo:256], v4_[:, :, qlo:256], AF.Exp, [bf("t4")], [bf("t4"), bf("t4b")], scale=-0.5)
                        for a_ in range(2):
                            h = 2 * pr + a_
                            STT(v1_[:, a_, qlo:256], v2_[:, a_, qlo:256], ptab[:, _poff["gng"] + h:_poff["gng"] + h + 1], v4_[:, a_, qlo:256],
                                ALU.mult, ALU.mult, [bf("t2"), bf("t4"), bf("ptab")], [bf("t1_0")])
                        TT(oT[:, 4 + 2 * pr:6 + 2 * pr, qlo:256], v1_[:, :, qlo:256], sg[:, 2 * pr:2 * pr + 2, qlo:256], ALU.mult,
                           [bf("t1_0"), bf(f"sg{pr}")], [bf(f"oT{4 + 2 * pr}"), bf(f"oT{5 + 2 * pr}")])
                    stages += [pa, pb_, pc]

            def filler(n=1):
                for _ in range(n):
                    if stages:
                        stages.pop(0)()

            if own:
                for u in range(2):
                    wt, bw = unit_in(256 * u)
                    filler(2)
                    bank, bb = proj_fm(wt, bw, [0, 128], lo=qlo)
                    for half in range(2):
                        rws = slice(64 * half, 64 * half + 64)
                        ACT(QTc[rws, 2 * u:2 * u + 2, half, qlo:256], v2(bank[:])[rws, :, qlo:256], AF.Copy, [bb], [bf("QT")], scale=0.125)
            for u in range(2):
                hh = [h for h in (2 * u, 2 * u + 1) if h in hs_kv]
                filler(2)
                if not hh:
                    continue
                wt, bw = unit_in(512 + 256 * u)
                bank, bb = proj_fm(wt, bw, [128 * (h - 2 * u) for h in hh])
                for ci, h in enumerate(hh):
                    s0 = (2 * T) % RING[h]
                    wr = [bf(f"KT{h}_{s0}"), bf(f"KT{h}_{s0 + 1}")]
                    if ci == 0:
                        ACT(KT[h][:, s0 * 128:s0 * 128 + 256], bank[:, 0:256], AF.Copy, [bb], wr)
                    else:
                        CP(KT[h][:, s0 * 128:s0 * 128 + 256], bank[:, 256:512], [bb], wr)
            ck("dk")
            for u in range(2):
                hh = [h for h in (2 * u, 2 * u + 1) if h in hs_kv]
                if not hh:
                    filler(2)
                    continue
                wt, bw = unit_in(1024 + 256 * u)
                for tt in range(2):
                    filler(2)
                    bank, bb = proj_tm(wt, bw, tt)
                    for h in hh:
                        slot = (2 * T + tt) % RING[h]
                        c = 128 * (h - 2 * u)
                        CP(VV[h][:, slot, :], bank[:, c:c + 128], [bb], [bf(f"VV{h}_{slot}")])
            filler(100)
            ck("ret")
            if not own:
                return
            ck("retpost")
            in_att[0] = True
            for h in range(4):
                new_st = attention(T, h, qlo)
                while att_stages:
                    att_stages.pop(0)()
                att_stages.extend(new_st)
            ck("attn")
            ensure_loaded(min(2 * T + 4, 2 * last_t))
            tts = [1] if mode == "halo" else [0, 1]
            ounits = [load_unit(wout_s[cb], 8, "wout") for cb in range(4)]
            obanks = [(O2[:, 0, :], bf("O0")), (O2[:, 1, :], bf("O1")), (S2[:, 0, :], bf("S0")), (S2[:, 1, :], bf("S1"))]
            korder = [4, 5, 6, 7, 0, 1, 2]
            oldo = bool(os.environ.get("KOLDO"))
            if oldo:
                korder = [4, 5, 6, 7, 0, 1, 2, 3]
            for cb in range(4):
                wt, bw = ounits[cb]
                bank, bb = obanks[cb]
                firstmm = True
                for _ in range(2):
                    if att_stages:
                        att_stages.pop(0)()
                for tt in tts:
                    for kc in korder:
                        mm(bank[:, 256 * tt:256 * tt + 256], oT[:, kc, 128 * tt:128 * tt + 128], wt[:, kc, :], firstmm, oldo and kc == 3,
                           [bf(f"oT{kc}"), bw], [bb], inc=(oldo and kc == 3), skip_group_check=True)
                        firstmm = False
            while att_stages:
                att_stages.pop(0)()
            in_att[0] = False
            for cb in range(4):
                wt, bw = ounits[cb]
                bank, bb = obanks[cb]
                for tt in tts:
                    if oldo:
                        continue
                    mm(bank[:, 256 * tt:256 * tt + 256], oT[:, 3, 128 * tt:128 * tt + 128], wt[:, 3, :], False, True,
                       [bf("oT3"), bw], [bb], inc=(tt == tts[-1]), skip_group_check=True)
                for tt in tts:
                    i = 2 * T + tt
                    xs = xb[i % 4][:, 256 * cb:256 * cb + 256]
                    TT(xs, xs, bank[:, 256 * tt:256 * tt + 256], ALU.add, [bb], [bf(f"xb{i % 4}")])
            ck("oproj")
            for tt in tts:
                norm_T(2 * T + tt, tt, "g2")
            chain = []
            for j in range(11):
                wa, bwa = load_unit(wup_s[j], 8, "wup")
                if full:
                    wb, bwb = load_unit(wup_s[11 + j], 8, "wup")
                for sub in range(2):
                    fc = 2 * j + sub
                    bank, bb = galloc()
                    for kc in range(8):
                        mm(bank[:, qlo:256], wa[:, kc, 128 * sub:128 * sub + 128], hT[:, kc, qlo:256], kc == 0, kc == 7, [bf("hT"), bwa], [bb],
                           inc=(kc == 7 and not full))
                    if not full:
                        ACT(carry[:, fc, :], bank[:, 254:256], AF.Copy, [bb], [bf(f"carry{fc}")])
                        continue
                    for kc in range(8):
                        mm(bank[:, 256:512], wb[:, kc, 128 * sub:128 * sub + 128], hT[:, kc, :], kc == 0, kc == 7, [bf("hT"), bwb], [bb], inc=(kc == 7))
                    ae, bae = aext[fc % 2], bf(f"aext{fc % 2}")
                    c_, bc_ = cc[fc % 3], bf(f"cc{fc % 3}")
                    bcar = bf(f"carry{fc}")
                    cwc = [ptab[:, _poff["cw"] + k * 22 + fc:_poff["cw"] + k * 22 + fc + 1] for k in range(3)]
                    ACT(ae[:, 0:2], carry[:, fc, :], AF.Copy, [bcar], [bae])
                    ACT(ae[:, 2:258], bank[:, 0:256], AF.Copy, [bb], [bae])
                    ACT(c_[:], ae[:, 0:256], AF.Identity, [bae, bf("ptab")], [bc_], scale=cwc[0], bias=ptab[:, _poff["cb"] + fc:_poff["cb"] + fc + 1])
                    ACT(carry[:, fc, :], ae[:, 256:258], AF.Copy, [bae], [bcar])
                    if chain:
                        chain[-1][0]()
                    STT(c_[:], ae[:, 1:257], cwc[1], c_[:], ALU.mult, ALU.add, [bae, bc_], [bc_])
                    STT(c_[:], ae[:, 2:258], cwc[2], c_[:], ALU.mult, ALU.add, [bae, bc_], [bc_])
                    if chain:
                        chain[-1][1]()
                        chain.pop()

                    def _gelu(c_=c_, bc_=bc_):
                        ACT(c_[:], c_[:], AF.Gelu_apprx_tanh, [bc_], [bc_])

                    def _mult(c_=c_, bc_=bc_, bank=bank, bb=bb, fc=fc):
                        TT(gT[:, fc, :], c_[:], bank[:, 256:512], ALU.mult, [bc_, bb], [bf("gT")])
                    chain.append((_gelu, _mult))
            while chain:
                chain[-1][0]()
                chain[-1][1]()
                chain.pop()
            ck("ffn_up")
            if not full:
                return
            if T + 1 < last_t and not os.environ.get("KNOPRE"):
                for tt in range(2):
                    norm_T(2 * (T + 1) + tt, tt, "g1")
                prenorm.add(T + 1)
            for cb in range(4):
                units = []
                for kg in range(3):
                    nk = 8 if kg < 2 else 6
                    units.append(load_unit(wdn_s[cb, kg, :, 0:nk, :], nk, "wdn"))
                bank, bb = galloc()
                for tt in range(2):
                    for fc in range(NFC):
                        wt, bw = units[fc // 8]
                        mm(bank[:, 256 * tt:256 * tt + 256], gT[:, fc, 128 * tt:128 * tt + 128], wt[:, fc % 8, :], fc == 0, fc == NFC - 1,
                           [bf("gT"), bw], [bb], inc=(fc == NFC - 1))
                for tt in range(2):
                    i = 2 * T + tt
                    xs = xb[i % 4][:, 256 * cb:256 * cb + 256]
                    TT(xs, xs, bank[:, 256 * tt:256 * tt + 256], ALU.add, [bb], [bf(f"xb{i % 4}")])
            ck("ffn_dn")
            for tt in range(2):
                i = 2 * T + tt
                xt, bx = xb[i % 4], bf(f"xb{i % 4}")
                sc = ssq[:, (i % 2) * 4 + 2:(i % 2) * 4 + 3]
                rs = ssq[:, (i % 2) * 4 + 3:(i % 2) * 4 + 4]
                bs = bf(f"ssqf{i % 2}")
                ACT(xn[i % 2][:], xt[:], AF.Square, [bx], [bf(f"xn{i % 2}"), bs], accum_out=sc)
                ACT(rs, sc, AF.Ln, [bs, bf("epsc")], [bs], scale=1.0 / D, bias=epsc[:])
                ACT(rs, rs, AF.Exp, [bs], [bs], scale=-0.5)
                STT(xt[:], xt[:], rs, ptab[:, _poff["gf"]:_poff["gf"] + D], ALU.mult, ALU.mult, [bx, bs, bf("ptab")], [bx])
                r0 = (T - h0) * 256 + tt * 128
                out_toks.append(S.dma("pool", f"out{i % 4}", yout[r0:r0 + 128, :], xt[:], reads=[bx]))

        out_toks = []
        nload = [0]

        def ensure_loaded(upto):
            while nload[0] < upto:
                load_x(nload[0])
                nload[0] += 1

        try:
            ck("init")
            for T in range(0, last_t):
                _CUR[0] = T
                ensure_loaded(2 * T + 2)
                if T < first_t:
                    hist_mode[0] = True
                    tile(T, "hist")
                    hist_mode[0] = False
                    if T == first_t - 1:
                        ucache.clear()
                    ck("hist")
                elif T < h0:
                    tile(T, "halo")
                    ck("halo")
                else:
                    tile(T, "own")
                    ck("own")
        except _Stop:
            pass
        for e_ in ("pe", "act", "dve"):
            pass
        for tok in out_toks[-4:]:
            S.wait_tok("pool", tok)
        S.emit()
    return nc


_NC_CACHE = {}


def kernel(**inp):
    inp = {k: np.asarray(v) for k, v in inp.items()}
    x = inp["x"].astype(np.float32)
    if "nc" not in _NC_CACHE:
        _NC_CACHE["nc"] = build_nc()
    nc = _NC_CACHE["nc"]
    ptab = _ptab(inp)
    ctabs = [_ctab(0), _ctab(1)]
    in_maps = []
    for c in range(8):
        b, hf = c // 2, c % 2
        if hf == 1:
            xl = np.ascontiguousarray(x[b])
        else:
            xl = np.concatenate([np.zeros((4096, D), np.float32), x[b, :4096]], axis=0)
        in_maps.append({"x": xl, "ctab": ctabs[hf], "ptab": ptab,
                        "w_in": np.ascontiguousarray(inp["w_in"][0]), "w_out": np.ascontiguousarray(inp["w_out"][0]),
                        "w_up": np.ascontiguousarray(inp["w_up"][0]), "w_down": np.ascontiguousarray(inp["w_down"][0])})
    res = run_bass_kernel_spmd(nc, in_maps, core_ids=list(range(8)))
    out = np.zeros((4, 8192, D), np.float32)
    for c in range(8):
        b, hf = c // 2, c % 2
        out[b, hf * 4096:(hf + 1) * 4096] = res.results[c]["y"]
    return out
```

```python
import os
import numpy as np
from contextlib import ExitStack
import concourse.bass as bass
import concourse.mybir as mybir
from concourse.bass_utils import run_bass_kernel_spmd

F32 = mybir.dt.float32
BF16 = mybir.dt.bfloat16
AF = mybir.ActivationFunctionType
ALU = mybir.AluOpType

D = 1024
NH = 4
FF = 2816
NFC = 22
SLOPES = [2.0 ** -2, 2.0 ** -4, 2.0 ** -6, 2.0 ** -8]
GAM = [1.0 - 2.0 ** (-5 - h) for h in range(4)]
THR = 120.0
NEG = -30000.0
EPS = 1e-6
LAM_INIT = 0.2
RING = [8, 18, 64, 64]
NW = 8
FIRST_T = 15
LAST_T = 32

_off = {}
_cur = 0
for _n, _w in (("ident", 128), ("ones", 128), ("avg", 128), ("sel0", 128), ("sel1", 128), ("B", 512),
               ("Dt", 896), ("DM", 512), ("Ddec", 256), ("Dq", 256), ("g128", 2), ("e0", 128), ("e32", 128)):
    _off[_n] = _cur
    _cur += _w
NCT = _cur
_poff = {}
_cur = 0
for _n, _w in (("g1", 8), ("g2", 8), ("gf", 1024), ("subg", 1), ("gng", 4), ("cw", 66), ("cb", 22), ("lamv", 256)):
    _poff[_n] = _cur
    _cur += _w
NPT = _cur
DIDX = {(0, 0): 0, (1, 0): 1, (1, 1): 2, (2, 0): 3, (2, 1): 4, (3, 0): 5, (3, 1): 6}


def _ctab(hf):
    p = np.arange(128, dtype=np.float64)
    t = np.zeros((128, NCT), np.float64)
    t[:, _off["ident"]:_off["ident"] + 128] = np.eye(128)
    t[:, _off["ones"]:_off["ones"] + 128] = 1.0
    t[:, _off["avg"]:_off["avg"] + 128] = 1.0 / 128
    t[0, _off["e0"]:_off["e0"] + 128] = 1.0
    t[32, _off["e32"]:_off["e32"] + 128] = 1.0
    t[:, _off["sel0"] + 0] = 1.0
    t[:, _off["sel1"] + 32] = 1.0
    for h in range(4):
        m = SLOPES[h]
        for st in range(2):
            for dj in range(64):
                v = m * (p - 128.0 * dj)
                if st == 1 and hf == 0:
                    v = np.full(128, NEG)
                t[:, _off["B"] + (h * 2 + st) * 64 + dj] = v
    k = p[:, None]
    q = p[None, :]
    allowed = (k // 64) <= (q // 64)
    for (h, j), di in DIDX.items():
        m = SLOPES[h]
        v = np.where(allowed, -m * np.abs(q - k) + m * (128.0 * j + q), NEG)
        t[:, _off["Dt"] + di * 128:_off["Dt"] + di * 128 + 128] = v
    for h in range(4):
        v = np.where(allowed, GAM[h] ** np.abs(q - k) / 8.0, 0.0)
        t[:, _off["DM"] + h * 128:_off["DM"] + h * 128 + 128] = v
        t[:, _off["Ddec"] + h * 64:_off["Ddec"] + h * 64 + 64] = (GAM[h] ** (127.0 - p))[:, None]
    for pair in range(2):
        for half in range(2):
            h = 2 * pair + half
            rows = slice(64 * half, 64 * half + 64)
            t[rows, _off["Dq"] + pair * 128:_off["Dq"] + pair * 128 + 128] = (GAM[h] ** (p + 1.0) / 8.0)[None, :]
            t[rows, _off["g128"] + pair] = GAM[h] ** 128.0
    return t.astype(np.float32)


def _ptab(inp):
    t = np.zeros((128, NPT), np.float32)
    t[:, _poff["g1"]:_poff["g1"] + 8] = inp["norm_mix_g"][0].reshape(8, 128).T
    t[:, _poff["g2"]:_poff["g2"] + 8] = inp["norm_ffn_g"][0].reshape(8, 128).T
    t[:, _poff["gf"]:_poff["gf"] + 1024] = np.broadcast_to(inp["final_norm_g"][None, :], (128, 1024))
    t[:, _poff["subg"]] = inp["diff_subln_g"][0]
    t[:, _poff["gng"]:_poff["gng"] + 4] = inp["ret_gn_g"][0].reshape(4, 128).T
    t[:, _poff["cw"]:_poff["cw"] + 66] = inp["conv_w"][0].reshape(3, 22, 128).transpose(2, 0, 1).reshape(128, 66)
    t[:, _poff["cb"]:_poff["cb"] + 22] = inp["conv_b"][0].reshape(22, 128).T
    lv = np.concatenate([inp["lambda_q1"][0], inp["lambda_k1"][0], inp["lambda_q2"][0], inp["lambda_k2"][0]])
    t[:, _poff["lamv"]:_poff["lamv"] + 256] = np.broadcast_to(lv[None, :], (128, 256))
    return t


class Tok:
    __slots__ = ("sem", "val", "eng")

    def __init__(self, sem, val, eng):
        self.sem, self.val, self.eng = sem, val, eng


class Buf:
    def __init__(self, name=""):
        self.name = name
        self.w = None
        self.rs = []
        self.excl = name in ("S0", "S1", "O0", "O1", "Zb", "TR", "G0", "G1", "SA", "SB")
        self.group = []
        self.alias = []


class Sched:
    ENGS = ("pe", "act", "dve", "pool", "sp")

    def __init__(self, nc, es):
        self.nc = nc
        self.es = es
        self.prog = {e: [] for e in self.ENGS}
        self.sem = {e: es.enter_context(nc.semaphore("ps_" + e)) for e in self.ENGS}
        self.count = {e: 0 for e in self.ENGS}
        self.waited = {e: {} for e in self.ENGS}
        self.dsems = {}
        self.dcount = {}

    def _waits(self, eng, reads, writes):
        need = []
        for b in reads:
            if b.w is not None:
                need.append((b.w, True))
            for ob in b.alias:
                if ob.w is not None:
                    need.append((ob.w, True))
            if b.excl:
                for t in b.rs:
                    need.append((t, False))
                for ob in b.group + b.alias:
                    for t in ob.rs:
                        need.append((t, False))
        for b in writes:
            if b.w is not None:
                need.append((b.w, False))
            for t in b.rs:
                need.append((t, False))
            for ob in b.alias:
                if ob.w is not None:
                    need.append((ob.w, False))
                for t in ob.rs:
                    need.append((t, False))
        best = {}
        for t, raw in need:
            if t.eng == eng and (eng == "pe" or not raw):
                continue
            k = id(t.sem)
            if self.waited[eng].get(k, 0) >= t.val:
                continue
            if k not in best or best[k].val < t.val:
                best[k] = t
        out = []
        for k, t in best.items():
            self.waited[eng][k] = t.val
            out.append(t)
            if os.environ.get("KLOG"):
                print("LOG", eng, "wait", t.eng, t.val)
        return out

    def _emit_waits(self, eng, toks):
        for t in toks:
            self.prog[eng].append(lambda e, sem=t.sem, val=t.val: e.wait_ge(sem, val))

    def _commit(self, tok, reads, writes):
        for b in reads:
            b.rs = [t for t in b.rs if t.sem is not tok.sem]
            b.rs.append(tok)
        for b in writes:
            b.w = tok
            b.rs = []

    def op(self, eng, fn, reads=(), writes=(), inc=True):
        toks = self._waits(eng, reads, writes)
        emb = None
        if toks and not os.environ.get("KNOEMB"):
            emb = toks[-1]
            toks = toks[:-1]
        self._emit_waits(eng, toks)
        if emb is not None:
            fn0, esem, eval_ = fn, emb.sem, emb.val
            fn = lambda e, fn0=fn0, esem=esem, eval_=eval_: fn0(e)._wait_ge(esem, eval_)
        sem = self.sem[eng]
        tok = Tok(sem, self.count[eng] + 1, eng)
        if os.environ.get("KLOG"):
            print("LOG", eng, "op", inc, self.count[eng] + 1, [b.name for b in reads], [b.name for b in writes])
        if inc:
            self.count[eng] += 1
            self.prog[eng].append(lambda e, fn=fn, sem=sem: fn(e).then_inc(sem, 1))
        else:
            self.prog[eng].append(lambda e, fn=fn: fn(e))
        self._commit(tok, reads, writes)
        return tok

    def dma(self, eng, dname, out, in_, reads=(), writes=()):
        self._emit_waits(eng, self._waits(eng, reads, writes))
        if dname not in self.dsems:
            self.dsems[dname] = self.es.enter_context(self.nc.semaphore("d_" + dname))
            self.dcount[dname] = 0
        sem = self.dsems[dname]
        self.dcount[dname] += 16
        tok = Tok(sem, self.dcount[dname], "dma")
        self.prog[eng].append(lambda e, out=out, in_=in_, sem=sem: e.dma_start(out=out, in_=in_).then_inc(sem, 16))
        self._commit(tok, reads, writes)
        return tok

    def wait_tok(self, eng, tok):
        k = id(tok.sem)
        if self.waited[eng].get(k, 0) >= tok.val:
            return
        self.waited[eng][k] = tok.val
        self.prog[eng].append(lambda e, sem=tok.sem, val=tok.val: e.wait_ge(sem, val))

    def emit(self):
        with self.nc.Block() as block:
            @block.sync
            def _(e):
                for f in self.prog["sp"]:
                    f(e)

            @block.tensor
            def _(e):
                for f in self.prog["pe"]:
                    f(e)

            @block.scalar
            def _(e):
                for f in self.prog["act"]:
                    f(e)

            @block.vector
            def _(e):
                for f in self.prog["dve"]:
                    f(e)

            @block.gpsimd
            def _(e):
                for f in self.prog["pool"]:
                    f(e)


import os


class _Stop(Exception):
    pass


_CUR = [0]


def ck(name):
    ks = os.environ.get("KSTOP")
    if ks == name or ks == f"{name}@{_CUR[0]}":
        raise _Stop()


def build_nc(h0=16, nown=16):
    first_t, last_t = h0 - 1, h0 + nown
    nc = bass.Bass("TRN2", target_bir_lowering=False)
    xin = nc.dram_tensor("x", [256 * (h0 + nown), D], F32, kind="ExternalInput").ap()
    ctab_d = nc.dram_tensor("ctab", [128, NCT], F32, kind="ExternalInput").ap()
    ptab_d = nc.dram_tensor("ptab", [128, NPT], F32, kind="ExternalInput").ap()
    w_in_d = nc.dram_tensor("w_in", [D, 3072], F32, kind="ExternalInput").ap()
    w_out_d = nc.dram_tensor("w_out", [D, D], F32, kind="ExternalInput").ap()
    w_up_d = nc.dram_tensor("w_up", [D, 2 * FF], F32, kind="ExternalInput").ap()
    w_dn_d = nc.dram_tensor("w_down", [FF, D], F32, kind="ExternalInput").ap()
    yout = nc.dram_tensor("y", [256 * nown, D], F32, kind="ExternalOutput").ap()
    win_s = nc.dram_tensor("win_s", [12, 128, 8, 256], BF16, kind="Internal").ap()
    wout_s = nc.dram_tensor("wout_s", [4, 128, 8, 256], BF16, kind="Internal").ap()
    wup_s = nc.dram_tensor("wup_s", [22, 128, 8, 256], BF16, kind="Internal").ap()
    wdn_s = nc.dram_tensor("wdn_s", [4, 3, 128, 8, 256], BF16, kind="Internal").ap()

    es = ExitStack()
    with es:
        S = Sched(nc, es)
        A = nc.alloc_sbuf_tensor

        def PS(name, shape, dt=F32):
            return es.enter_context(nc.psum_tensor(name, shape, dt))

        ctab = A("ctab_t", [128, NCT], F32)
        ptab = A("ptab_t", [128, NPT], F32)
        idb = A("idb", [128, 128], BF16)
        selb = A("selb", [128, 256], BF16)
        epsc = A("epsc", [128, 1], F32)
        lamc = A("lamc", [128, 4], F32)
        lamt = A("lamt", [128, 128], F32)
        gcol = A("gcol", [128, 1], F32)
        KT = [A(f"KT{h}", [128, RING[h] * 128], BF16) for h in range(4)]
        VV = [A(f"VV{h}", [128, RING[h], 128], BF16) for h in range(4)]
        wring = [A(f"wr{i}", [128, 8, 256], BF16) for i in range(NW)]
        xb = [A(f"xb{i}", [128, D], F32) for i in range(4)]
        xn = [A(f"xn{i}", [128, D], BF16) for i in range(2)]
        ssq = A("ssq", [128, 8], F32)
        hT = A("hT", [128, 8, 256], BF16)
        QTc = A("QTc", [128, 4, 2, 256], BF16)
        rqT = A("rqT", [128, 2, 256], BF16)
        rqdT = A("rqdT", [128, 2, 256], BF16)
        rkT = A("rkT", [128, 2, 256], BF16)
        kd = [A(f"kd{i}", [128, 256], BF16) for i in range(2)]
        rvt = [A(f"rvt{i}", [128, 512], BF16) for i in range(2)]
        St = A("St", [128, 2, 128], F32)
        Stb = A("Stb", [128, 2, 128], BF16)
        sg = A("sg", [128, 4, 256], F32)
        oT = A("oT", [128, 8, 256], BF16)
        PT = [A(f"PT{i}", [128, 2, 256], BF16) for i in range(3)]
        dtmp = [A("dtmp0", [128, 2, 128], F32)] * 2
        zaccs = [A("zacc0", [128, 512], F32), A("zacc1", [128, 512], F32)]
        epsz = A("epsz", [128, 1], F32)
        t1 = A("t1", [128, 512], F32)
        t1s = [t1, A("t1b", [128, 512], F32)]
        t2 = A("t2", [128, 512], F32)
        t3 = A("t3", [128, 512], F32)
        t4 = A("t4", [128, 512], F32)
        AT = [A(f"AT{i}", [128, 128], BF16) for i in range(4)]
        gT = A("gT", [128, NFC, 256], BF16)
        aext = [A(f"aext{i}", [128, 258], F32) for i in range(2)]
        cc = [A(f"cc{i}", [128, 256], F32) for i in range(3)]
        carry = A("carry", [128, NFC, 2], F32)

        S2 = PS("S2", [128, 2, 512])
        O2 = PS("O2", [128, 2, 512])
        Zb = PS("Zb", [128, 512])
        TRf = PS("TR", [128, 512], F32)
        TR = TRf[:].bitcast(BF16)
        G = [PS("G0", [128, 512]), PS("G1", [128, 512])]

        B = {}

        def bf(name):
            if name not in B:
                B[name] = Buf(name)
            return B[name]

        in_att = [False]
        gctr = [0]

        def galloc():
            if in_att[0]:
                i = gctr[0] % 2
                gctr[0] += 1
                return G[i], bf(f"G{i}")
            i = gctr[0] % 4
            gctr[0] += 1
            if i < 2:
                return G[i], bf(f"G{i}")
            return S2[:, i - 2, :], bf(f"S{i - 2}")

        def ct(name, lo=0, n=None):
            o = _off[name] + lo
            return ctab[:, o:o + (n if n is not None else 1)]

        def pt(name, lo=0, n=None):
            o = _poff[name] + lo
            return ptab[:, o:o + (n if n is not None else 1)]

        tc = S.dma("sp", "c0", ctab[:], ctab_d, writes=[bf("ctab")])
        tp = S.dma("sp", "c1", ptab[:], ptab_d, writes=[bf("ptab")])
        for wsrc, wdst, nm in ((w_in_d, win_s, "win"), (w_out_d, wout_s, "wout"), (w_up_d, wup_s, "wup")):
            tok = None
            for kc in range(8):
                tok = S.dma("pool", "cast_" + nm, wdst[:, :, kc, :], wsrc[128 * kc:128 * kc + 128, :].rearrange("p (u n) -> u p n", n=256))
            bf(nm).w = tok
        tok = None
        for rb in range(NFC):
            tok = S.dma("pool", "cast_wdn", wdn_s[:, rb // 8, :, rb % 8, :], w_dn_d[128 * rb:128 * rb + 128, :].rearrange("p (u n) -> u p n", n=256))
        bf("wdn").w = tok
        S.op("dve", lambda e: e.tensor_copy(out=idb[:], in_=ct("ident", 0, 128)), reads=[bf("ctab")], writes=[bf("idb")])
        S.op("dve", lambda e: e.tensor_copy(out=selb[:], in_=ct("sel0", 0, 256)), reads=[bf("ctab")], writes=[bf("selb")])
        S.op("dve", lambda e: e.memset(epsc[:], EPS), writes=[bf("epsc")])
        S.op("dve", lambda e: e.memset(St[:], 0.0), writes=[bf("St")])
        S.op("dve", lambda e: e.memset(Stb[:], 0.0), writes=[bf("Stb")])
        S.op("dve", lambda e: e.memset(carry[:], 0.0), writes=[bf("carry")])
        S.op("dve", lambda e: e.memset(oT[:], 0.0), writes=[bf(f"oT{k_}") for k_ in range(8)])
        S.op("dve", lambda e: e.memset(QTc[:], 0.0), writes=[bf("QT")])
        S.op("dve", lambda e: e.memset(epsz[:], 1e-18), writes=[bf("epsz")])
        S.op("dve", lambda e: e.tensor_tensor(out=lamt[:, 0:64], in0=pt("lamv", 0, 64), in1=pt("lamv", 64, 64), op=ALU.mult),
             reads=[bf("ptab")], writes=[bf("lamt")])
        S.op("dve", lambda e: e.tensor_tensor(out=lamt[:, 64:128], in0=pt("lamv", 128, 64), in1=pt("lamv", 192, 64), op=ALU.mult),
             reads=[bf("ptab")], writes=[bf("lamt")])
        S.op("dve", lambda e: e.reduce_sum(out=lamc[:, 2:3], in_=lamt[:, 0:64], axis=mybir.AxisListType.X), reads=[bf("lamt")], writes=[bf("lamc")])
        S.op("dve", lambda e: e.reduce_sum(out=lamc[:, 3:4], in_=lamt[:, 64:128], axis=mybir.AxisListType.X), reads=[bf("lamt")], writes=[bf("lamc")])
        S.op("act", lambda e: e.activation(out=lamc[:, 2:4], in_=lamc[:, 2:4], func=AF.Exp), reads=[bf("lamc")], writes=[bf("lamc")])
        S.op("dve", lambda e: e.tensor_tensor(out=lamc[:, 0:1], in0=lamc[:, 2:3], in1=lamc[:, 3:4], op=ALU.subtract), reads=[bf("lamc")], writes=[bf("lamc")])
        S.op("dve", lambda e: e.tensor_scalar(out=lamc[:, 1:2], in0=lamc[:, 0:1], scalar1=-1.0, scalar2=-LAM_INIT, op0=ALU.mult, op1=ALU.add),
             reads=[bf("lamc")], writes=[bf("lamc2")])
        S.op("dve", lambda e: e.tensor_scalar(out=gcol[:], in0=pt("subg"), scalar1=1.0 - LAM_INIT, scalar2=None, op0=ALU.mult),
             reads=[bf("ptab")], writes=[bf("gcol")])

        wctr = [0]

        def load_unit(src, nk, srcbuf):
            i = wctr[0] % NW
            wctr[0] += 1
            S.dma("sp", f"w{i}", wring[i][:, 0:nk, :], src, reads=[bf(srcbuf)], writes=[bf(f"wr{i}")])
            return wring[i], bf(f"wr{i}")

        ucache = {}
        hist_mode = [False]

        def unit_in(c0):
            if hist_mode[0] and c0 in ucache:
                return ucache[c0]
            r = load_unit(win_s[c0 // 256], 8, "win")
            if hist_mode[0]:
                ucache[c0] = r
            return r

        def load_x(i):
            S.dma("sp", f"x{i % 4}", xb[i % 4][:], xin[i * 128:(i + 1) * 128, :], writes=[bf(f"xb{i % 4}")])

        def mm(out, lhsT, rhs, start, stop, reads, writes, inc, **kw):
            S.op("pe", lambda e: e.matmul(out=out, lhsT=lhsT, rhs=rhs, start=start, stop=stop, **kw), reads=reads, writes=writes, inc=inc)


        def ACT(out, in_, func, reads, writes, **kw):
            return S.op("act", lambda e: e.activation(out=out, in_=in_, func=func, **kw), reads=reads, writes=writes)

        def TT(out, in0, in1, op, reads, writes, eng="dve"):
            S.op(eng, lambda e: e.tensor_tensor(out=out, in0=in0, in1=in1, op=op), reads=reads, writes=writes)

        def TS(out, in0, s1, s2, op0, op1, reads, writes):
            if op1 is None:
                S.op("dve", lambda e: e.tensor_scalar(out=out, in0=in0, scalar1=s1, scalar2=None, op0=op0), reads=reads, writes=writes)
            else:
                S.op("dve", lambda e: e.tensor_scalar(out=out, in0=in0, scalar1=s1, scalar2=s2, op0=op0, op1=op1), reads=reads, writes=writes)

        def STT(out, in0, scalar, in1, op0, op1, reads, writes):
            S.op("dve", lambda e: e.scalar_tensor_tensor(out=out, in0=in0, scalar=scalar, in1=in1, op0=op0, op1=op1), reads=reads, writes=writes)

        def CP(out, in_, reads, writes, eng="dve"):
            if os.environ.get("KCP") == "copy":
                S.op(eng, lambda e: e.tensor_copy(out=out, in_=in_), reads=reads, writes=writes)
            else:
                S.op(eng, lambda e: e.tensor_scalar(out=out, in0=in_, scalar1=1.0, scalar2=None, op0=ALU.mult), reads=reads, writes=writes)

        def TRN(out, in_, reads, writes, inc):
            S.op("pe", lambda e: e.transpose(out=out, in_=in_, identity=idb[:]), reads=reads, writes=writes, inc=inc)

        def norm_T(i, tt, gname):
            xt, bx = xb[i % 4], bf(f"xb{i % 4}")
            xnn, bxn = xn[i % 2], bf(f"xn{i % 2}")
            sc = ssq[:, (i % 2) * 4:(i % 2) * 4 + 1]
            rs = ssq[:, (i % 2) * 4 + 1:(i % 2) * 4 + 2]
            bs = bf(f"ssq{i % 2}")
            ACT(xnn[:], xt[:], AF.Square, [bx], [bxn, bs], accum_out=sc)
            ACT(rs, sc, AF.Ln, [bs, bf("epsc")], [bs], scale=1.0 / D, bias=epsc[:])
            ACT(rs, rs, AF.Exp, [bs], [bs], scale=-0.5)
            ACT(xnn[:], xt[:], AF.Copy, [bx, bs], [bxn], scale=rs)
            for c in range(8):
                TRN(TR[:, c * 128:(c + 1) * 128], xnn[:, c * 128:(c + 1) * 128], [bxn, bf("idb")], [bf("TR")], inc=(c == 7))
            for c in range(8):
                TS(hT[:, c, tt * 128:(tt + 1) * 128], TR[:, c * 128:(c + 1) * 128], pt(gname, c, 1), None, ALU.mult, None,
                   [bf("TR"), bf("ptab")], [bf("hT")])

        def v2(ap):
            return ap.rearrange("p (a b) -> p a b", a=2)

        def proj_fm(wt, bw, cols, lo=0):
            bank, bb = galloc()
            n = len(cols)
            for ci, c0 in enumerate(cols):
                for kc in range(8):
                    mm(bank[:, ci * 256 + lo:ci * 256 + 256], wt[:, kc, c0:c0 + 128], hT[:, kc, lo:256], kc == 0, kc == 7,
                       [bf("hT"), bw], [bb], inc=(kc == 7 and ci == n - 1))
            return bank, bb

        def proj_tm(wt, bw, tt):
            bank, bb = galloc()
            for kc in range(8):
                mm(bank[:, 0:256], hT[:, kc, tt * 128:(tt + 1) * 128], wt[:, kc, :], kc == 0, kc == 7,
                   [bf("hT"), bw], [bb], inc=(kc == 7))
            return bank, bb

        def kt_lo(h, T):
            lo = 0
            Q0 = 256 * T
            for kt in range(2 * T + 2):
                if (Q0 - 128 * kt - 127) * SLOPES[h] >= THR:
                    lo = kt + 1
            return lo

        def need_kv(h, T):
            for kt in (2 * T, 2 * T + 1):
                for Tq in range(max(T, first_t), last_t):
                    if kt >= kt_lo(h, Tq):
                        return True
            return False

        def segs(h, T, kt, qlo):
            out = []
            if h == 0:
                for qb in (0, 1):
                    lo, hi = max(qlo, 128 * qb), 128 * qb + 128
                    if lo >= hi:
                        continue
                    ktd = 2 * T + qb
                    if kt < ktd:
                        out.append((lo, hi, "past", ktd - kt, 0))
                    elif kt == ktd:
                        out.append((lo, hi, "diag", 0, 128 * qb))
            else:
                if kt < 2 * T:
                    out.append((qlo, 256, "past", 2 * T - kt, 0))
                elif kt == 2 * T:
                    if qlo < 128:
                        out.append((qlo, 128, "diag", 0, 0))
                    out.append((max(qlo, 128), 256, "past", 0, 0))
                else:
                    out.append((max(qlo, 128), 256, "diag", 1, 128))
            return out

        actr = [0]
        att_stages = []
        ones_c = _off["ones"]
        avg_ap = ctab[:, _off["avg"]:_off["avg"] + 128]

        def sbank(pb):
            return S2[:, pb, :] if pb < 2 else TRf[:, :]

        def sbuf_(pb):
            return bf(f"S{pb}") if pb < 2 else bf("TR")

        def attention(T, h, qlo):
            pbO = h % 2
            bO = bf(f"O{pbO}")
            bZ = bf("Zb")
            zoff = 256 * (h % 2)
            kts = [kt for kt in range(kt_lo(h, T), 2 * T + 2) if segs(h, T, kt, qlo)]
            info = []
            for kt in kts:
                sg_ = segs(h, T, kt, qlo)
                clo = min(s[0] for s in sg_)
                pb = actr[0] % 3
                pi = actr[0] % 3
                actr[0] += 1
                info.append((kt, sg_, clo, pb, pi))

            def qk(ki):
                kt, sg_, clo, pb, pi = info[ki]
                slot = kt % RING[h]
                if clo == 0:
                    mm(sbank(pb), KT[h][:, slot * 128:(slot + 1) * 128], QTc[:, h, :, :].rearrange("p a b -> p (a b)"),
                       True, True, [bf(f"KT{h}_{slot}"), bf("QT")], [sbuf_(pb)], inc=True)
                else:
                    for half in range(2):
                        mm(sbank(pb)[:, 256 * half + clo:256 * half + 256], KT[h][:, slot * 128:(slot + 1) * 128], QTc[:, h, half, clo:256],
                           True, True, [bf(f"KT{h}_{slot}"), bf("QT")], [sbuf_(pb)], inc=(half == 1))

            exp_toks = []
            add_toks = []
            pool_toks = []
            qk(0)
            if len(info) > 1:
                qk(1)
            for ki, (kt, sg_, clo, pb, pi) in enumerate(info):
                bS = sbuf_(pb)
                slot = kt % RING[h]
                bV = bf(f"VV{h}_{slot}")
                bP = bf(f"PT{pi}")
                sset = 1 if kt < 2 * h0 else 0
                if ki >= 2:
                    S.wait_tok("act", add_toks[ki - 2])
                for (lo, hi, typ, prm, bs0) in sg_:
                    if typ == "past":
                        col = _off["B"] + (h * 2 + sset) * 64 + prm
                        tk = ACT(PT[pi][:, :, lo:hi], v2(sbank(pb))[:, :, lo:hi], AF.Exp, [bS, bf("ctab")], [bP], bias=ctab[:, col:col + 1])
                    else:
                        di = DIDX[(h, prm)]
                        dsel = (ki + h) % 2
                        dt_, bd = dtmp[dsel], bf("dtmp0")
                        dcol = _off["Dt"] + di * 128
                        for half in range(2):
                            TT(dt_[:, half, lo - bs0:hi - bs0], sbank(pb)[:, 256 * half + lo:256 * half + hi],
                               ctab[:, dcol + lo - bs0:dcol + hi - bs0], ALU.add, [bS, bf("ctab")], [bd])
                        tk = ACT(PT[pi][:, :, lo:hi], dt_[:, :, lo - bs0:hi - bs0], AF.Exp, [bd], [bP])
                exp_toks.append(tk)
                if ki + 2 < len(info):
                    qk(ki + 2)
                if ki % 2 == 1 and att_stages:
                    att_stages.pop(0)()
                first, last = (ki == 0), (ki == len(info) - 1)
                if clo == 0:
                    mm(O2[:, pbO, :], VV[h][:, slot, :], PT[pi][:].rearrange("p a b -> p (a b)"), first, last,
                       [bV, bP], [bO], inc=last, skip_group_check=True)
                else:
                    for half in range(2):
                        mm(O2[:, pbO, 256 * half + clo:256 * half + 256], VV[h][:, slot, :], PT[pi][:, half, clo:256], first and half == 0, last,
                           [bV, bP], [bO], inc=(last and half == 1), skip_group_check=True)
                zv = v2(Zb[:])
                zaccv = v2(zaccs[h % 2][:])
                bzB = bf(f"zaccB{h % 2}")
                S.wait_tok("pool", exp_toks[ki])
                if first:
                    tokp = S.op("pool", lambda e, zaccv=zaccv, pi=pi, clo=clo: e.tensor_copy(out=zaccv[:, 1, clo:256], in_=PT[pi][:, 1, clo:256]),
                                reads=[], writes=[bzB])
                else:
                    tokp = S.op("pool", lambda e, zaccv=zaccv, pi=pi, clo=clo: e.tensor_tensor(out=zaccv[:, 1, clo:256], in0=zaccv[:, 1, clo:256],
                                                                                             in1=PT[pi][:, 1, clo:256], op=ALU.add), reads=[bzB], writes=[bzB])
                if pool_toks:
                    S.wait_tok("dve", pool_toks[-1])
                pool_toks.append(tokp)
                if first:
                    tokd = S.op("dve", lambda e, zv=zv, pi=pi, clo=clo: e.tensor_copy(out=zv[:, 0, clo:256], in_=PT[pi][:, 0, clo:256]), reads=[bP], writes=[bZ])
                else:
                    tokd = S.op("dve", lambda e, zv=zv, pi=pi, clo=clo: e.tensor_tensor(out=zv[:, 0, clo:256], in0=zv[:, 0, clo:256],
                                                                                       in1=PT[pi][:, 0, clo:256], op=ALU.add), reads=[bZ, bP], writes=[bZ])
                add_toks.append(tokd)
            S.wait_tok("act", pool_toks[-1])
            zacc, bza = zaccs[h % 2], bf(f"zacc{h % 2}")
            ACT(zacc[:, qlo:256], Zb[:, qlo:256], AF.Copy, [bZ], [bza])
            bzB_ = bf(f"zaccB{h % 2}")
            t1h, bt1 = t1s[h % 2], bf(f"t1_{h % 2}")
            t1v = v2(t1h[:])
            ACT(t1v[:, :, qlo:256], v2(O2[:, pbO, :])[:, :, qlo:256], AF.Copy, [bO], [bt1])
            hold = {}
            rinv = v2(t4[:])
            t2v = v2(t2[:])

            def s1():
                bank, bb = galloc()
                if qlo == 0:
                    mm(bank[:, :], ctab[:, ones_c:ones_c + 128], zacc[:, :], True, True, [bza, bzB_, bf("ctab")], [bb], inc=True)
                else:
                    for half in range(2):
                        mm(bank[:, 256 * half + qlo:256 * half + 256], ctab[:, ones_c:ones_c + 128], zacc[:, 256 * half + qlo:256 * half + 256],
                           True, True, [bza, bzB_, bf("ctab")], [bb], inc=(half == 1))
                ACT(rinv[:, :, qlo:256], v2(bank[:])[:, :, qlo:256], AF.Ln, [bb, bf("epsz")], [bf("t4"), bf("t4b")], bias=epsz[:])

            def s2():
                ACT(rinv[:, :, qlo:256], rinv[:, :, qlo:256], AF.Exp, [bf("t4")], [bf("t4"), bf("t4b")], scale=-1.0)
                TS(rinv[:, 1, qlo:256], rinv[:, 1, qlo:256], lamc[:, 1:2], None, ALU.mult, None, [bf("t4"), bf("lamc2")], [bf("t4"), bf("t4b")])

            def s3():
                TT(t2v[:, :, qlo:256], t1v[:, :, qlo:256], rinv[:, :, qlo:256], ALU.mult, [bt1, bf("t4")], [bf("t2")])
                TT(t3[:, qlo:256], t2v[:, 0, qlo:256], t2v[:, 1, qlo:256], ALU.add, [bf("t2")], [bf("t3")])
                ACT(t2[:, qlo:256], t3[:, qlo:256], AF.Square, [bf("t3")], [bf("t2")])

            def s4():
                bank2, bb2 = galloc()
                mm(bank2[:, qlo:256], avg_ap, t2[:, qlo:256], True, True, [bf("t2"), bf("ctab")], [bb2], inc=True)
                ACT(t2[:, 256 + qlo:512], bank2[:, qlo:256], AF.Ln, [bb2, bf("epsc")], [bf("t2")], bias=epsc[:])

            def s5():
                ACT(t2[:, 256 + qlo:512], t2[:, 256 + qlo:512], AF.Exp, [bf("t2")], [bf("t2")], scale=-0.5)
                STT(oT[:, h, qlo:256], t3[:, qlo:256], gcol[:], t2[:, 256 + qlo:512], ALU.mult, ALU.mult,
                    [bf("t3"), bf("t2"), bf("gcol")], [bf(f"oT{h}")])
            return [s1, s2, s3, s4, s5]

        prenorm = set()

        def tile(T, mode):
            own = mode != "hist"
            qlo = 254 if mode == "halo" else 0
            full = mode == "own"
            if T not in prenorm:
                for tt in range(2):
                    norm_T(2 * T + tt, tt, "g1")
            if not own:
                ensure_loaded(min(2 * T + 4, 2 * last_t))
            ck("norm")
            hs_kv = [h for h in range(4) if need_kv(h, T)]
            if own:
                wt, bw = unit_in(1536)
                bank, bb = proj_fm(wt, bw, [0, 128], lo=qlo)
                bv = v2(bank[:])
                ACT(rqT[:, :, qlo:256], bv[:, :, qlo:256], AF.Copy, [bb], [bf("rqT")])
                dqv = v2(ctab[:, _off["Dq"]:_off["Dq"] + 256])
                for tt in range(2):
                    lo = max(qlo, 128 * tt)
                    if lo >= 128 * tt + 128:
                        continue
                    TT(rqdT[:, :, lo:128 * tt + 128], bv[:, :, lo:128 * tt + 128], dqv[:, :, lo - 128 * tt:128], ALU.mult,
                       [bb, bf("ctab")], [bf("rqdT")])
            wt, bw = unit_in(1792)
            for tt in range(2):
                bank, bb = proj_tm(wt, bw, tt)
                TT(kd[tt][:], bank[:, 0:256], ctab[:, _off["Ddec"]:_off["Ddec"] + 256], ALU.mult, [bb, bf("ctab")], [bf(f"kd{tt}")])
            if own:
                bank, bb = proj_fm(wt, bw, [0, 128], lo=0)
                ACT(rkT[:], v2(bank[:]), AF.Copy, [bb], [bf("rkT")])
            ck("rk")
            for u in range(2):
                wt, bw = unit_in(2048 + 256 * u)
                for tt in range(2):
                    bank, bb = proj_tm(wt, bw, tt)
                    ACT(rvt[tt][:, 256 * u:256 * u + 256], bank[:, 0:256], AF.Copy, [bb], [bf(f"rvt{tt}_{u}")])
            if own:
                for u in range(2):
                    wt, bw = unit_in(2560 + 256 * u)
                    bank, bb = proj_fm(wt, bw, [0, 128], lo=qlo)
                    ACT(sg[:, 2 * u:2 * u + 2, qlo:256], v2(bank[:])[:, :, qlo:256], AF.Silu, [bb], [bf(f"sg{u}")])
            stages = []
            rbanks = [(O2[:, 0, :], bf("O0")), (O2[:, 1, :], bf("O1"))] if own else None
            for tt in range(2):
                if own and not (mode == "halo" and tt == 0):
                    lo = max(qlo, 128 * tt) - 128 * tt

                    def st_a(tt=tt, lo=lo):
                        for h in range(4):
                            rows = slice(64 * (h % 2), 64 * (h % 2) + 64)
                            pr = h // 2
                            so = 128 * pr
                            mm(S2[:, h % 2, so + lo:so + 128], rkT[rows, pr, 128 * tt:128 * tt + 128], rqT[rows, pr, 128 * tt + lo:128 * tt + 128],
                               True, True, [bf("rkT"), bf("rqT")], [bf(f"S{h % 2}")], inc=True)

                        st_b(tt, lo)

                    def st_b(tt, lo):
                        for h in range(4):
                            so = 128 * (h // 2)
                            dmc = _off["DM"] + 128 * h
                            TT(AT[h][:, lo:128], S2[:, h % 2, so + lo:so + 128], ctab[:, dmc + lo:dmc + 128], ALU.mult,
                               [bf(f"S{h % 2}"), bf("ctab")], [bf(f"AT{h}")])

                    def st_c(tt=tt, lo=lo):
                        for h in range(4):
                            rows = slice(64 * (h % 2), 64 * (h % 2) + 64)
                            pr = h // 2
                            rb = rbanks[pr][0]
                            rbb = [bf(f"O{pr}")]
                            oo = 256 * (h % 2) + 128 * tt
                            mm(rb[:, oo + lo:oo + 128], rvt[tt][:, 128 * h:128 * h + 128], AT[h][:, lo:128], True, False,
                               [bf(f"AT{h}"), bf(f"rvt{tt}_{h // 2}")], rbb, inc=False)
                            mm(rb[:, oo + lo:oo + 128], Stb[rows, pr, :], rqdT[rows, pr, 128 * tt + lo:128 * tt + 128], False, True,
                               [bf("Stb"), bf("rqdT")], rbb, inc=True)
                    stages += [st_a, st_c]

                def st_d(tt=tt):
                    bank, bb = galloc()
                    kvv = v2(bank[:, 0:256])
                    for h in range(4):
                        mm(kvv[64 * (h % 2):64 * (h % 2) + 64, h // 2, :], kd[tt][:, 64 * h:64 * h + 64], rvt[tt][:, 128 * h:128 * h + 128],
                           True, True, [bf(f"kd{tt}"), bf(f"rvt{tt}_{h // 2}")], [bb], inc=(h == 3))

                    def st_e():
                        for pr in range(2):
                            STT(St[:, pr, :], St[:, pr, :], ctab[:, _off["g128"] + pr:_off["g128"] + pr + 1], kvv[:, pr, :], ALU.mult, ALU.add,
                                [bb, bf("St"), bf("ctab")], [bf("St")])
                        ACT(Stb[:], St[:], AF.Copy, [bf("St")], [bf("Stb")])
                    st_e()
                stages.append(st_d)
            if own:
                v1_, v2_, v3_, v4_ = v2(t1[:]), v2(t2[:]), v2(t3[:]), v2(t4[:])
                for pr in range(2):
                    hold = {}

                    def pa(pr=pr, hold=hold):
                        rv_ = v2(rbanks[pr][0])
                        ACT(v1_[:, :, qlo:256], rv_[:, :, qlo:256], AF.Copy, [bf(f"O{pr}")], [bf("t1_0")])
                        bk, bkb = Zb, bf("Zb")
                        bkv = v2(bk[:])
                        for a_ in range(2):
                            mm(bkv[:, a_, qlo:256], avg_ap, v1_[:, a_, qlo:256], True, True, [bf("t1_0"), bf("ctab")], [bkb], inc=(a_ == 1))
                        hold["bkv"], hold["bkb"] = bkv, bkb

                    def pb_(pr=pr, hold=hold):
                        TT(v2_[:, :, qlo:256], v1_[:, :, qlo:256], hold["bkv"][:, :, qlo:256], ALU.subtract, [bf("t1_0"), hold["bkb"]], [bf("t2")])
                        ACT(v3_[:, :, qlo:256], v2_[:, :, qlo:256], AF.Square, [bf("t2")], [bf("t3")])
                        bk2, bkb2 = rbanks[pr][0], bf(f"O{pr}")
                        bkv2 = v2(bk2)
                        for a_ in range(2):
                            mm(bkv2[:, a_, qlo:256], avg_ap, v3_[:, a_, qlo:256], True, True, [bf("t3"), bf("ctab")], [bkb2], inc=(a_ == 1))
                        hold["bkv2"], hold["bkb2"] = bkv2, bkb2

                    def pc(pr=pr, hold=hold):
                        ACT(v4_[:, :, qlo:256], hold["bkv2"][:, :, qlo:256], AF.Ln, [hold["bkb2"], bf("epsc")], [bf("t4"), bf("t4b")], bias=epsc[:])
                        ACT(v4_[:, :, qlo:256], v4_[:, :, qlo:256], AF.Exp, [bf("t4")], [bf("t4"), bf("t4b")], scale=-0.5)
                        for a_ in range(2):
                            h = 2 * pr + a_
                            STT(v1_[:, a_, qlo:256], v2_[:, a_, qlo:256], ptab[:, _poff["gng"] + h:_poff["gng"] + h + 1], v4_[:, a_, qlo:256],
                                ALU.mult, ALU.mult, [bf("t2"), bf("t4"), bf("ptab")], [bf("t1_0")])
                        TT(oT[:, 4 + 2 * pr:6 + 2 * pr, qlo:256], v1_[:, :, qlo:256], sg[:, 2 * pr:2 * pr + 2, qlo:256], ALU.mult,
                           [bf("t1_0"), bf(f"sg{pr}")], [bf(f"oT{4 + 2 * pr}"), bf(f"oT{5 + 2 * pr}")])
                    stages += [pa, pb_, pc]

            def filler(n=1):
                for _ in range(n):
                    if stages:
                        stages.pop(0)()

            if own:
                for u in range(2):
                    wt, bw = unit_in(256 * u)
                    filler(2)
                    bank, bb = proj_fm(wt, bw, [0, 128], lo=qlo)
                    for half in range(2):
                        rws = slice(64 * half, 64 * half + 64)
                        ACT(QTc[rws, 2 * u:2 * u + 2, half, qlo:256], v2(bank[:])[rws, :, qlo:256], AF.Copy, [bb], [bf("QT")], scale=0.125)
            for u in range(2):
                hh = [h for h in (2 * u, 2 * u + 1) if h in hs_kv]
                filler(2)
                if not hh:
                    continue
                wt, bw = unit_in(512 + 256 * u)
                bank, bb = proj_fm(wt, bw, [128 * (h - 2 * u) for h in hh])
                for ci, h in enumerate(hh):
                    s0 = (2 * T) % RING[h]
                    wr = [bf(f"KT{h}_{s0}"), bf(f"KT{h}_{s0 + 1}")]
                    if ci == 0:
                        ACT(KT[h][:, s0 * 128:s0 * 128 + 256], bank[:, 0:256], AF.Copy, [bb], wr)
                    else:
                        CP(KT[h][:, s0 * 128:s0 * 128 + 256], bank[:, 256:512], [bb], wr)
            ck("dk")
            for u in range(2):
                hh = [h for h in (2 * u, 2 * u + 1) if h in hs_kv]
                if not hh:
                    filler(2)
                    continue
                wt, bw = unit_in(1024 + 256 * u)
                for tt in range(2):
                    filler(2)
                    bank, bb = proj_tm(wt, bw, tt)
                    for h in hh:
                        slot = (2 * T + tt) % RING[h]
                        c = 128 * (h - 2 * u)
                        CP(VV[h][:, slot, :], bank[:, c:c + 128], [bb], [bf(f"VV{h}_{slot}")])
            filler(100)
            ck("ret")
            if not own:
                return
            ck("retpost")
            in_att[0] = True
            for h in range(4):
                new_st = attention(T, h, qlo)
                while att_stages:
                    att_stages.pop(0)()
                att_stages.extend(new_st)
            ck("attn")
            ensure_loaded(min(2 * T + 4, 2 * last_t))
            tts = [1] if mode == "halo" else [0, 1]
            ounits = [load_unit(wout_s[cb], 8, "wout") for cb in range(4)]
            obanks = [(O2[:, 0, :], bf("O0")), (O2[:, 1, :], bf("O1")), (S2[:, 0, :], bf("S0")), (S2[:, 1, :], bf("S1"))]
            korder = [4, 5, 6, 7, 0, 1, 2]
            oldo = bool(os.environ.get("KOLDO"))
            if oldo:
                korder = [4, 5, 6, 7, 0, 1, 2, 3]
            for cb in range(4):
                wt, bw = ounits[cb]
                bank, bb = obanks[cb]
                firstmm = True
                for _ in range(2):
                    if att_stages:
                        att_stages.pop(0)()
                for tt in tts:
                    for kc in korder:
                        mm(bank[:, 256 * tt:256 * tt + 256], oT[:, kc, 128 * tt:128 * tt + 128], wt[:, kc, :], firstmm, oldo and kc == 3,
                           [bf(f"oT{kc}"), bw], [bb], inc=(oldo and kc == 3), skip_group_check=True)
                        firstmm = False
            while att_stages:
                att_stages.pop(0)()
            in_att[0] = False
            for cb in range(4):
                wt, bw = ounits[cb]
                bank, bb = obanks[cb]
                for tt in tts:
                    if oldo:
                        continue
                    mm(bank[:, 256 * tt:256 * tt + 256], oT[:, 3, 128 * tt:128 * tt + 128], wt[:, 3, :], False, True,
                       [bf("oT3"), bw], [bb], inc=(tt == tts[-1]), skip_group_check=True)
                for tt in tts:
                    i = 2 * T + tt
                    xs = xb[i % 4][:, 256 * cb:256 * cb + 256]
                    TT(xs, xs, bank[:, 256 * tt:256 * tt + 256], ALU.add, [bb], [bf(f"xb{i % 4}")])
            ck("oproj")
            for tt in tts:
                norm_T(2 * T + tt, tt, "g2")
            chain = []
            for j in range(11):
                wa, bwa = load_unit(wup_s[j], 8, "wup")
                if full:
                    wb, bwb = load_unit(wup_s[11 + j], 8, "wup")
                for sub in range(2):
                    fc = 2 * j + sub
                    bank, bb = galloc()
                    for kc in range(8):
                        mm(bank[:, qlo:256], wa[:, kc, 128 * sub:128 * sub + 128], hT[:, kc, qlo:256], kc == 0, kc == 7, [bf("hT"), bwa], [bb],
                           inc=(kc == 7 and not full))
                    if not full:
                        ACT(carry[:, fc, :], bank[:, 254:256], AF.Copy, [bb], [bf(f"carry{fc}")])
                        continue
                    for kc in range(8):
                        mm(bank[:, 256:512], wb[:, kc, 128 * sub:128 * sub + 128], hT[:, kc, :], kc == 0, kc == 7, [bf("hT"), bwb], [bb], inc=(kc == 7))
                    ae, bae = aext[fc % 2], bf(f"aext{fc % 2}")
                    c_, bc_ = cc[fc % 3], bf(f"cc{fc % 3}")
                    bcar = bf(f"carry{fc}")
                    cwc = [ptab[:, _poff["cw"] + k * 22 + fc:_poff["cw"] + k * 22 + fc + 1] for k in range(3)]
                    ACT(ae[:, 0:2], carry[:, fc, :], AF.Copy, [bcar], [bae])
                    ACT(ae[:, 2:258], bank[:, 0:256], AF.Copy, [bb], [bae])
                    ACT(c_[:], ae[:, 0:256], AF.Identity, [bae, bf("ptab")], [bc_], scale=cwc[0], bias=ptab[:, _poff["cb"] + fc:_poff["cb"] + fc + 1])
                    ACT(carry[:, fc, :], ae[:, 256:258], AF.Copy, [bae], [bcar])
                    if chain:
                        chain[-1][0]()
                    STT(c_[:], ae[:, 1:257], cwc[1], c_[:], ALU.mult, ALU.add, [bae, bc_], [bc_])
                    STT(c_[:], ae[:, 2:258], cwc[2], c_[:], ALU.mult, ALU.add, [bae, bc_], [bc_])
                    if chain:
                        chain[-1][1]()
                        chain.pop()

                    def _gelu(c_=c_, bc_=bc_):
                        ACT(c_[:], c_[:], AF.Gelu_apprx_tanh, [bc_], [bc_])

                    def _mult(c_=c_, bc_=bc_, bank=bank, bb=bb, fc=fc):
                        TT(gT[:, fc, :], c_[:], bank[:, 256:512], ALU.mult, [bc_, bb], [bf("gT")])
                    chain.append((_gelu, _mult))
            while chain:
                chain[-1][0]()
                chain[-1][1]()
                chain.pop()
            ck("ffn_up")
            if not full:
                return
            for cb in range(4):
                units = []
                for kg in range(3):
                    nk = 8 if kg < 2 else 6
                    units.append(load_unit(wdn_s[cb, kg, :, 0:nk, :], nk, "wdn"))
                bank, bb = galloc()
                for tt in range(2):
                    for fc in range(NFC):
                        wt, bw = units[fc // 8]
                        mm(bank[:, 256 * tt:256 * tt + 256], gT[:, fc, 128 * tt:128 * tt + 128], wt[:, fc % 8, :], fc == 0, fc == NFC - 1,
                           [bf("gT"), bw], [bb], inc=(fc == NFC - 1))
                if cb == 0 and T + 1 < last_t:
                    for tt in range(2):
                        norm_T(2 * (T + 1) + tt, tt, "g1")
                    prenorm.add(T + 1)
                for tt in range(2):
                    i = 2 * T + tt
                    xs = xb[i % 4][:, 256 * cb:256 * cb + 256]
                    TT(xs, xs, bank[:, 256 * tt:256 * tt + 256], ALU.add, [bb], [bf(f"xb{i % 4}")])
            ck("ffn_dn")
            for tt in range(2):
                i = 2 * T + tt
                xt, bx = xb[i % 4], bf(f"xb{i % 4}")
                sc = ssq[:, (i % 2) * 4 + 2:(i % 2) * 4 + 3]
                rs = ssq[:, (i % 2) * 4 + 3:(i % 2) * 4 + 4]
                bs = bf(f"ssqf{i % 2}")
                ACT(xn[i % 2][:], xt[:], AF.Square, [bx], [bf(f"xn{i % 2}"), bs], accum_out=sc)
                ACT(rs, sc, AF.Ln, [bs, bf("epsc")], [bs], scale=1.0 / D, bias=epsc[:])
                ACT(rs, rs, AF.Exp, [bs], [bs], scale=-0.5)
                STT(xt[:], xt[:], rs, ptab[:, _poff["gf"]:_poff["gf"] + D], ALU.mult, ALU.mult, [bx, bs, bf("ptab")], [bx])
                r0 = (T - h0) * 256 + tt * 128
                out_toks.append(S.dma("pool", f"out{i % 4}", yout[r0:r0 + 128, :], xt[:], reads=[bx]))

        out_toks = []
        nload = [0]

        def ensure_loaded(upto):
            while nload[0] < upto:
                load_x(nload[0])
                nload[0] += 1

        try:
            ck("init")
            for T in range(0, last_t):
                _CUR[0] = T
                ensure_loaded(2 * T + 2)
                if T < first_t:
                    hist_mode[0] = True
                    tile(T, "hist")
                    hist_mode[0] = False
                    if T == first_t - 1:
                        ucache.clear()
                    ck("hist")
                elif T < h0:
                    tile(T, "halo")
                    ck("halo")
                else:
                    tile(T, "own")
                    ck("own")
        except _Stop:
            pass
        for e_ in ("pe", "act", "dve"):
            pass
        for tok in out_toks[-4:]:
            S.wait_tok("pool", tok)
        S.emit()
    return nc


_NC_CACHE = {}


def kernel(**inp):
    inp = {k: np.asarray(v) for k, v in inp.items()}
    x = inp["x"].astype(np.float32)
    if "nc" not in _NC_CACHE:
        _NC_CACHE["nc"] = build_nc()
    nc = _NC_CACHE["nc"]
    ptab = _ptab(inp)
    ctabs = [_ctab(0), _ctab(1)]
    in_maps = []
    for c in range(8):
        b, hf = c // 2, c % 2
        if hf == 1:
            xl = np.ascontiguousarray(x[b])
        else:
            xl = np.concatenate([np.zeros((4096, D), np.float32), x[b, :4096]], axis=0)
        in_maps.append({"x": xl, "ctab": ctabs[hf], "ptab": ptab,
                        "w_in": np.ascontiguousarray(inp["w_in"][0]), "w_out": np.ascontiguousarray(inp["w_out"][0]),
                        "w_up": np.ascontiguousarray(inp["w_up"][0]), "w_down": np.ascontiguousarray(inp["w_down"][0])})
    res = run_bass_kernel_spmd(nc, in_maps, core_ids=list(range(8)))
    out = np.zeros((4, 8192, D), np.float32)
    for c in range(8):
        b, hf = c // 2, c % 2
        out[b, hf * 4096:(hf + 1) * 4096] = res.results[c]["y"]
    return out
```

```python
import os
import numpy as np
from contextlib import ExitStack
import concourse.bass as bass
import concourse.mybir as mybir
from concourse.bass_utils import run_bass_kernel_spmd

F32 = mybir.dt.float32
BF16 = mybir.dt.bfloat16
AF = mybir.ActivationFunctionType
ALU = mybir.AluOpType

D = 1024
NH = 4
FF = 2816
NFC = 22
SLOPES = [2.0 ** -2, 2.0 ** -4, 2.0 ** -6, 2.0 ** -8]
GAM = [1.0 - 2.0 ** (-5 - h) for h in range(4)]
THR = 120.0
NEG = -30000.0
EPS = 1e-6
LAM_INIT = 0.2
RING = [8, 18, 64, 64]
NW = 8
FIRST_T = 15
LAST_T = 32

_off = {}
_cur = 0
for _n, _w in (("ident", 128), ("ones", 128), ("avg", 128), ("sel0", 128), ("sel1", 128), ("B", 512),
               ("Dt", 896), ("DM", 512), ("Ddec", 256), ("Dq", 256), ("g128", 2), ("e0", 128), ("e32", 128)):
    _off[_n] = _cur
    _cur += _w
NCT = _cur
_poff = {}
_cur = 0
for _n, _w in (("g1", 8), ("g2", 8), ("gf", 1024), ("subg", 1), ("gng", 4), ("cw", 66), ("cb", 22), ("lamv", 256)):
    _poff[_n] = _cur
    _cur += _w
NPT = _cur
DIDX = {(0, 0): 0, (1, 0): 1, (1, 1): 2, (2, 0): 3, (2, 1): 4, (3, 0): 5, (3, 1): 6}


def _ctab(hf):
    p = np.arange(128, dtype=np.float64)
    t = np.zeros((128, NCT), np.float64)
    t[:, _off["ident"]:_off["ident"] + 128] = np.eye(128)
    t[:, _off["ones"]:_off["ones"] + 128] = 1.0
    t[:, _off["avg"]:_off["avg"] + 128] = 1.0 / 128
    t[0, _off["e0"]:_off["e0"] + 128] = 1.0
    t[32, _off["e32"]:_off["e32"] + 128] = 1.0
    t[:, _off["sel0"] + 0] = 1.0
    t[:, _off["sel1"] + 32] = 1.0
    for h in range(4):
        m = SLOPES[h]
        for st in range(2):
            for dj in range(64):
                v = m * (p - 128.0 * dj)
                if st == 1 and hf == 0:
                    v = np.full(128, NEG)
                t[:, _off["B"] + (h * 2 + st) * 64 + dj] = v
    k = p[:, None]
    q = p[None, :]
    allowed = (k // 64) <= (q // 64)
    for (h, j), di in DIDX.items():
        m = SLOPES[h]
        v = np.where(allowed, -m * np.abs(q - k) + m * (128.0 * j + q), NEG)
        t[:, _off["Dt"] + di * 128:_off["Dt"] + di * 128 + 128] = v
    for h in range(4):
        v = np.where(allowed, GAM[h] ** np.abs(q - k) / 8.0, 0.0)
        t[:, _off["DM"] + h * 128:_off["DM"] + h * 128 + 128] = v
        t[:, _off["Ddec"] + h * 64:_off["Ddec"] + h * 64 + 64] = (GAM[h] ** (127.0 - p))[:, None]
    for pair in range(2):
        for half in range(2):
            h = 2 * pair + half
            rows = slice(64 * half, 64 * half + 64)
            t[rows, _off["Dq"] + pair * 128:_off["Dq"] + pair * 128 + 128] = (GAM[h] ** (p + 1.0) / 8.0)[None, :]
            t[rows, _off["g128"] + pair] = GAM[h] ** 128.0
    return t.astype(np.float32)


def _ptab(inp):
    t = np.zeros((128, NPT), np.float32)
    t[:, _poff["g1"]:_poff["g1"] + 8] = inp["norm_mix_g"][0].reshape(8, 128).T
    t[:, _poff["g2"]:_poff["g2"] + 8] = inp["norm_ffn_g"][0].reshape(8, 128).T
    t[:, _poff["gf"]:_poff["gf"] + 1024] = np.broadcast_to(inp["final_norm_g"][None, :], (128, 1024))
    t[:, _poff["subg"]] = inp["diff_subln_g"][0]
    t[:, _poff["gng"]:_poff["gng"] + 4] = inp["ret_gn_g"][0].reshape(4, 128).T
    t[:, _poff["cw"]:_poff["cw"] + 66] = inp["conv_w"][0].reshape(3, 22, 128).transpose(2, 0, 1).reshape(128, 66)
    t[:, _poff["cb"]:_poff["cb"] + 22] = inp["conv_b"][0].reshape(22, 128).T
    lv = np.concatenate([inp["lambda_q1"][0], inp["lambda_k1"][0], inp["lambda_q2"][0], inp["lambda_k2"][0]])
    t[:, _poff["lamv"]:_poff["lamv"] + 256] = np.broadcast_to(lv[None, :], (128, 256))
    return t


class Tok:
    __slots__ = ("sem", "val", "eng")

    def __init__(self, sem, val, eng):
        self.sem, self.val, self.eng = sem, val, eng


class Buf:
    def __init__(self, name=""):
        self.name = name
        self.w = None
        self.rs = []
        self.excl = name in ("S0", "S1", "O0", "O1", "Zb", "TR", "G0", "G1", "SA", "SB")
        self.group = []
        self.alias = []


class Sched:
    ENGS = ("pe", "act", "dve", "pool", "sp")

    def __init__(self, nc, es):
        self.nc = nc
        self.es = es
        self.prog = {e: [] for e in self.ENGS}
        self.sem = {e: es.enter_context(nc.semaphore("ps_" + e)) for e in self.ENGS}
        self.count = {e: 0 for e in self.ENGS}
        self.waited = {e: {} for e in self.ENGS}
        self.dsems = {}
        self.dcount = {}

    def _waits(self, eng, reads, writes):
        need = []
        for b in reads:
            if b.w is not None:
                need.append((b.w, True))
            for ob in b.alias:
                if ob.w is not None:
                    need.append((ob.w, True))
            if b.excl:
                for t in b.rs:
                    need.append((t, False))
                for ob in b.group + b.alias:
                    for t in ob.rs:
                        need.append((t, False))
        for b in writes:
            if b.w is not None:
                need.append((b.w, b.excl))
            for t in b.rs:
                need.append((t, b.excl))
            for ob in b.alias:
                if ob.w is not None:
                    need.append((ob.w, False))
                for t in ob.rs:
                    need.append((t, False))
        best = {}
        for t, raw in need:
            if t.eng == eng and (eng == "pe" or not raw):
                continue
            k = id(t.sem)
            if self.waited[eng].get(k, 0) >= t.val:
                continue
            if k not in best or best[k].val < t.val:
                best[k] = t
        out = []
        for k, t in best.items():
            self.waited[eng][k] = t.val
            out.append(t)
            if os.environ.get("KLOG"):
                print("LOG", eng, "wait", t.eng, t.val)
        return out

    def _emit_waits(self, eng, toks):
        for t in toks:
            self.prog[eng].append(lambda e, sem=t.sem, val=t.val: e.wait_ge(sem, val))

    def _commit(self, tok, reads, writes):
        for b in reads:
            b.rs = [t for t in b.rs if t.sem is not tok.sem]
            b.rs.append(tok)
        for b in writes:
            b.w = tok
            b.rs = []

    def op(self, eng, fn, reads=(), writes=(), inc=True):
        toks = self._waits(eng, reads, writes)
        emb = None
        if toks and not os.environ.get("KNOEMB"):
            emb = toks[-1]
            toks = toks[:-1]
        self._emit_waits(eng, toks)
        if emb is not None:
            fn0, esem, eval_ = fn, emb.sem, emb.val
            fn = lambda e, fn0=fn0, esem=esem, eval_=eval_: fn0(e)._wait_ge(esem, eval_)
        sem = self.sem[eng]
        tok = Tok(sem, self.count[eng] + 1, eng)
        if os.environ.get("KLOG"):
            print("LOG", eng, "op", inc, self.count[eng] + 1, [b.name for b in reads], [b.name for b in writes])
        if inc:
            self.count[eng] += 1
            self.prog[eng].append(lambda e, fn=fn, sem=sem: fn(e).then_inc(sem, 1))
        else:
            self.prog[eng].append(lambda e, fn=fn: fn(e))
        self._commit(tok, reads, writes)
        return tok

    def dma(self, eng, dname, out, in_, reads=(), writes=()):
        self._emit_waits(eng, self._waits(eng, reads, writes))
        if dname not in self.dsems:
            self.dsems[dname] = self.es.enter_context(self.nc.semaphore("d_" + dname))
            self.dcount[dname] = 0
        sem = self.dsems[dname]
        self.dcount[dname] += 16
        tok = Tok(sem, self.dcount[dname], "dma")
        self.prog[eng].append(lambda e, out=out, in_=in_, sem=sem: e.dma_start(out=out, in_=in_).then_inc(sem, 16))
        self._commit(tok, reads, writes)
        return tok

    def wait_tok(self, eng, tok):
        k = id(tok.sem)
        if self.waited[eng].get(k, 0) >= tok.val:
            return
        self.waited[eng][k] = tok.val
        self.prog[eng].append(lambda e, sem=tok.sem, val=tok.val: e.wait_ge(sem, val))

    def emit(self):
        with self.nc.Block() as block:
            @block.sync
            def _(e):
                for f in self.prog["sp"]:
                    f(e)

            @block.tensor
            def _(e):
                for f in self.prog["pe"]:
                    f(e)

            @block.scalar
            def _(e):
                for f in self.prog["act"]:
                    f(e)

            @block.vector
            def _(e):
                for f in self.prog["dve"]:
                    f(e)

            @block.gpsimd
            def _(e):
                for f in self.prog["pool"]:
                    f(e)


import os


class _Stop(Exception):
    pass


_CUR = [0]


def ck(name):
    ks = os.environ.get("KSTOP")
    if ks == name or ks == f"{name}@{_CUR[0]}":
        raise _Stop()


def build_nc(h0=16, nown=16):
    first_t, last_t = h0 - 1, h0 + nown
    nc = bass.Bass("TRN2", target_bir_lowering=False)
    xin = nc.dram_tensor("x", [256 * (h0 + nown), D], F32, kind="ExternalInput").ap()
    ctab_d = nc.dram_tensor("ctab", [128, NCT], F32, kind="ExternalInput").ap()
    ptab_d = nc.dram_tensor("ptab", [128, NPT], F32, kind="ExternalInput").ap()
    w_in_d = nc.dram_tensor("w_in", [D, 3072], F32, kind="ExternalInput").ap()
    w_out_d = nc.dram_tensor("w_out", [D, D], F32, kind="ExternalInput").ap()
    w_up_d = nc.dram_tensor("w_up", [D, 2 * FF], F32, kind="ExternalInput").ap()
    w_dn_d = nc.dram_tensor("w_down", [FF, D], F32, kind="ExternalInput").ap()
    yout = nc.dram_tensor("y", [256 * nown, D], F32, kind="ExternalOutput").ap()
    win_s = nc.dram_tensor("win_s", [12, 128, 8, 256], BF16, kind="Internal").ap()
    wout_s = nc.dram_tensor("wout_s", [4, 128, 8, 256], BF16, kind="Internal").ap()
    wup_s = nc.dram_tensor("wup_s", [22, 128, 8, 256], BF16, kind="Internal").ap()
    wdn_s = nc.dram_tensor("wdn_s", [4, 3, 128, 8, 256], BF16, kind="Internal").ap()

    es = ExitStack()
    with es:
        S = Sched(nc, es)
        A = nc.alloc_sbuf_tensor

        def PS(name, shape, dt=F32):
            return es.enter_context(nc.psum_tensor(name, shape, dt))

        ctab = A("ctab_t", [128, NCT], F32)
        ptab = A("ptab_t", [128, NPT], F32)
        idb = A("idb", [128, 128], BF16)
        selb = A("selb", [128, 256], BF16)
        epsc = A("epsc", [128, 1], F32)
        lamc = A("lamc", [128, 4], F32)
        lamt = A("lamt", [128, 128], F32)
        gcol = A("gcol", [128, 1], F32)
        KT = [A(f"KT{h}", [128, RING[h] * 128], BF16) for h in range(4)]
        VV = [A(f"VV{h}", [128, RING[h], 128], BF16) for h in range(4)]
        wring = [A(f"wr{i}", [128, 8, 256], BF16) for i in range(NW)]
        xb = [A(f"xb{i}", [128, D], F32) for i in range(4)]
        xn = [A(f"xn{i}", [128, D], BF16) for i in range(2)]
        ssq = A("ssq", [128, 8], F32)
        hT = A("hT", [128, 8, 256], BF16)
        QTc = A("QTc", [128, 4, 2, 256], BF16)
        rqT = A("rqT", [128, 2, 256], BF16)
        rqdT = A("rqdT", [128, 2, 256], BF16)
        rkT = A("rkT", [128, 2, 256], BF16)
        kd = [A(f"kd{i}", [128, 256], BF16) for i in range(2)]
        rvt = [A(f"rvt{i}", [128, 512], BF16) for i in range(2)]
        St = A("St", [128, 2, 128], F32)
        Stb = A("Stb", [128, 2, 128], BF16)
        sg = A("sg", [128, 4, 256], F32)
        oT = A("oT", [128, 8, 256], BF16)
        PT = [A(f"PT{i}", [128, 2, 256], BF16) for i in range(3)]
        dtmp = [A("dtmp0", [128, 2, 128], F32)] * 2
        zaccs = [A("zacc0", [128, 512], F32), A("zacc1", [128, 512], F32)]
        epsz = A("epsz", [128, 1], F32)
        t1 = A("t1", [128, 512], F32)
        t1s = [t1, A("t1b", [128, 512], F32)]
        t2 = A("t2", [128, 512], F32)
        t3 = A("t3", [128, 512], F32)
        t4 = A("t4", [128, 512], F32)
        AT = [A(f"AT{i}", [128, 128], BF16) for i in range(4)]
        gT = A("gT", [128, NFC, 256], BF16)
        aext = [A(f"aext{i}", [128, 258], F32) for i in range(2)]
        cc = [A(f"cc{i}", [128, 256], F32) for i in range(3)]
        carry = A("carry", [128, NFC, 2], F32)

        S2 = PS("S2", [128, 2, 512])
        O2 = PS("O2", [128, 2, 512])
        Zb = PS("Zb", [128, 512])
        TRf = PS("TR", [128, 512], F32)
        TR = TRf[:].bitcast(BF16)
        G = [PS("G0", [128, 512]), PS("G1", [128, 512])]

        B = {}

        def bf(name):
            if name not in B:
                B[name] = Buf(name)
            return B[name]

        in_att = [False]
        gctr = [0]

        def galloc():
            if in_att[0]:
                i = gctr[0] % 2
                gctr[0] += 1
                return G[i], bf(f"G{i}")
            i = gctr[0] % 4
            gctr[0] += 1
            if i < 2:
                return G[i], bf(f"G{i}")
            return S2[:, i - 2, :], bf(f"S{i - 2}")

        def ct(name, lo=0, n=None):
            o = _off[name] + lo
            return ctab[:, o:o + (n if n is not None else 1)]

        def pt(name, lo=0, n=None):
            o = _poff[name] + lo
            return ptab[:, o:o + (n if n is not None else 1)]

        tc = S.dma("sp", "c0", ctab[:], ctab_d, writes=[bf("ctab")])
        tp = S.dma("sp", "c1", ptab[:], ptab_d, writes=[bf("ptab")])
        for wsrc, wdst, nm in ((w_in_d, win_s, "win"), (w_out_d, wout_s, "wout"), (w_up_d, wup_s, "wup")):
            tok = None
            for kc in range(8):
                tok = S.dma("pool", "cast_" + nm, wdst[:, :, kc, :], wsrc[128 * kc:128 * kc + 128, :].rearrange("p (u n) -> u p n", n=256))
            bf(nm).w = tok
        tok = None
        for rb in range(NFC):
            tok = S.dma("pool", "cast_wdn", wdn_s[:, rb // 8, :, rb % 8, :], w_dn_d[128 * rb:128 * rb + 128, :].rearrange("p (u n) -> u p n", n=256))
        bf("wdn").w = tok
        S.op("dve", lambda e: e.tensor_copy(out=idb[:], in_=ct("ident", 0, 128)), reads=[bf("ctab")], writes=[bf("idb")])
        S.op("dve", lambda e: e.tensor_copy(out=selb[:], in_=ct("sel0", 0, 256)), reads=[bf("ctab")], writes=[bf("selb")])
        S.op("dve", lambda e: e.memset(epsc[:], EPS), writes=[bf("epsc")])
        S.op("dve", lambda e: e.memset(St[:], 0.0), writes=[bf("St")])
        S.op("dve", lambda e: e.memset(Stb[:], 0.0), writes=[bf("Stb")])
        S.op("dve", lambda e: e.memset(carry[:], 0.0), writes=[bf("carry")])
        S.op("dve", lambda e: e.memset(oT[:], 0.0), writes=[bf(f"oT{k_}") for k_ in range(8)])
        S.op("dve", lambda e: e.memset(QTc[:], 0.0), writes=[bf("QT")])
        S.op("dve", lambda e: e.memset(epsz[:], 1e-18), writes=[bf("epsz")])
        S.op("dve", lambda e: e.tensor_tensor(out=lamt[:, 0:64], in0=pt("lamv", 0, 64), in1=pt("lamv", 64, 64), op=ALU.mult),
             reads=[bf("ptab")], writes=[bf("lamt")])
        S.op("dve", lambda e: e.tensor_tensor(out=lamt[:, 64:128], in0=pt("lamv", 128, 64), in1=pt("lamv", 192, 64), op=ALU.mult),
             reads=[bf("ptab")], writes=[bf("lamt")])
        S.op("dve", lambda e: e.reduce_sum(out=lamc[:, 2:3], in_=lamt[:, 0:64], axis=mybir.AxisListType.X), reads=[bf("lamt")], writes=[bf("lamc")])
        S.op("dve", lambda e: e.reduce_sum(out=lamc[:, 3:4], in_=lamt[:, 64:128], axis=mybir.AxisListType.X), reads=[bf("lamt")], writes=[bf("lamc")])
        S.op("act", lambda e: e.activation(out=lamc[:, 2:4], in_=lamc[:, 2:4], func=AF.Exp), reads=[bf("lamc")], writes=[bf("lamc")])
        S.op("dve", lambda e: e.tensor_tensor(out=lamc[:, 0:1], in0=lamc[:, 2:3], in1=lamc[:, 3:4], op=ALU.subtract), reads=[bf("lamc")], writes=[bf("lamc")])
        S.op("dve", lambda e: e.tensor_scalar(out=lamc[:, 1:2], in0=lamc[:, 0:1], scalar1=-1.0, scalar2=-LAM_INIT, op0=ALU.mult, op1=ALU.add),
             reads=[bf("lamc")], writes=[bf("lamc2")])
        S.op("dve", lambda e: e.tensor_scalar(out=gcol[:], in0=pt("subg"), scalar1=1.0 - LAM_INIT, scalar2=None, op0=ALU.mult),
             reads=[bf("ptab")], writes=[bf("gcol")])

        wctr = [0]

        def load_unit(src, nk, srcbuf):
            i = wctr[0] % NW
            wctr[0] += 1
            S.dma("sp", f"w{i}", wring[i][:, 0:nk, :], src, reads=[bf(srcbuf)], writes=[bf(f"wr{i}")])
            return wring[i], bf(f"wr{i}")

        ucache = {}
        hist_mode = [False]

        def unit_in(c0):
            if hist_mode[0] and c0 in ucache:
                return ucache[c0]
            r = load_unit(win_s[c0 // 256], 8, "win")
            if hist_mode[0]:
                ucache[c0] = r
            return r

        def load_x(i):
            S.dma("sp", f"x{i % 4}", xb[i % 4][:], xin[i * 128:(i + 1) * 128, :], writes=[bf(f"xb{i % 4}")])

        def mm(out, lhsT, rhs, start, stop, reads, writes, inc, **kw):
            S.op("pe", lambda e: e.matmul(out=out, lhsT=lhsT, rhs=rhs, start=start, stop=stop, **kw), reads=reads, writes=writes, inc=inc)


        def ACT(out, in_, func, reads, writes, **kw):
            return S.op("act", lambda e: e.activation(out=out, in_=in_, func=func, **kw), reads=reads, writes=writes)

        def TT(out, in0, in1, op, reads, writes, eng="dve"):
            S.op(eng, lambda e: e.tensor_tensor(out=out, in0=in0, in1=in1, op=op), reads=reads, writes=writes)

        def TS(out, in0, s1, s2, op0, op1, reads, writes):
            if op1 is None:
                S.op("dve", lambda e: e.tensor_scalar(out=out, in0=in0, scalar1=s1, scalar2=None, op0=op0), reads=reads, writes=writes)
            else:
                S.op("dve", lambda e: e.tensor_scalar(out=out, in0=in0, scalar1=s1, scalar2=s2, op0=op0, op1=op1), reads=reads, writes=writes)

        def STT(out, in0, scalar, in1, op0, op1, reads, writes):
            S.op("dve", lambda e: e.scalar_tensor_tensor(out=out, in0=in0, scalar=scalar, in1=in1, op0=op0, op1=op1), reads=reads, writes=writes)

        def CP(out, in_, reads, writes, eng="dve"):
            if os.environ.get("KCP") == "copy":
                S.op(eng, lambda e: e.tensor_copy(out=out, in_=in_), reads=reads, writes=writes)
            else:
                S.op(eng, lambda e: e.tensor_scalar(out=out, in0=in_, scalar1=1.0, scalar2=None, op0=ALU.mult), reads=reads, writes=writes)

        def TRN(out, in_, reads, writes, inc):
            S.op("pe", lambda e: e.transpose(out=out, in_=in_, identity=idb[:]), reads=reads, writes=writes, inc=inc)

        def norm_T(i, tt, gname):
            xt, bx = xb[i % 4], bf(f"xb{i % 4}")
            xnn, bxn = xn[i % 2], bf(f"xn{i % 2}")
            sc = ssq[:, (i % 2) * 4:(i % 2) * 4 + 1]
            rs = ssq[:, (i % 2) * 4 + 1:(i % 2) * 4 + 2]
            bs = bf(f"ssq{i % 2}")
            ACT(xnn[:], xt[:], AF.Square, [bx], [bxn, bs], accum_out=sc)
            ACT(rs, sc, AF.Ln, [bs, bf("epsc")], [bs], scale=1.0 / D, bias=epsc[:])
            ACT(rs, rs, AF.Exp, [bs], [bs], scale=-0.5)
            ACT(xnn[:], xt[:], AF.Copy, [bx, bs], [bxn], scale=rs)
            for c in range(8):
                TRN(TR[:, c * 128:(c + 1) * 128], xnn[:, c * 128:(c + 1) * 128], [bxn, bf("idb")], [bf("TR")], inc=(c == 7))
            for c in range(8):
                TS(hT[:, c, tt * 128:(tt + 1) * 128], TR[:, c * 128:(c + 1) * 128], pt(gname, c, 1), None, ALU.mult, None,
                   [bf("TR"), bf("ptab")], [bf("hT")])

        def v2(ap):
            return ap.rearrange("p (a b) -> p a b", a=2)

        def proj_fm(wt, bw, cols, lo=0):
            bank, bb = galloc()
            n = len(cols)
            for ci, c0 in enumerate(cols):
                for kc in range(8):
                    mm(bank[:, ci * 256 + lo:ci * 256 + 256], wt[:, kc, c0:c0 + 128], hT[:, kc, lo:256], kc == 0, kc == 7,
                       [bf("hT"), bw], [bb], inc=(kc == 7 and ci == n - 1))
            return bank, bb

        def proj_tm(wt, bw, tt):
            bank, bb = galloc()
            for kc in range(8):
                mm(bank[:, 0:256], hT[:, kc, tt * 128:(tt + 1) * 128], wt[:, kc, :], kc == 0, kc == 7,
                   [bf("hT"), bw], [bb], inc=(kc == 7))
            return bank, bb

        def kt_lo(h, T):
            lo = 0
            Q0 = 256 * T
            for kt in range(2 * T + 2):
                if (Q0 - 128 * kt - 127) * SLOPES[h] >= THR:
                    lo = kt + 1
            return lo

        def need_kv(h, T):
            for kt in (2 * T, 2 * T + 1):
                for Tq in range(max(T, first_t), last_t):
                    if kt >= kt_lo(h, Tq):
                        return True
            return False

        def segs(h, T, kt, qlo):
            out = []
            if h == 0:
                for qb in (0, 1):
                    lo, hi = max(qlo, 128 * qb), 128 * qb + 128
                    if lo >= hi:
                        continue
                    ktd = 2 * T + qb
                    if kt < ktd:
                        out.append((lo, hi, "past", ktd - kt, 0))
                    elif kt == ktd:
                        out.append((lo, hi, "diag", 0, 128 * qb))
            else:
                if kt < 2 * T:
                    out.append((qlo, 256, "past", 2 * T - kt, 0))
                elif kt == 2 * T:
                    if qlo < 128:
                        out.append((qlo, 128, "diag", 0, 0))
                    out.append((max(qlo, 128), 256, "past", 0, 0))
                else:
                    out.append((max(qlo, 128), 256, "diag", 1, 128))
            return out

        actr = [0]
        att_stages = []
        ones_c = _off["ones"]
        avg_ap = ctab[:, _off["avg"]:_off["avg"] + 128]

        def sbank(pb):
            return S2[:, pb, :] if pb < 2 else TRf[:, :]

        def sbuf_(pb):
            return bf(f"S{pb}") if pb < 2 else bf("TR")

        def attention(T, h, qlo):
            pbO = h % 2
            bO = bf(f"O{pbO}")
            bZ = bf("Zb")
            zoff = 256 * (h % 2)
            kts = [kt for kt in range(kt_lo(h, T), 2 * T + 2) if segs(h, T, kt, qlo)]
            info = []
            for kt in kts:
                sg_ = segs(h, T, kt, qlo)
                clo = min(s[0] for s in sg_)
                pb = actr[0] % 3
                pi = actr[0] % 3
                actr[0] += 1
                info.append((kt, sg_, clo, pb, pi))

            def qk(ki):
                kt, sg_, clo, pb, pi = info[ki]
                slot = kt % RING[h]
                if clo == 0:
                    mm(sbank(pb), KT[h][:, slot * 128:(slot + 1) * 128], QTc[:, h, :, :].rearrange("p a b -> p (a b)"),
                       True, True, [bf(f"KT{h}_{slot}"), bf("QT")], [sbuf_(pb)], inc=True)
                else:
                    for half in range(2):
                        mm(sbank(pb)[:, 256 * half + clo:256 * half + 256], KT[h][:, slot * 128:(slot + 1) * 128], QTc[:, h, half, clo:256],
                           True, True, [bf(f"KT{h}_{slot}"), bf("QT")], [sbuf_(pb)], inc=(half == 1))

            exp_toks = []
            add_toks = []
            pool_toks = []
            qk(0)
            if len(info) > 1:
                qk(1)
            for ki, (kt, sg_, clo, pb, pi) in enumerate(info):
                bS = sbuf_(pb)
                slot = kt % RING[h]
                bV = bf(f"VV{h}_{slot}")
                bP = bf(f"PT{pi}")
                sset = 1 if kt < 2 * h0 else 0
                if ki >= 2:
                    S.wait_tok("act", add_toks[ki - 2])
                for (lo, hi, typ, prm, bs0) in sg_:
                    if typ == "past":
                        col = _off["B"] + (h * 2 + sset) * 64 + prm
                        tk = ACT(PT[pi][:, :, lo:hi], v2(sbank(pb))[:, :, lo:hi], AF.Exp, [bS, bf("ctab")], [bP], bias=ctab[:, col:col + 1])
                    else:
                        di = DIDX[(h, prm)]
                        dsel = (ki + h) % 2
                        dt_, bd = dtmp[dsel], bf("dtmp0")
                        dcol = _off["Dt"] + di * 128
                        for half in range(2):
                            TT(dt_[:, half, lo - bs0:hi - bs0], sbank(pb)[:, 256 * half + lo:256 * half + hi],
                               ctab[:, dcol + lo - bs0:dcol + hi - bs0], ALU.add, [bS, bf("ctab")], [bd])
                        tk = ACT(PT[pi][:, :, lo:hi], dt_[:, :, lo - bs0:hi - bs0], AF.Exp, [bd], [bP])
                exp_toks.append(tk)
                if ki + 2 < len(info):
                    qk(ki + 2)
                if ki % 2 == 1 and att_stages:
                    att_stages.pop(0)()
                first, last = (ki == 0), (ki == len(info) - 1)
                if clo == 0:
                    mm(O2[:, pbO, :], VV[h][:, slot, :], PT[pi][:].rearrange("p a b -> p (a b)"), first, last,
                       [bV, bP], [bO], inc=last, skip_group_check=True)
                else:
                    for half in range(2):
                        mm(O2[:, pbO, 256 * half + clo:256 * half + 256], VV[h][:, slot, :], PT[pi][:, half, clo:256], first and half == 0, last,
                           [bV, bP], [bO], inc=(last and half == 1), skip_group_check=True)
                zv = v2(Zb[:])
                zaccv = v2(zaccs[h % 2][:])
                bzB = bf(f"zaccB{h % 2}")
                S.wait_tok("pool", exp_toks[ki])
                if first:
                    tokp = S.op("pool", lambda e, zaccv=zaccv, pi=pi, clo=clo: e.tensor_copy(out=zaccv[:, 1, clo:256], in_=PT[pi][:, 1, clo:256]),
                                reads=[], writes=[bzB])
                else:
                    tokp = S.op("pool", lambda e, zaccv=zaccv, pi=pi, clo=clo: e.tensor_tensor(out=zaccv[:, 1, clo:256], in0=zaccv[:, 1, clo:256],
                                                                                             in1=PT[pi][:, 1, clo:256], op=ALU.add), reads=[bzB], writes=[bzB])
                if pool_toks:
                    S.wait_tok("dve", pool_toks[-1])
                pool_toks.append(tokp)
                if first:
                    tokd = S.op("dve", lambda e, zv=zv, pi=pi, clo=clo: e.tensor_copy(out=zv[:, 0, clo:256], in_=PT[pi][:, 0, clo:256]), reads=[bP], writes=[bZ])
                else:
                    tokd = S.op("dve", lambda e, zv=zv, pi=pi, clo=clo: e.tensor_tensor(out=zv[:, 0, clo:256], in0=zv[:, 0, clo:256],
                                                                                       in1=PT[pi][:, 0, clo:256], op=ALU.add), reads=[bZ, bP], writes=[bZ])
                add_toks.append(tokd)
            S.wait_tok("act", pool_toks[-1])
            zacc, bza = zaccs[h % 2], bf(f"zacc{h % 2}")
            CP(zacc[:, qlo:256], Zb[:, qlo:256], [bZ], [bza])
            bzB_ = bf(f"zaccB{h % 2}")
            t1h, bt1 = t1s[h % 2], bf(f"t1_{h % 2}")
            t1v = v2(t1h[:])
            CP(t1v[:, :, qlo:256], v2(O2[:, pbO, :])[:, :, qlo:256], [bO], [bt1])
            hold = {}
            rinv = v2(t4[:])
            t2v = v2(t2[:])

            def s1():
                bank, bb = galloc()
                if qlo == 0:
                    mm(bank[:, :], ctab[:, ones_c:ones_c + 128], zacc[:, :], True, True, [bza, bzB_, bf("ctab")], [bb], inc=True)
                else:
                    for half in range(2):
                        mm(bank[:, 256 * half + qlo:256 * half + 256], ctab[:, ones_c:ones_c + 128], zacc[:, 256 * half + qlo:256 * half + 256],
                           True, True, [bza, bzB_, bf("ctab")], [bb], inc=(half == 1))
                ACT(rinv[:, :, qlo:256], v2(bank[:])[:, :, qlo:256], AF.Ln, [bb, bf("epsz")], [bf("t4"), bf("t4b")], bias=epsz[:])

            def s2():
                ACT(rinv[:, :, qlo:256], rinv[:, :, qlo:256], AF.Exp, [bf("t4")], [bf("t4"), bf("t4b")], scale=-1.0)
                TS(rinv[:, 1, qlo:256], rinv[:, 1, qlo:256], lamc[:, 1:2], None, ALU.mult, None, [bf("t4"), bf("lamc2")], [bf("t4"), bf("t4b")])

            def s3():
                TT(t2v[:, :, qlo:256], t1v[:, :, qlo:256], rinv[:, :, qlo:256], ALU.mult, [bt1, bf("t4")], [bf("t2")])
                TT(t3[:, qlo:256], t2v[:, 0, qlo:256], t2v[:, 1, qlo:256], ALU.add, [bf("t2")], [bf("t3")])
                TT(t2[:, qlo:256], t3[:, qlo:256], t3[:, qlo:256], ALU.mult, [bf("t3")], [bf("t2")], eng="pool")

            def s4():
                bank2, bb2 = galloc()
                mm(bank2[:, qlo:256], avg_ap, t2[:, qlo:256], True, True, [bf("t2"), bf("ctab")], [bb2], inc=True)
                ACT(t2[:, 256 + qlo:512], bank2[:, qlo:256], AF.Ln, [bb2, bf("epsc")], [bf("t2")], bias=epsc[:])

            def s5():
                ACT(t2[:, 256 + qlo:512], t2[:, 256 + qlo:512], AF.Exp, [bf("t2")], [bf("t2")], scale=-0.5)
                STT(oT[:, h, qlo:256], t3[:, qlo:256], gcol[:], t2[:, 256 + qlo:512], ALU.mult, ALU.mult,
                    [bf("t3"), bf("t2"), bf("gcol")], [bf(f"oT{h}")])
            return [s1, s2, s3, s4, s5]

        prenorm = set()

        def tile(T, mode):
            own = mode != "hist"
            qlo = 254 if mode == "halo" else 0
            full = mode == "own"
            if T not in prenorm:
                for tt in range(2):
                    norm_T(2 * T + tt, tt, "g1")
            if not own:
                ensure_loaded(min(2 * T + 4, 2 * last_t))
            ck("norm")
            hs_kv = [h for h in range(4) if need_kv(h, T)]
            if own:
                wt, bw = unit_in(1536)
                bank, bb = proj_fm(wt, bw, [0, 128], lo=qlo)
                bv = v2(bank[:])
                ACT(rqT[:, :, qlo:256], bv[:, :, qlo:256], AF.Copy, [bb], [bf("rqT")])
                dqv = v2(ctab[:, _off["Dq"]:_off["Dq"] + 256])
                for tt in range(2):
                    lo = max(qlo, 128 * tt)
                    if lo >= 128 * tt + 128:
                        continue
                    TT(rqdT[:, :, lo:128 * tt + 128], bv[:, :, lo:128 * tt + 128], dqv[:, :, lo - 128 * tt:128], ALU.mult,
                       [bb, bf("ctab")], [bf("rqdT")])
            wt, bw = unit_in(1792)
            for tt in range(2):
                bank, bb = proj_tm(wt, bw, tt)
                TT(kd[tt][:], bank[:, 0:256], ctab[:, _off["Ddec"]:_off["Ddec"] + 256], ALU.mult, [bb, bf("ctab")], [bf(f"kd{tt}")])
            if own:
                bank, bb = proj_fm(wt, bw, [0, 128], lo=0)
                ACT(rkT[:], v2(bank[:]), AF.Copy, [bb], [bf("rkT")])
            ck("rk")
            for u in range(2):
                wt, bw = unit_in(2048 + 256 * u)
                for tt in range(2):
                    bank, bb = proj_tm(wt, bw, tt)
                    ACT(rvt[tt][:, 256 * u:256 * u + 256], bank[:, 0:256], AF.Copy, [bb], [bf(f"rvt{tt}_{u}")])
            if own:
                for u in range(2):
                    wt, bw = unit_in(2560 + 256 * u)
                    bank, bb = proj_fm(wt, bw, [0, 128], lo=qlo)
                    ACT(sg[:, 2 * u:2 * u + 2, qlo:256], v2(bank[:])[:, :, qlo:256], AF.Silu, [bb], [bf(f"sg{u}")])
            stages = []
            rbanks = [(O2[:, 0, :], bf("O0")), (O2[:, 1, :], bf("O1"))] if own else None
            for tt in range(2):
                if own and not (mode == "halo" and tt == 0):
                    lo = max(qlo, 128 * tt) - 128 * tt

                    def st_a(tt=tt, lo=lo):
                        for h in range(4):
                            rows = slice(64 * (h % 2), 64 * (h % 2) + 64)
                            pr = h // 2
                            so = 128 * pr
                            mm(S2[:, h % 2, so + lo:so + 128], rkT[rows, pr, 128 * tt:128 * tt + 128], rqT[rows, pr, 128 * tt + lo:128 * tt + 128],
                               True, True, [bf("rkT"), bf("rqT")], [bf(f"S{h % 2}")], inc=True)

                        st_b(tt, lo)

                    def st_b(tt, lo):
                        for h in range(4):
                            so = 128 * (h // 2)
                            dmc = _off["DM"] + 128 * h
                            TT(AT[h][:, lo:128], S2[:, h % 2, so + lo:so + 128], ctab[:, dmc + lo:dmc + 128], ALU.mult,
                               [bf(f"S{h % 2}"), bf("ctab")], [bf(f"AT{h}")])

                    def st_c(tt=tt, lo=lo):
                        for h in range(4):
                            rows = slice(64 * (h % 2), 64 * (h % 2) + 64)
                            pr = h // 2
                            rb = rbanks[pr][0]
                            rbb = [bf(f"O{pr}")]
                            oo = 256 * (h % 2) + 128 * tt
                            mm(rb[:, oo + lo:oo + 128], rvt[tt][:, 128 * h:128 * h + 128], AT[h][:, lo:128], True, False,
                               [bf(f"AT{h}"), bf(f"rvt{tt}_{h // 2}")], rbb, inc=False)
                            mm(rb[:, oo + lo:oo + 128], Stb[rows, pr, :], rqdT[rows, pr, 128 * tt + lo:128 * tt + 128], False, True,
                               [bf("Stb"), bf("rqdT")], rbb, inc=True)
                    stages += [st_a, st_c]

                def st_d(tt=tt):
                    bank, bb = galloc()
                    kvv = v2(bank[:, 0:256])
                    for h in range(4):
                        mm(kvv[64 * (h % 2):64 * (h % 2) + 64, h // 2, :], kd[tt][:, 64 * h:64 * h + 64], rvt[tt][:, 128 * h:128 * h + 128],
                           True, True, [bf(f"kd{tt}"), bf(f"rvt{tt}_{h // 2}")], [bb], inc=(h == 3))

                    def st_e():
                        for pr in range(2):
                            STT(St[:, pr, :], St[:, pr, :], ctab[:, _off["g128"] + pr:_off["g128"] + pr + 1], kvv[:, pr, :], ALU.mult, ALU.add,
                                [bb, bf("St"), bf("ctab")], [bf("St")])
                        ACT(Stb[:], St[:], AF.Copy, [bf("St")], [bf("Stb")])
                    st_e()
                stages.append(st_d)
            if own:
                v1_, v2_, v3_, v4_ = v2(t1[:]), v2(t2[:]), v2(t3[:]), v2(t4[:])
                for pr in range(2):
                    hold = {}

                    def pa(pr=pr, hold=hold):
                        rv_ = v2(rbanks[pr][0])
                        ACT(v1_[:, :, qlo:256], rv_[:, :, qlo:256], AF.Copy, [bf(f"O{pr}")], [bf("t1_0")])
                        bk, bkb = Zb, bf("Zb")
                        bkv = v2(bk[:])
                        for a_ in range(2):
                            mm(bkv[:, a_, qlo:256], avg_ap, v1_[:, a_, qlo:256], True, True, [bf("t1_0"), bf("ctab")], [bkb], inc=(a_ == 1))
                        hold["bkv"], hold["bkb"] = bkv, bkb

                    def pb_(pr=pr, hold=hold):
                        TT(v2_[:, :, qlo:256], v1_[:, :, qlo:256], hold["bkv"][:, :, qlo:256], ALU.subtract, [bf("t1_0"), hold["bkb"]], [bf("t2")])
                        ACT(v3_[:, :, qlo:256], v2_[:, :, qlo:256], AF.Square, [bf("t2")], [bf("t3")])
                        bk2, bkb2 = rbanks[pr][0], bf(f"O{pr}")
                        bkv2 = v2(bk2)
                        for a_ in range(2):
                            mm(bkv2[:, a_, qlo:256], avg_ap, v3_[:, a_, qlo:256], True, True, [bf("t3"), bf("ctab")], [bkb2], inc=(a_ == 1))
                        hold["bkv2"], hold["bkb2"] = bkv2, bkb2

                    def pc(pr=pr, hold=hold):
                        ACT(v4_[:, :, qlo:256], hold["bkv2"][:, :, qlo:256], AF.Ln, [hold["bkb2"], bf("epsc")], [bf("t4"), bf("t4b")], bias=epsc[:])
                        ACT(v4_[:, :, qlo:256], v4_[:, :, qlo:256], AF.Exp, [bf("t4")], [bf("t4"), bf("t4b")], scale=-0.5)
                        for a_ in range(2):
                            h = 2 * pr + a_
                            STT(v1_[:, a_, qlo:256], v2_[:, a_, qlo:256], ptab[:, _poff["gng"] + h:_poff["gng"] + h + 1], v4_[:, a_, qlo:256],
                                ALU.mult, ALU.mult, [bf("t2"), bf("t4"), bf("ptab")], [bf("t1_0")])
                        TT(oT[:, 4 + 2 * pr:6 + 2 * pr, qlo:256], v1_[:, :, qlo:256], sg[:, 2 * pr:2 * pr + 2, qlo:256], ALU.mult,
                           [bf("t1_0"), bf(f"sg{pr}")], [bf(f"oT{4 + 2 * pr}"), bf(f"oT{5 + 2 * pr}")])
                    stages += [pa, pb_, pc]

            def filler(n=1):
                for _ in range(n):
                    if stages:
                        stages.pop(0)()

            if own:
                for u in range(2):
                    wt, bw = unit_in(256 * u)
                    filler(2)
                    bank, bb = proj_fm(wt, bw, [0, 128], lo=qlo)
                    for half in range(2):
                        rws = slice(64 * half, 64 * half + 64)
                        ACT(QTc[rws, 2 * u:2 * u + 2, half, qlo:256], v2(bank[:])[rws, :, qlo:256], AF.Copy, [bb], [bf("QT")], scale=0.125)
            for u in range(2):
                hh = [h for h in (2 * u, 2 * u + 1) if h in hs_kv]
                filler(2)
                if not hh:
                    continue
                wt, bw = unit_in(512 + 256 * u)
                bank, bb = proj_fm(wt, bw, [128 * (h - 2 * u) for h in hh])
                for ci, h in enumerate(hh):
                    s0 = (2 * T) % RING[h]
                    wr = [bf(f"KT{h}_{s0}"), bf(f"KT{h}_{s0 + 1}")]
                    if ci == 0:
                        ACT(KT[h][:, s0 * 128:s0 * 128 + 256], bank[:, 0:256], AF.Copy, [bb], wr)
                    else:
                        CP(KT[h][:, s0 * 128:s0 * 128 + 256], bank[:, 256:512], [bb], wr)
            ck("dk")
            for u in range(2):
                hh = [h for h in (2 * u, 2 * u + 1) if h in hs_kv]
                if not hh:
                    filler(2)
                    continue
                wt, bw = unit_in(1024 + 256 * u)
                for tt in range(2):
                    filler(2)
                    bank, bb = proj_tm(wt, bw, tt)
                    for h in hh:
                        slot = (2 * T + tt) % RING[h]
                        c = 128 * (h - 2 * u)
                        CP(VV[h][:, slot, :], bank[:, c:c + 128], [bb], [bf(f"VV{h}_{slot}")])
            filler(100)
            ck("ret")
            if not own:
                return
            ck("retpost")
            in_att[0] = True
            for h in range(4):
                new_st = attention(T, h, qlo)
                while att_stages:
                    att_stages.pop(0)()
                att_stages.extend(new_st)
            ck("attn")
            ensure_loaded(min(2 * T + 4, 2 * last_t))
            tts = [1] if mode == "halo" else [0, 1]
            ounits = [load_unit(wout_s[cb], 8, "wout") for cb in range(4)]
            obanks = [(O2[:, 0, :], bf("O0")), (O2[:, 1, :], bf("O1")), (S2[:, 0, :], bf("S0")), (S2[:, 1, :], bf("S1"))]
            korder = [4, 5, 6, 7, 0, 1, 2]
            oldo = bool(os.environ.get("KOLDO"))
            if oldo:
                korder = [4, 5, 6, 7, 0, 1, 2, 3]
            for cb in range(4):
                wt, bw = ounits[cb]
                bank, bb = obanks[cb]
                firstmm = True
                for _ in range(2):
                    if att_stages:
                        att_stages.pop(0)()
                for tt in tts:
                    for kc in korder:
                        mm(bank[:, 256 * tt:256 * tt + 256], oT[:, kc, 128 * tt:128 * tt + 128], wt[:, kc, :], firstmm, oldo and kc == 3,
                           [bf(f"oT{kc}"), bw], [bb], inc=(oldo and kc == 3), skip_group_check=True)
                        firstmm = False
            while att_stages:
                att_stages.pop(0)()
            in_att[0] = False
            for cb in range(4):
                wt, bw = ounits[cb]
                bank, bb = obanks[cb]
                for tt in tts:
                    if oldo:
                        continue
                    mm(bank[:, 256 * tt:256 * tt + 256], oT[:, 3, 128 * tt:128 * tt + 128], wt[:, 3, :], False, True,
                       [bf("oT3"), bw], [bb], inc=(tt == tts[-1]), skip_group_check=True)
                for tt in tts:
                    i = 2 * T + tt
                    xs = xb[i % 4][:, 256 * cb:256 * cb + 256]
                    TT(xs, xs, bank[:, 256 * tt:256 * tt + 256], ALU.add, [bb], [bf(f"xb{i % 4}")])
            ck("oproj")
            for tt in tts:
                norm_T(2 * T + tt, tt, "g2")
            chain = []
            for j in range(11):
                wa, bwa = load_unit(wup_s[j], 8, "wup")
                if full:
                    wb, bwb = load_unit(wup_s[11 + j], 8, "wup")
                for sub in range(2):
                    fc = 2 * j + sub
                    bank, bb = galloc()
                    for kc in range(8):
                        mm(bank[:, qlo:256], wa[:, kc, 128 * sub:128 * sub + 128], hT[:, kc, qlo:256], kc == 0, kc == 7, [bf("hT"), bwa], [bb],
                           inc=(kc == 7 and not full))
                    if not full:
                        ACT(carry[:, fc, :], bank[:, 254:256], AF.Copy, [bb], [bf(f"carry{fc}")])
                        continue
                    for kc in range(8):
                        mm(bank[:, 256:512], wb[:, kc, 128 * sub:128 * sub + 128], hT[:, kc, :], kc == 0, kc == 7, [bf("hT"), bwb], [bb], inc=(kc == 7))
                    ae, bae = aext[fc % 2], bf(f"aext{fc % 2}")
                    c_, bc_ = cc[fc % 3], bf(f"cc{fc % 3}")
                    bcar = bf(f"carry{fc}")
                    cwc = [ptab[:, _poff["cw"] + k * 22 + fc:_poff["cw"] + k * 22 + fc + 1] for k in range(3)]
                    ACT(ae[:, 0:2], carry[:, fc, :], AF.Copy, [bcar], [bae])
                    ACT(ae[:, 2:258], bank[:, 0:256], AF.Copy, [bb], [bae])
                    ACT(c_[:], ae[:, 0:256], AF.Identity, [bae, bf("ptab")], [bc_], scale=cwc[0], bias=ptab[:, _poff["cb"] + fc:_poff["cb"] + fc + 1])
                    ACT(carry[:, fc, :], ae[:, 256:258], AF.Copy, [bae], [bcar])
                    if chain:
                        chain[-1][0]()
                    STT(c_[:], ae[:, 1:257], cwc[1], c_[:], ALU.mult, ALU.add, [bae, bc_], [bc_])
                    STT(c_[:], ae[:, 2:258], cwc[2], c_[:], ALU.mult, ALU.add, [bae, bc_], [bc_])
                    if chain:
                        chain[-1][1]()
                        chain.pop()

                    def _gelu(c_=c_, bc_=bc_):
                        ACT(c_[:], c_[:], AF.Gelu_apprx_tanh, [bc_], [bc_])

                    def _mult(c_=c_, bc_=bc_, bank=bank, bb=bb, fc=fc):
                        TT(gT[:, fc, :], c_[:], bank[:, 256:512], ALU.mult, [bc_, bb], [bf("gT")])
                    chain.append((_gelu, _mult))
            while chain:
                chain[-1][0]()
                chain[-1][1]()
                chain.pop()
            ck("ffn_up")
            if not full:
                return
            for cb in range(4):
                units = []
                for kg in range(3):
                    nk = 8 if kg < 2 else 6
                    units.append(load_unit(wdn_s[cb, kg, :, 0:nk, :], nk, "wdn"))
                bank, bb = galloc()
                for tt in range(2):
                    for fc in range(NFC):
                        wt, bw = units[fc // 8]
                        mm(bank[:, 256 * tt:256 * tt + 256], gT[:, fc, 128 * tt:128 * tt + 128], wt[:, fc % 8, :], fc == 0, fc == NFC - 1,
                           [bf("gT"), bw], [bb], inc=(fc == NFC - 1))
                if cb == 0 and T + 1 < last_t:
                    for tt in range(2):
                        norm_T(2 * (T + 1) + tt, tt, "g1")
                    prenorm.add(T + 1)
                for tt in range(2):
                    i = 2 * T + tt
                    xs = xb[i % 4][:, 256 * cb:256 * cb + 256]
                    TT(xs, xs, bank[:, 256 * tt:256 * tt + 256], ALU.add, [bb], [bf(f"xb{i % 4}")])
            ck("ffn_dn")
            for tt in range(2):
                i = 2 * T + tt
                xt, bx = xb[i % 4], bf(f"xb{i % 4}")
                sc = ssq[:, (i % 2) * 4 + 2:(i % 2) * 4 + 3]
                rs = ssq[:, (i % 2) * 4 + 3:(i % 2) * 4 + 4]
                bs = bf(f"ssqf{i % 2}")
                ACT(xn[i % 2][:], xt[:], AF.Square, [bx], [bf(f"xn{i % 2}"), bs], accum_out=sc)
                ACT(rs, sc, AF.Ln, [bs, bf("epsc")], [bs], scale=1.0 / D, bias=epsc[:])
                ACT(rs, rs, AF.Exp, [bs], [bs], scale=-0.5)
                STT(xt[:], xt[:], rs, ptab[:, _poff["gf"]:_poff["gf"] + D], ALU.mult, ALU.mult, [bx, bs, bf("ptab")], [bx])
                r0 = (T - h0) * 256 + tt * 128
                out_toks.append(S.dma("pool", f"out{i % 4}", yout[r0:r0 + 128, :], xt[:], reads=[bx]))

        out_toks = []
        nload = [0]

        def ensure_loaded(upto):
            while nload[0] < upto:
                load_x(nload[0])
                nload[0] += 1

        try:
            ck("init")
            for T in range(0, last_t):
                _CUR[0] = T
                ensure_loaded(2 * T + 2)
                if T < first_t:
                    hist_mode[0] = True
                    tile(T, "hist")
                    hist_mode[0] = False
                    if T == first_t - 1:
                        ucache.clear()
                    ck("hist")
                elif T < h0:
                    tile(T, "halo")
                    ck("halo")
                else:
                    tile(T, "own")
                    ck("own")
        except _Stop:
            pass
        for e_ in ("pe", "act", "dve"):
            pass
        for tok in out_toks[-4:]:
            S.wait_tok("pool", tok)
        S.emit()
    return nc


_NC_CACHE = {}


def kernel(**inp):
    inp = {k: np.asarray(v) for k, v in inp.items()}
    x = inp["x"].astype(np.float32)
    if "nc" not in _NC_CACHE:
        _NC_CACHE["nc"] = build_nc()
    nc = _NC_CACHE["nc"]
    ptab = _ptab(inp)
    ctabs = [_ctab(0), _ctab(1)]
    in_maps = []
    for c in range(8):
        b, hf = c // 2, c % 2
        if hf == 1:
            xl = np.ascontiguousarray(x[b])
        else:
            xl = np.concatenate([np.zeros((4096, D), np.float32), x[b, :4096]], axis=0)
        in_maps.append({"x": xl, "ctab": ctabs[hf], "ptab": ptab,
                        "w_in": np.ascontiguousarray(inp["w_in"][0]), "w_out": np.ascontiguousarray(inp["w_out"][0]),
                        "w_up": np.ascontiguousarray(inp["w_up"][0]), "w_down": np.ascontiguousarray(inp["w_down"][0])})
    res = run_bass_kernel_spmd(nc, in_maps, core_ids=list(range(8)))
    out = np.zeros((4, 8192, D), np.float32)
    for c in range(8):
        b, hf = c // 2, c % 2
        out[b, hf * 4096:(hf + 1) * 4096] = res.results[c]["y"]
    return out
```
